# Optimizing a Trainium2 kernel written in Bass

```python
import math
import jax, jax.numpy as jnp
from jax import lax
import numpy as np

D_MODEL = 2048
BATCH = 4
SEQ = 4096
DEPTH = 1

MEM_LEN = 256
EPS = 1e-6
Q_BLOCK = 128
NEG_BIG = 1e9

MLA_HEADS = 8
MLA_NOPE = 128
MLA_ROPE = 64
MLA_V = 128
MLA_Q_RANK = 512
MLA_KV_RANK = 512
ROPE_THETA = 10000.0

NSA_HEADS = 4
NSA_DK = 192
NSA_DV = 128
CMP_LEN = 32
CMP_STRIDE = 16
SLC_LEN = 64
SLC_TOPN = 16
WIN = 512

MEM_HEADS = 4
MEM_DH = 128

MLA_WIDTH = MLA_HEADS * MLA_V
NSA_WIDTH = NSA_HEADS * NSA_DV
MEM_WIDTH = MEM_HEADS * MEM_DH
MIX_WIDTH = MLA_WIDTH + NSA_WIDTH + MEM_WIDTH

IN_SPLITS = (
    MLA_Q_RANK, MLA_KV_RANK, MLA_ROPE, MLA_WIDTH,
    NSA_HEADS * NSA_DK, NSA_DK, NSA_DV, NSA_DK, NSA_DV,
    NSA_DK, NSA_DV, 3 * NSA_HEADS, NSA_WIDTH,
    MEM_WIDTH, MEM_WIDTH,
)
D_IN = sum(IN_SPLITS)

kernel_name = "hybrid_mla_nsa_memory_block"


def rmsnorm(x, g):
    xf = x.astype(jnp.float32)
    y = xf * lax.rsqrt(jnp.mean(xf * xf, axis=-1, keepdims=True) + EPS)
    return (y * g.astype(jnp.float32)).astype(x.dtype)


def masked_softmax(s, mask):
    s = jnp.where(mask, s.astype(jnp.float32), -1e30)
    m = jnp.max(s, axis=-1, keepdims=True)
    p = jnp.exp(s - m) * mask
    return p / (jnp.sum(p, axis=-1, keepdims=True) + 1e-20)


def alibi_slopes(n):
    return 2.0 ** (-8.0 * jnp.arange(1, n + 1, dtype=jnp.float32) / n)


def apply_rope(x, cos, sin):
    x1, x2 = jnp.split(x.astype(jnp.float32), 2, axis=-1)
    return jnp.concatenate([x1 * cos - x2 * sin, x1 * sin + x2 * cos], axis=-1).astype(x.dtype)


def to_blocks(a):
    b, s = a.shape[:2]
    return jnp.moveaxis(a.reshape(b, s // Q_BLOCK, Q_BLOCK, *a.shape[2:]), 1, 0)


def from_blocks(a):
    a = jnp.moveaxis(a, 0, 1)
    return a.reshape(a.shape[0], -1, *a.shape[3:])


def mla_mixer(c_q, c_kv, k_rope, cos, sin, q_norm_g, w_uq, kv_norm_g, w_ukv):
    b, s, _ = c_q.shape
    q = (rmsnorm(c_q, q_norm_g) @ w_uq).reshape(b, s, MLA_HEADS, MLA_NOPE + MLA_ROPE)
    q = jnp.concatenate([q[..., :MLA_NOPE],
                         apply_rope(q[..., MLA_NOPE:], cos[:, None], sin[:, None])], axis=-1)
    kv = (rmsnorm(c_kv, kv_norm_g) @ w_ukv).reshape(b, s, MLA_HEADS, MLA_NOPE + MLA_V)
    k_pe = apply_rope(k_rope, cos, sin)
    k = jnp.concatenate([kv[..., :MLA_NOPE],
                         jnp.broadcast_to(k_pe[:, :, None], (b, s, MLA_HEADS, MLA_ROPE))], axis=-1)
    v = kv[..., MLA_NOPE:]
    scale = (MLA_NOPE + MLA_ROPE) ** -0.5
    kpos = jnp.arange(s)

    def block(args):
        qb, i = args
        qpos = i * Q_BLOCK + jnp.arange(Q_BLOCK)
        sc = jnp.einsum('bqhd,bkhd->bhqk', qb, k, preferred_element_type=jnp.float32) * scale
        p = masked_softmax(sc, kpos[None, :] <= qpos[:, None])
        return jnp.einsum('bhqk,bkhd->bqhd', p.astype(v.dtype), v)

    o = lax.map(block, (to_blocks(q), jnp.arange(s // Q_BLOCK)))
    return from_blocks(o).reshape(b, s, MLA_WIDTH)


def compress(a, pe, w1, w2):
    b, s, d = a.shape
    ch = a.reshape(b, s // CMP_STRIDE, CMP_STRIDE, d)
    blocks = jnp.concatenate([ch[:, :-1], ch[:, 1:]], axis=2) + pe
    return jax.nn.silu(blocks.reshape(b, -1, CMP_LEN * d) @ w1) @ w2


def nsa_mixer(q, k_c, v_c, k_s, v_s, k_w, v_w, gate_logits,
              cmp_pe_k, cmp_pe_v, cmp_w1k, cmp_w2k, cmp_w1v, cmp_w2v):
    b, s, _ = q.shape
    q = q.reshape(b, s, NSA_HEADS, NSA_DK)
    gates = jax.nn.sigmoid(gate_logits.astype(jnp.float32)).reshape(b, s, NSA_HEADS, 3)
    scale = NSA_DK ** -0.5
    slopes = alibi_slopes(NSA_HEADS)[None, :, None, None]

    k_cmp = compress(k_c, cmp_pe_k, cmp_w1k, cmp_w2k)
    v_cmp = compress(v_c, cmp_pe_v, cmp_w1v, cmp_w2v)
    n_c = k_cmp.shape[1]
    c_start = jnp.arange(n_c) * CMP_STRIDE
    cmp_end = c_start + CMP_LEN - 1
    cmp_pos = c_start.astype(jnp.float32) + (CMP_LEN - 1) / 2.0

    n_s = s // SLC_LEN
    top_n = min(SLC_TOPN, n_s)
    k_blk = k_s.reshape(b, n_s, SLC_LEN, NSA_DK)
    v_blk = v_s.reshape(b, n_s, SLC_LEN, NSA_DV)
    s_start = jnp.arange(n_s) * SLC_LEN
    overlap = ((c_start[:, None] < s_start[None, :] + SLC_LEN) &
               (c_start[:, None] + CMP_LEN > s_start[None, :])).astype(jnp.float32)
    j = jnp.arange(n_s)

    k_wp = jnp.pad(k_w, ((0, 0), (WIN, 0), (0, 0)))
    v_wp = jnp.pad(v_w, ((0, 0), (WIN, 0), (0, 0)))

    def block(args):
        qb, gb, i = args
        t = i * Q_BLOCK + jnp.arange(Q_BLOCK)
        tf = t.astype(jnp.float32)

        sc = jnp.einsum('bqhd,bnd->bhqn', qb, k_cmp, preferred_element_type=jnp.float32) * scale
        sc = sc - slopes * (tf[:, None] - cmp_pos[None, :])
        p_cmp = masked_softmax(sc, cmp_end[None, :] <= t[:, None])
        o_cmp = jnp.einsum('bhqn,bnd->bqhd', p_cmp.astype(v_cmp.dtype), v_cmp)

        imp = jnp.einsum('bhqn,nm->bqm', p_cmp, overlap)
        cur = t // SLC_LEN
        forced = (j[None, :] == 0) | (j[None, :] == cur[:, None]) | (j[None, :] == cur[:, None] - 1)
        imp = jnp.where(forced, NEG_BIG, imp)
        imp = jnp.where(j[None, :] > cur[:, None], -NEG_BIG, imp)
        _, idx = lax.top_k(imp, top_n)
        ks = jax.vmap(lambda kb, ix: kb[ix])(k_blk, idx)
        vs = jax.vmap(lambda vb, ix: vb[ix])(v_blk, idx).reshape(b, Q_BLOCK, top_n * SLC_LEN, NSA_DV)
        spos = (idx[..., None] * SLC_LEN + jnp.arange(SLC_LEN)).reshape(b, Q_BLOCK, top_n * SLC_LEN)
        ss = jnp.einsum('bqhd,bqnld->bhqnl', qb, ks, preferred_element_type=jnp.float32)
        ss = ss.reshape(b, NSA_HEADS, Q_BLOCK, top_n * SLC_LEN) * scale
        ss = ss - slopes * (tf[None, None, :, None] - spos[:, None].astype(jnp.float32))
        p_s = masked_softmax(ss, (spos <= t[None, :, None])[:, None])
        o_slc = jnp.einsum('bhqk,bqkd->bqhd', p_s.astype(vs.dtype), vs)

        kw = lax.dynamic_slice_in_dim(k_wp, i * Q_BLOCK, WIN + Q_BLOCK, axis=1)
        vw = lax.dynamic_slice_in_dim(v_wp, i * Q_BLOCK, WIN + Q_BLOCK, axis=1)
        wpos = i * Q_BLOCK - WIN + jnp.arange(WIN + Q_BLOCK)
        rel = t[:, None] - wpos[None, :]
        sw = jnp.einsum('bqhd,bkd->bhqk', qb, kw, preferred_element_type=jnp.float32) * scale
        sw = sw - slopes * rel.astype(jnp.float32)
        p_w = masked_softmax(sw, (rel >= 0) & (rel < WIN) & (wpos[None, :] >= 0))
        o_win = jnp.einsum('bhqk,bkd->bqhd', p_w.astype(vw.dtype), vw)

        o = gb[..., 0:1] * o_cmp + gb[..., 1:2] * o_slc + gb[..., 2:3] * o_win
        return o.astype(qb.dtype)

    o = lax.map(block, (to_blocks(q), to_blocks(gates), jnp.arange(s // Q_BLOCK)))
    return from_blocks(o).reshape(b, s, NSA_WIDTH)


def memory_mixer(q, mem, mem_norm_g, w_mem_kv):
    b, s, _ = q.shape
    q = q.reshape(b, s, MEM_HEADS, MEM_DH)
    kv = (rmsnorm(mem, mem_norm_g) @ w_mem_kv).reshape(b, mem.shape[1], 2, MEM_HEADS, MEM_DH)
    k, v = kv[:, :, 0], kv[:, :, 1]
    sc = jnp.einsum('bshd,bmhd->bhsm', q, k, preferred_element_type=jnp.float32) * MEM_DH ** -0.5
    p = jax.nn.softmax(sc, axis=-1)
    return jnp.einsum('bhsm,bmhd->bshd', p.astype(v.dtype), v).reshape(b, s, MEM_WIDTH)


def hybrid_layer(x, mem, cos, sin, norm_g, w_in, q_norm_g, w_uq, kv_norm_g, w_ukv,
                 cmp_pe_k, cmp_pe_v, cmp_w1k, cmp_w2k, cmp_w1v, cmp_w2v,
                 mem_norm_g, w_mem_kv, w_out):
    h = rmsnorm(x, norm_g) @ w_in
    offsets = np.cumsum(IN_SPLITS)[:-1].tolist()
    (c_q, c_kv, k_rope, z_mla, q_nsa, k_c, v_c, k_s, v_s, k_w, v_w, g_nsa, z_nsa,
     q_mem, z_mem) = jnp.split(h, offsets, axis=-1)
    o_mla = mla_mixer(c_q, c_kv, k_rope, cos, sin, q_norm_g, w_uq, kv_norm_g, w_ukv) * jax.nn.silu(z_mla)
    o_nsa = nsa_mixer(q_nsa, k_c, v_c, k_s, v_s, k_w, v_w, g_nsa,
                      cmp_pe_k, cmp_pe_v, cmp_w1k, cmp_w2k, cmp_w1v, cmp_w2v) * jax.nn.silu(z_nsa)
    o_mem = memory_mixer(q_mem, mem, mem_norm_g, w_mem_kv) * jax.nn.silu(z_mem)
    o = jnp.concatenate([o_mla, o_nsa, o_mem], axis=-1) @ w_out
    return x + o.astype(x.dtype)


def setup_inputs(seed: int = 0) -> dict:
    key = jax.random.key(seed)
    ks = jax.random.split(key, 20)

    def nrm(k, shape, scale):
        return jax.random.normal(k, shape, jnp.float32) * scale

    def gain(k, shape):
        return 1.0 + 0.01 * jax.random.normal(k, shape, jnp.float32)

    return {
        "x": nrm(ks[0], (BATCH, SEQ, D_MODEL), 1.0),
        "mem": nrm(ks[1], (BATCH, MEM_LEN, D_MODEL), 1.0),
        "norm_g": gain(ks[2], (DEPTH, D_MODEL)),
        "w_in": nrm(ks[3], (DEPTH, D_MODEL, D_IN), D_MODEL ** -0.5),
        "q_norm_g": gain(ks[4], (DEPTH, MLA_Q_RANK)),
        "w_uq": nrm(ks[5], (DEPTH, MLA_Q_RANK, MLA_HEADS * (MLA_NOPE + MLA_ROPE)), MLA_Q_RANK ** -0.5),
        "kv_norm_g": gain(ks[6], (DEPTH, MLA_KV_RANK)),
        "w_ukv": nrm(ks[7], (DEPTH, MLA_KV_RANK, MLA_HEADS * (MLA_NOPE + MLA_V)), MLA_KV_RANK ** -0.5),
        "cmp_pe_k": nrm(ks[8], (DEPTH, CMP_LEN, NSA_DK), 0.02),
        "cmp_pe_v": nrm(ks[9], (DEPTH, CMP_LEN, NSA_DV), 0.02),
        "cmp_w1k": nrm(ks[10], (DEPTH, CMP_LEN * NSA_DK, NSA_DK), (CMP_LEN * NSA_DK) ** -0.5),
        "cmp_w2k": nrm(ks[11], (DEPTH, NSA_DK, NSA_DK), NSA_DK ** -0.5),
        "cmp_w1v": nrm(ks[12], (DEPTH, CMP_LEN * NSA_DV, NSA_DV), (CMP_LEN * NSA_DV) ** -0.5),
        "cmp_w2v": nrm(ks[13], (DEPTH, NSA_DV, NSA_DV), NSA_DV ** -0.5),
        "mem_norm_g": gain(ks[14], (DEPTH, D_MODEL)),
        "w_mem_kv": nrm(ks[15], (DEPTH, D_MODEL, 2 * MEM_WIDTH), D_MODEL ** -0.5),
        "w_out": nrm(ks[16], (DEPTH, MIX_WIDTH, D_MODEL), MIX_WIDTH ** -0.5),
        "final_norm_g": gain(ks[17], (D_MODEL,)),
    }


def reference(x, mem, norm_g, w_in, q_norm_g, w_uq, kv_norm_g, w_ukv,
              cmp_pe_k, cmp_pe_v, cmp_w1k, cmp_w2k, cmp_w1v, cmp_w2v,
              mem_norm_g, w_mem_kv, w_out, final_norm_g):
    s = x.shape[1]
    pos = jnp.arange(s, dtype=jnp.float32)
    inv_freq = ROPE_THETA ** (-jnp.arange(0, MLA_ROPE, 2, dtype=jnp.float32) / MLA_ROPE)
    ang = pos[:, None] * inv_freq[None, :]
    cos, sin = jnp.cos(ang), jnp.sin(ang)
    for l in range(DEPTH):
        x = hybrid_layer(x, mem, cos, sin, norm_g[l], w_in[l], q_norm_g[l], w_uq[l],
                         kv_norm_g[l], w_ukv[l], cmp_pe_k[l], cmp_pe_v[l], cmp_w1k[l],
                         cmp_w2k[l], cmp_w1v[l], cmp_w2v[l], mem_norm_g[l], w_mem_kv[l], w_out[l])
    return rmsnorm(x, final_norm_g)
```

```python
import os
from contextlib import ExitStack
import numpy as np
import ml_dtypes
import concourse.bass as bass
import concourse.mybir as mybir
from concourse.bass_utils import run_bass_kernel_spmd

F32 = mybir.dt.float32
BF16 = mybir.dt.bfloat16
AF = mybir.ActivationFunctionType
ALU = mybir.AluOpType

D = 2048
S = 4096
TQ = 2048
NEG = -30000.0
EPS = 1e-6
SC_MLA = 192 ** -0.5
SC_NSA = 192 ** -0.5
SC_MEM = 128 ** -0.5
NCOL = 5452
DEBUG = bool(int(os.environ.get("MK_DEBUG", "0")))
STOP_AFTER = os.environ.get("MK_STOP", "")


class Prog:
    ENG = ("pe", "act", "dve", "pool", "sp")

    def __init__(self, nc, stack):
        self.nc = nc
        self.stack = stack
        self.E = {}
        for e in self.ENG:
            self.E[e] = dict(sem=self._sem("s_" + e), n=0, known={}, ops=[])
        self.streams = {}
        self.last_w = {}
        self.reads = {}
        self.clock = {}
        self.nwaits = 0
        self.nops = 0

    def _sem(self, name):
        return self.stack.enter_context(self.nc.semaphore(name))

    def stream(self, name):
        if name not in self.streams:
            self.streams[name] = dict(sem=self._sem("d_" + name), n=0)
        return self.streams[name]

    def _deps(self, eng, reads, writes):
        toks = []
        for k in reads:
            t = self.last_w.get(k)
            if t is not None:
                toks.append(t)
        for k in writes:
            t = self.last_w.get(k)
            if t is not None:
                toks.append(t)
            toks.extend(self.reads.get(k, ()))
        known = self.E[eng]["known"]
        need = {}
        for (sid, val) in toks:
            if eng == "pe" and sid == "pe":
                continue
            if known.get(sid, 0) >= val:
                continue
            if need.get(sid, 0) < val:
                need[sid] = val
        waits = []
        for sid, val in need.items():
            if known.get(sid, 0) >= val:
                continue
            ck = self.clock.get((sid, val))
            if ck:
                for a, b in ck.items():
                    if a == eng:
                        continue
                    if known.get(a, 0) < b:
                        known[a] = b
            known[sid] = max(known.get(sid, 0), val)
            waits.append((sid, val))
        return waits

    def _semobj(self, sid):
        if sid in self.E:
            return self.E[sid]["sem"]
        return self.streams[sid]["sem"]

    def op(self, eng, fn, reads=(), writes=()):
        waits = self._deps(eng, reads, writes)
        E = self.E[eng]
        E["n"] += 1
        tok = (eng, E["n"])
        ck = dict(E["known"])
        ck[eng] = E["n"]
        self.clock[tok] = ck
        E["ops"].append((waits, fn, (eng, 1)))
        self._record(tok, reads, writes)
        self.nwaits += len(waits)
        self.nops += 1
        return tok

    def dma(self, q, stream, fn, reads=(), writes=()):
        waits = self._deps(q, reads, writes)
        St = self.stream(stream)
        St["n"] += 16
        tok = (stream, St["n"])
        ck = dict(self.E[q]["known"])
        ck[stream] = St["n"]
        self.clock[tok] = ck
        self.E[q]["ops"].append((waits, fn, (stream, 16)))
        self._record(tok, reads, writes)
        self.nwaits += len(waits)
        self.nops += 1
        return tok

    def _record(self, tok, reads, writes):
        for k in reads:
            self.reads.setdefault(k, []).append(tok)
        for k in writes:
            self.last_w[k] = tok
            self.reads[k] = []

    def barrier(self):
        allt = {}
        for e in self.ENG:
            if self.E[e]["n"] > 0:
                allt[e] = self.E[e]["n"]
        for s, St in self.streams.items():
            if St["n"] > 0:
                allt[s] = St["n"]
        for e in self.ENG:
            E = self.E[e]
            waits = []
            for sid, val in allt.items():
                if sid == e and e == "pe":
                    continue
                if E["known"].get(sid, 0) < val:
                    waits.append((sid, val))
                    E["known"][sid] = val
            if waits:
                E["ops"].append((waits, None, None))
                self.nwaits += len(waits)
        self.last_w = {}
        self.reads = {}
        self.clock = {}

    def emit(self):
        nc = self.nc
        self.barrier()
        with nc.Block() as block:
            def replay(ename):
                def body(engine):
                    for waits, fn, inc in self.E[ename]["ops"]:
                        for sid, val in waits:
                            engine.wait_ge(self._semobj(sid), val)
                        if fn is not None:
                            ins = fn(engine)
                            ins.then_inc(self._semobj(inc[0]), inc[1])
                return body
            block.tensor(replay("pe"))
            block.scalar(replay("act"))
            block.vector(replay("dve"))
            block.gpsimd(replay("pool"))
            block.sync(replay("sp"))


def build_program():
    nc = bass.Bass("TRN2", target_bir_lowering=False)
    skind = "ExternalOutput" if DEBUG else "Internal"

    def din(name, shape, dt=F32):
        return nc.dram_tensor(name, list(shape), dt, kind="ExternalInput").ap()

    def dscr(name, shape, dt=BF16):
        return nc.dram_tensor(name, list(shape), dt, kind=skind).ap()

    x_own = din("x_own", [TQ, D]); x_oth = din("x_oth", [TQ, D]); memx = din("memx", [256, D])
    win_r = din("win_r", [D, NCOL]); g16 = din("g16", [128, 16]); qg4 = din("qg4", [128, 4]); kvg4 = din("kvg4", [128, 4])
    mg16 = din("mg16", [128, 16]); w_uq_r = din("w_uq_r", [512, 8, 256]); w_ukv = din("w_ukv", [512, 2048])
    w1k = din("w1k", [192, 32, 192]); w1v = din("w1v", [128, 32, 128]); pekT = din("pekT", [192, 32]); pevT = din("pevT", [128, 32])
    w2k = din("w2k", [192, 192]); w2v = din("w2v", [128, 128]); w_mkv = din("w_mkv", [D, 1024]); w_out = din("w_out", [D, D])
    fg_rep = din("fg_rep", [128, D])
    ropeCk = din("ropeCk", [64, S]); ropeSk = din("ropeSk", [64, S]); ropeCq = din("ropeCq", [64, TQ]); ropeSq = din("ropeSq", [64, TQ])
    kb_mla = din("kb_mla", [128, 32]); kb_nsa = din("kb_nsa", [128, 4, 32]); cb_nsa = din("cb_nsa", [128, 4, 2])
    qalibi = din("qalibi", [2, 4, TQ], BF16); maskc = din("maskc", [128, 2, TQ], BF16); ovl = din("ovl", [128, 2, 64], BF16)
    selhi = din("selhi", [128, 16, 64]); sello = din("sello", [128, 16, 64]); esel = din("esel", [64, S], BF16)
    tri_c = din("tri_c", [128, 128], BF16); tri_w = din("tri_w", [128, 128], BF16); ident = din("ident", [128, 128], BF16)
    y_out = nc.dram_tensor("y_out", [TQ, D], F32, kind="ExternalOutput").ap()

    ckvnT = dscr("ckvnT", [512, S]); cqnT = dscr("cqnT", [512, TQ]); kpe = dscr("kpe", [64, S])
    zs = dscr("zs", [TQ, D]); qnsaT = dscr("qnsaT", [4, 192, TQ]); kcT = dscr("kcT", [192, S], F32); vcT = dscr("vcT", [128, S], F32)
    ksT = dscr("ksT", [192, S]); kwT = dscr("kwT", [192, S]); vsw = dscr("vsw", [S, 256]); gn = dscr("gn", [TQ, 12], F32)
    qmemT = dscr("qmemT", [512, TQ]); mixT = dscr("mixT", [D, TQ])

    with ExitStack() as st:
        ARENA = 47000
        arena = st.enter_context(nc.sbuf_tensor("arena", [128, ARENA], F32))
        psb = [st.enter_context(nc.psum_tensor(f"psb{i}", [128, 512], F32)) for i in range(8)]
        P = Prog(nc, st)
        off = [0]

        def alloc(n, dt=F32):
            nb = n * (4 if dt == F32 else 2)
            ncol = (nb + 3) // 4
            assert off[0] + ncol <= ARENA, ("SBUF overflow", off[0], ncol)
            a = arena[:, off[0]:off[0] + ncol]
            off[0] += ncol
            if dt != F32:
                a = a.bitcast(dt)[:, 0:n]
            return a

        def PS(i):
            return psb[i].ap()

        def PSB(i):
            return psb[i].ap().bitcast(BF16)

        def mm(out, lhsT, rhs, start, stop, r, w):
            P.op("pe", lambda e: e.matmul(out, lhsT=lhsT, rhs=rhs, start=start, stop=stop), reads=r, writes=w)

        def tp(out, in_, idn, r, w):
            P.op("pe", lambda e: e.transpose(out=out, in_=in_, identity=idn), reads=r, writes=w)

        def act(out, in_, func, r, w, scale=None, bias=None, accum=None):
            kw = {}
            if scale is not None:
                kw["scale"] = scale
            if bias is not None:
                kw["bias"] = bias
            if accum is not None:
                kw["accum_out"] = accum
            P.op("act", lambda e: e.activation(out=out, in_=in_, func=func, **kw), reads=r, writes=w)

        def tsc(eng, out, in0, s1, op0, r, w, s2=None, op1=None):
            if op1 is None:
                P.op(eng, lambda e: e.tensor_scalar(out=out, in0=in0, scalar1=s1, scalar2=None, op0=op0), reads=r, writes=w)
            else:
                P.op(eng, lambda e: e.tensor_scalar(out=out, in0=in0, scalar1=s1, scalar2=s2, op0=op0, op1=op1), reads=r, writes=w)

        def tt(eng, out, in0, in1, op, r, w):
            P.op(eng, lambda e: e.tensor_tensor(out=out, in0=in0, in1=in1, op=op), reads=r, writes=w)

        def stt(out, in0, scalar, in1, op0, op1, r, w):
            P.op("dve", lambda e: e.scalar_tensor_tensor(out=out, in0=in0, scalar=scalar, in1=in1, op0=op0, op1=op1), reads=r, writes=w)

        def cp(eng, out, in_, r, w):
            if eng == "act":
                act(out, in_, AF.Copy, r, w)
            else:
                P.op(eng, lambda e: e.tensor_copy(out=out, in_=in_), reads=r, writes=w)

        def recip(out, in_, r, w):
            P.op("dve", lambda e: e.reciprocal(out=out, in_=in_), reads=r, writes=w)

        def mset(eng, ap, val, w):
            P.op(eng, lambda e: e.memset(ap, val), writes=w)

        def ld(key, out, in_, q="sp", extra_r=()):
            P.dma(q, "L" + key, lambda e: e.dma_start(out=out, in_=in_), reads=list(extra_r), writes=[key])

        def stv(key, out, in_, dkey, q="sp"):
            P.dma(q, "S" + key, lambda e: e.dma_start(out=out, in_=in_), reads=[key], writes=[dkey])

        def rstd_of(ss, key, dim):
            tsc("dve", ss, ss, 1.0 / dim, ALU.mult, [key], [key], s2=EPS, op1=ALU.add)
            act(ss, ss, AF.Sqrt, [key], [key])
            recip(ss, ss, [key], [key])

        rr = {"ps": 0, "ev": 0}

        def next_ps(lo=0, hi=4):
            i = lo + rr["ps"] % (hi - lo)
            rr["ps"] += 1
            return i

        def ev_eng():
            rr["ev"] += 1
            return "act" if rr["ev"] % 2 else "dve"

        ident_t = alloc(128, BF16); tric_t = alloc(128, BF16); triw_t = alloc(128, BF16)
        ld("ident", ident_t, ident); ld("tric", tric_t, tri_c); ld("triw", triw_t, tri_w)
        Ptiles = [alloc(512, BF16) for _ in range(3)]
        persist0 = off[0]

        def phase_A():
            g16_t = alloc(16); ld("g16", g16_t, g16)
            xnT = alloc(16 * TQ, BF16)
            xnT3 = xnT.rearrange("p (c t) -> p c t", c=16)
            xs = [alloc(D) for _ in range(2)]
            xn = [alloc(D, BF16) for _ in range(2)]
            ssx = [alloc(1) for _ in range(2)]
            wst = [alloc(4 * 512) for _ in range(2)]
            wbf = [alloc(16 * 512, BF16) for _ in range(2)]
            stg = [alloc(512) for _ in range(4)]
            sse = [alloc(1) for _ in range(4)]
            rC = [alloc(512) for _ in range(2)]; rS = [alloc(512) for _ in range(2)]
            rt = [alloc(512) for _ in range(2)]

            sc = {"stg": 0, "w": 0, "wp": 0, "rope": 0}

            def next_stg():
                i = sc["stg"] % 4
                sc["stg"] += 1
                return i

            def build_xnT(xsrc):
                for blk in range(16):
                    sl = blk % 2
                    ld(f"xs{sl}", xs[sl], xsrc[blk * 128:(blk + 1) * 128, :])
                    act(xn[sl], xs[sl], AF.Square, [f"xs{sl}"], [f"xn{sl}", f"ssx{sl}"], accum=ssx[sl])
                    rstd_of(ssx[sl], f"ssx{sl}", D)
                    act(xn[sl], xs[sl], AF.Copy, [f"xs{sl}", f"ssx{sl}"], [f"xn{sl}"], scale=ssx[sl])
                    for half in range(2):
                        bank = 6 + half
                        pT = PSB(bank)
                        for c in range(8):
                            cc = half * 8 + c
                            tp(pT[:, c * 128:(c + 1) * 128], xn[sl][:, cc * 128:(cc + 1) * 128], ident_t, [f"xn{sl}", "ident"], [f"ps{bank}"])
                        cp("dve" if half == 0 else "act", xnT3[:, half * 8:(half + 1) * 8, blk * 128:(blk + 1) * 128],
                           pT[:, 0:1024].rearrange("p (c t) -> p c t", c=8), [f"ps{bank}"], [f"xnT_{blk}_{half}"])

            def xnT_keys(t0, nt):
                ks = []
                for blk in range(t0 // 128, (t0 + nt) // 128):
                    ks += [f"xnT_{blk}_0", f"xnT_{blk}_1"]
                return ks

            def load_wtile(c0, ncols):
                sl = sc["w"] % 2
                sc["w"] += 1
                w3 = wbf[sl].rearrange("p (c n) -> p c n", c=16)
                for pc in range(4):
                    ws = sc["wp"] % 2
                    sc["wp"] += 1
                    wv = wst[ws].rearrange("p (c n) -> p c n", c=4)
                    ld(f"wst{ws}", wv[:, :, 0:ncols],
                       win_r[pc * 512:(pc + 1) * 512, c0:c0 + ncols].rearrange("(c p) n -> p c n", p=128))
                    for c in range(4):
                        kc = pc * 4 + c
                        tsc("pool", w3[:, kc, 0:ncols], wv[:, c, 0:ncols], g16_t[:, kc:kc + 1], ALU.mult,
                            [f"wst{ws}", "g16"], [f"wbf{sl}"])
                return w3, f"wbf{sl}"

            def tm_unit(w3, wkey, wc0, n, blk, kind, dest):
                b = next_ps(0, 4)
                ps = PS(b)[:, 0:n]
                for c in range(16):
                    mm(ps, xnT3[:, c, blk * 128:(blk + 1) * 128], w3[:, c, wc0:wc0 + n], c == 0, c == 15,
                       xnT_keys(blk * 128, 128) + [wkey], [f"ps{b}"])
                si = next_stg()
                sk = f"stg{si}"
                if kind == "norm":
                    sb = stg[si].bitcast(BF16)
                    act(sb[:, 0:n], ps, AF.Square, [f"ps{b}"], [sk, f"sse{si}"], accum=sse[si])
                    rstd_of(sse[si], f"sse{si}", n)
                    act(sb[:, 0:n], ps, AF.Copy, [f"ps{b}", f"sse{si}"], [sk], scale=sse[si])
                    bank = 4 + (si % 2)
                    pT = PSB(bank)
                    for c in range(4):
                        tp(pT[:, c * 128:(c + 1) * 128], sb[:, c * 128:(c + 1) * 128], ident_t, [sk, "ident"], [f"ps{bank}"])
                    cp("dve", sb[:, 512:1024], pT[:, 0:512], [f"ps{bank}"], [sk])
                    stv(sk, dest.rearrange("(c p) t -> p c t", p=128)[:, :, blk * 128:(blk + 1) * 128],
                        sb[:, 512:1024].rearrange("p (c t) -> p c t", c=4), "dram_" + kind)
                elif kind == "silu":
                    sb = stg[si].bitcast(BF16)
                    act(sb[:, 0:n], ps, AF.Silu, [f"ps{b}"], [sk])
                    stv(sk, dest[blk * 128:(blk + 1) * 128, :], sb[:, 0:n], "dram_z")
                elif kind == "copy":
                    sb = stg[si].bitcast(BF16)
                    cp(ev_eng(), sb[:, 0:n], ps, [f"ps{b}"], [sk])
                    stv(sk, dest[blk * 128:(blk + 1) * 128, :], sb[:, 0:n], "dram_v")
                elif kind == "sigmoid":
                    act(stg[si][:, 0:n], ps, AF.Sigmoid, [f"ps{b}"], [sk])
                    stv(sk, dest[blk * 128:(blk + 1) * 128, :], stg[si][:, 0:n], "dram_g")

            def fm_mm(w3, wkey, wc0, m, tile):
                b = next_ps(0, 4)
                ps = PS(b)[0:m, :]
                for c in range(16):
                    mm(ps, w3[:, c, wc0:wc0 + m], xnT3[:, c, tile * 512:(tile + 1) * 512], c == 0, c == 15,
                       xnT_keys(tile * 512, 512) + [wkey], [f"ps{b}"])
                return b, ps

            def fm_unit(w3, wkey, wc0, m, tile, t_off, dest_rows, scale, dt):
                b, ps = fm_mm(w3, wkey, wc0, m, tile)
                si = next_stg()
                sk = f"stg{si}"
                sb = stg[si] if dt == F32 else stg[si].bitcast(BF16)
                eng = ev_eng()
                if eng == "act":
                    act(sb[0:m, 0:512], ps, AF.Copy, [f"ps{b}"], [sk], scale=scale)
                else:
                    tsc("dve", sb[0:m, 0:512], ps, scale, ALU.mult, [f"ps{b}"], [sk])
                stv(sk, dest_rows[:, t_off + tile * 512:t_off + (tile + 1) * 512], sb[0:m, 0:512], "dram_fm")

            def rope_unit(w3, wkey, wc0, tile, t_off):
                ba, pa = fm_mm(w3, wkey, wc0, 64, tile)
                bb, pb = fm_mm(w3, wkey, wc0 + 64, 64, tile)
                sl = sc["rope"] % 2
                sc["rope"] += 1
                tok0 = t_off + tile * 512
                ld(f"rC{sl}", rC[sl][0:64, :], ropeCk[:, tok0:tok0 + 512])
                ld(f"rS{sl}", rS[sl][0:64, :], ropeSk[:, tok0:tok0 + 512])
                tt("dve", rC[sl][0:64, :], pa, rC[sl][0:64, :], ALU.mult, [f"ps{ba}", f"rC{sl}"], [f"rC{sl}"])
                tt("dve", rS[sl][0:64, :], pb, rS[sl][0:64, :], ALU.mult, [f"ps{bb}", f"rS{sl}"], [f"rS{sl}"])
                rb = rt[sl].bitcast(BF16)
                tt("dve", rb[0:64, 0:512], rC[sl][0:64, :], rS[sl][0:64, :], ALU.add, [f"rC{sl}", f"rS{sl}"], [f"rt{sl}"])
                stv(f"rt{sl}", kpe[:, tok0:tok0 + 512], rb[0:64, 0:512], "dram_kpe")

            for ts_i, (xsrc, t_off, own) in enumerate([(x_oth, 0, False), (x_own, TQ, True)]):
                build_xnT(xsrc)
                w3, wk = load_wtile(0, 512)
                for blk in range(16):
                    tm_unit(w3, wk, 0, 512, blk, "norm", ckvnT[:, t_off:t_off + TQ])
                w3, wk = load_wtile(512, 256)
                for blk in range(16):
                    tm_unit(w3, wk, 0, 256, blk, "copy", vsw[t_off:t_off + TQ, :])
                w3, wk = load_wtile(768, 512)
                for tile in range(4):
                    rope_unit(w3, wk, 0, tile, t_off)
                    fm_unit(w3, wk, 128, 128, tile, t_off, kcT[0:128, :], 1.0, F32)
                    fm_unit(w3, wk, 256, 64, tile, t_off, kcT[128:192, :], 1.0, F32)
                    fm_unit(w3, wk, 320, 128, tile, t_off, ksT[0:128, :], 1.0, BF16)
                    fm_unit(w3, wk, 448, 64, tile, t_off, ksT[128:192, :], 1.0, BF16)
                w3, wk = load_wtile(1280, 320)
                for tile in range(4):
                    fm_unit(w3, wk, 0, 128, tile, t_off, kwT[0:128, :], 1.0, BF16)
                    fm_unit(w3, wk, 128, 64, tile, t_off, kwT[128:192, :], 1.0, BF16)
                    fm_unit(w3, wk, 192, 128, tile, t_off, vcT[0:128, :], 1.0, F32)
                if not own:
                    continue
                w3, wk = load_wtile(1600, 512)
                for blk in range(16):
                    tm_unit(w3, wk, 0, 512, blk, "norm", cqnT)
                for zi in range(4):
                    w3, wk = load_wtile(2112 + zi * 512, 512)
                    for blk in range(16):
                        tm_unit(w3, wk, 0, 512, blk, "silu", zs[:, zi * 512:(zi + 1) * 512])
                w3, wk = load_wtile(4160, 12)
                for blk in range(16):
                    tm_unit(w3, wk, 0, 12, blk, "sigmoid", gn)
                w3, wk = load_wtile(4172, 512)
                w3b, wkb_ = load_wtile(4684, 256)
                for tile in range(4):
                    for hh in range(4):
                        c0 = hh * 192
                        for (o, m) in ((0, 128), (128, 64)):
                            cc = c0 + o
                            if cc < 512:
                                fm_unit(w3, wk, cc, m, tile, 0, qnsaT[hh, o:o + m, :], SC_NSA, BF16)
                            else:
                                fm_unit(w3b, wkb_, cc - 512, m, tile, 0, qnsaT[hh, o:o + m, :], SC_NSA, BF16)
                w3, wk = load_wtile(4940, 512)
                for tile in range(4):
                    for hh in range(4):
                        fm_unit(w3, wk, hh * 128, 128, tile, 0, qmemT[hh * 128:(hh + 1) * 128, :], SC_MEM, BF16)

        def attention(tag, nslots, kbs_fn, qk_fn, bias_fn, v_fn, vcols, fin_fn):
            cnt = {"s": 0, "p": 0}
            for s in range(nslots):
                kbs = kbs_fn(s)
                cover = {j: [i for i, k in enumerate(kbs) if k["lo"] <= j * 128 < k["hi"]] for j in range(4)}
                for i, k in enumerate(kbs):
                    lo, hi, nk = k["lo"], k["hi"], k.get("nk", 128)
                    sbk = cnt["s"] % 2
                    cnt["s"] += 1
                    Sb = PS(sbk)
                    mms = qk_fn(s, k)
                    for mi, (lT, rh, clo, chi, rds) in enumerate(mms):
                        mm(Sb[0:nk, clo:chi], lT, rh, mi == 0, mi == len(mms) - 1, rds, [f"ps{sbk}"])
                    pi = cnt["p"] % 3
                    cnt["p"] += 1
                    Pt = Ptiles[pi]
                    bias = bias_fn(s, k)
                    act(Pt[0:nk, lo:hi], Sb[0:nk, lo:hi], AF.Exp, [f"ps{sbk}"] + bias[1], [f"Pt{pi}"], bias=bias[0])
                    vap, vkeys = v_fn(s, k)
                    for j in range(lo // 128, hi // 128):
                        mm(PS(2 + j)[:, 0:vcols], Pt[0:nk, j * 128:(j + 1) * 128], vap, i == cover[j][0], i == cover[j][-1],
                           [f"Pt{pi}"] + vkeys, [f"ps{2 + j}"])
                for j in range(4):
                    fin_fn(s, j, PS(2 + j), f"ps{2 + j}")

        def causal_kbs(s):
            kbs = [dict(kb=kb, lo=0, hi=512, mask=None) for kb in range(16)]
            for c in range(s):
                kbs += [dict(kb=16 + 4 * c + i, lo=0, hi=512, mask=None) for i in range(4)]
            kbs += [dict(kb=16 + 4 * s + i, lo=128 * i, hi=512, mask=("c", 128 * i)) for i in range(4)]
            return kbs

        def window_kbs(s):
            base = 12 if s == 0 else 16 + 4 * (s - 1)
            kbs = [dict(kb=base + i, lo=0, hi=128 * (i + 1), mask=("w", 128 * i)) for i in range(4)]
            kbs += [dict(kb=16 + 4 * s + i, lo=128 * i, hi=512, mask=("c", 128 * i)) for i in range(4)]
            return kbs

        def mask_mm(k):
            if k["mask"] is None:
                return []
            kind, c0 = k["mask"]
            t = tric_t if kind == "c" else triw_t
            return [(ident_t, t, c0, c0 + 128, ["ident", "tric", "triw"])]

        def phase_B():
            cqn = alloc(4 * TQ, BF16); cqn3 = cqn.rearrange("p (c t) -> p c t", c=4)
            ckv = alloc(4 * S, BF16); ckv3 = ckv.rearrange("p (c t) -> p c t", c=4)
            kpe_t = alloc(S, BF16)
            Cq = alloc(TQ); Sq = alloc(TQ)
            kbm = alloc(32); qg = alloc(4); kvg = alloc(4)
            ld("cqn", cqn3, cqnT.rearrange("(c p) t -> p c t", p=128))
            ld("ckv", ckv3, ckvnT.rearrange("(c p) t -> p c t", p=128))
            ld("kpe", kpe_t[0:64, :], kpe)
            ld("Cq", Cq[0:64, :], ropeCq); ld("Sq", Sq[0:64, :], ropeSq)
            ld("kbm", kbm, kb_mla); ld("qg", qg, qg4); ld("kvg", kvg, kvg4)
            wqs = alloc(4 * 256); wks = alloc(4 * 256)
            wqb = [alloc(4 * 256, BF16) for _ in range(2)]; wkb = [alloc(4 * 256, BF16) for _ in range(2)]
            QnT = [alloc(TQ, BF16) for _ in range(2)]; QrT = [alloc(TQ, BF16) for _ in range(2)]
            KnT = [alloc(S, BF16) for _ in range(2)]; Vt = [alloc(32 * 129, BF16) for _ in range(2)]
            zt = [alloc(16 * 128, BF16) for _ in range(2)]; mst = [alloc(TQ, BF16) for _ in range(2)]
            r1 = alloc(512); r2 = alloc(512)
            rden = [alloc(1) for _ in range(4)]
            mtm = [alloc(128, BF16) for _ in range(4)]
            for p in range(2):
                v3 = Vt[p].rearrange("p (b n) -> p b n", n=129)
                mset("pool", v3[:, :, 128:129], 1.0, [f"V{p}"])

            for h in range(8):
                p = h % 2
                wq3 = wqs.rearrange("p (c n) -> p c n", c=4); wk3 = wks.rearrange("p (c n) -> p c n", c=4)
                wqb3 = wqb[p].rearrange("p (c n) -> p c n", c=4); wkb3 = wkb[p].rearrange("p (c n) -> p c n", c=4)
                ld("wqs", wq3, w_uq_r[:, h, :].rearrange("(c p) n -> p c n", p=128))
                ld("wks", wk3, w_ukv[:, h * 256:(h + 1) * 256].rearrange("(c p) n -> p c n", p=128))
                for c in range(4):
                    tsc("pool", wqb3[:, c, :], wq3[:, c, :], qg[:, c:c + 1], ALU.mult, ["wqs", "qg"], [f"wqb{p}"])
                    tsc("pool", wkb3[:, c, :], wk3[:, c, :], kvg[:, c:c + 1], ALU.mult, ["wks", "kvg"], [f"wkb{p}"])
                ld(f"zt{p}", zt[p].rearrange("p (j f) -> p j f", j=16),
                   zs[:, h * 128:(h + 1) * 128].rearrange("(j p) f -> p j f", p=128))
                for tile in range(4):
                    tsl = slice(tile * 512, (tile + 1) * 512)
                    b = next_ps(6, 8)
                    for c in range(4):
                        mm(PS(b), wqb3[:, c, 0:128], cqn3[:, c, tsl], c == 0, c == 3, [f"wqb{p}", "cqn"], [f"ps{b}"])
                    act(QnT[p][:, tsl], PS(b), AF.Copy, [f"ps{b}"], [f"Qn{p}_{tile}"], scale=SC_MLA)
                    ba = next_ps(6, 8)
                    for c in range(4):
                        mm(PS(ba)[0:64, :], wqb3[:, c, 128:192], cqn3[:, c, tsl], c == 0, c == 3, [f"wqb{p}", "cqn"], [f"ps{ba}"])
                    stt(r1[0:64, :], PS(ba)[0:64, :], SC_MLA, Cq[0:64, tsl], ALU.mult, ALU.mult, [f"ps{ba}", "Cq"], ["r1"])
                    bb = next_ps(6, 8)
                    for c in range(4):
                        mm(PS(bb)[0:64, :], wqb3[:, c, 192:256], cqn3[:, c, tsl], c == 0, c == 3, [f"wqb{p}", "cqn"], [f"ps{bb}"])
                    stt(r2[0:64, :], PS(bb)[0:64, :], SC_MLA, Sq[0:64, tsl], ALU.mult, ALU.mult, [f"ps{bb}", "Sq"], ["r2"])
                    tt("dve", QrT[p][0:64, tsl], r1[0:64, :], r2[0:64, :], ALU.add, ["r1", "r2"], [f"Qr{p}_{tile}"])
                for tile in range(8):
                    tsl = slice(tile * 512, (tile + 1) * 512)
                    b = next_ps(6, 8)
                    for c in range(4):
                        mm(PS(b), wkb3[:, c, 0:128], ckv3[:, c, tsl], c == 0, c == 3, [f"wkb{p}", "ckv"], [f"ps{b}"])
                    cp(ev_eng(), KnT[p][:, tsl], PS(b), [f"ps{b}"], [f"Kn{p}_{tile}"])
                v3 = Vt[p].rearrange("p (b n) -> p b n", n=129)
                for g in range(8):
                    b = next_ps(6, 8)
                    for i in range(4):
                        blk = g * 4 + i
                        for c in range(4):
                            mm(PS(b)[:, i * 128:(i + 1) * 128], ckv3[:, c, blk * 128:(blk + 1) * 128], wkb3[:, c, 128:256],
                               c == 0, c == 3, [f"wkb{p}", "ckv"], [f"ps{b}"])
                    cp(ev_eng(), v3[:, g * 4:(g + 1) * 4, 0:128], PS(b).rearrange("p (i n) -> p i n", i=4), [f"ps{b}"], [f"V{p}_{g}"])

                def qk_fn(s, k, p=p):
                    kb, lo, hi = k["kb"], k["lo"], k["hi"]
                    qs = slice(s * 512 + lo, s * 512 + hi)
                    out = [(KnT[p][:, kb * 128:(kb + 1) * 128], QnT[p][:, qs], lo, hi, [f"Kn{p}_{kb // 4}", f"Qn{p}_{s}"]),
                           (kpe_t[0:64, kb * 128:(kb + 1) * 128], QrT[p][0:64, qs], lo, hi, ["kpe", f"Qr{p}_{s}"])]
                    return out + mask_mm(k)

                def bias_fn(s, k):
                    return (kbm[:, k["kb"]:k["kb"] + 1], ["kbm"])

                def v_fn(s, k, p=p, v3=v3):
                    return v3[:, k["kb"], :], [f"V{p}", f"V{p}_{k['kb'] // 4}"]

                def fin_fn(s, j, O, okey, p=p, h=h):
                    jj = s * 4 + j
                    recip(rden[j], O[:, 128:129], [okey], [f"rden{j}"])
                    z3 = zt[p].rearrange("p (j f) -> p j f", j=16)
                    stt(mtm[j], O[:, 0:128], rden[j], z3[:, jj, :], ALU.mult, ALU.mult, [okey, f"rden{j}", f"zt{p}"], [f"mtm{j}"])
                    b = 6 + (jj % 2)
                    tp(PSB(b)[:, 0:128], mtm[j], ident_t, [f"mtm{j}", "ident"], [f"ps{b}"])
                    cp("act", mst[p][:, jj * 128:(jj + 1) * 128], PSB(b)[:, 0:128], [f"ps{b}"], [f"mst{p}"])

                attention("mla", 4, causal_kbs, qk_fn, bias_fn, v_fn, 129, fin_fn)
                stv(f"mst{p}", mixT[h * 128:(h + 1) * 128, :], mst[p], "dram_mix")

        def phase_C():
            kca = alloc(256, BF16); kcb = alloc(256, BF16); vca = alloc(2 * 193, BF16)
            vca3 = vca.rearrange("p (b n) -> p b n", n=193)
            mark = off[0]
            w1s = alloc(8 * 192)
            w1a = alloc(32 * 192, BF16); w1b = alloc(32 * 192, BF16)
            w2a = alloc(192, BF16); w2b = alloc(192, BF16); w2s = alloc(192)
            pea = alloc(32); peb = alloc(32)
            kf = alloc(S)
            lo_a = alloc(S, BF16); hi_a = alloc(S, BF16); lo_b = alloc(S, BF16); hi_b = alloc(S, BF16)
            sha = alloc(256, BF16); shb = alloc(256, BF16)

            def load_w1(src, nd, no, dst, dkey):
                d3 = dst.rearrange("p (l o) -> p l o", l=32)
                s3 = w1s.rearrange("p (l o) -> p l o", l=8)
                for q4 in range(4):
                    ld("w1s", s3[0:nd, :, 0:no], src[:, q4 * 8:(q4 + 1) * 8, :])
                    cp("pool", d3[0:nd, q4 * 8:(q4 + 1) * 8, 0:no], s3[0:nd, :, 0:no], ["w1s"], [dkey])

            def addpe(src_rows, nd, pe_t, pekey, lo_t, hi_t, lokey):
                ld("kf", kf[0:nd, :], src_rows)
                k3 = kf.rearrange("p (g l) -> p g l", l=16)
                l3 = lo_t.rearrange("p (g l) -> p g l", l=16); h3 = hi_t.rearrange("p (g l) -> p g l", l=16)
                tt("dve", l3[0:nd], k3[0:nd], pe_t[0:nd, 0:16].unsqueeze(1).to_broadcast([nd, 256, 16]), ALU.add,
                   ["kf", pekey], [lokey])
                tt("pool", h3[0:nd], k3[0:nd], pe_t[0:nd, 16:32].unsqueeze(1).to_broadcast([nd, 256, 16]), ALU.add,
                   ["kf", pekey], [lokey + "h"])

            def layer1(chunks, ochunks, outs, no):
                for oi, (o0, m) in enumerate(ochunks):
                    b = next_ps(6, 8)
                    ps = PS(b)[0:m, 0:255]
                    n_mm = 32 * len(chunks)
                    mi = 0
                    for l in range(32):
                        for (nd, wt, lo_t, hi_t, keys) in chunks:
                            w3 = wt.rearrange("p (l o) -> p l o", l=32)
                            if l < 16:
                                rhs = lo_t.rearrange("p (g l) -> p g l", l=16)[0:nd, 0:255, l]
                            else:
                                rhs = hi_t.rearrange("p (g l) -> p g l", l=16)[0:nd, 1:256, l - 16]
                            mm(ps, w3[0:nd, l, o0:o0 + m], rhs, mi == 0, mi == n_mm - 1, keys, [f"ps{b}"])
                            mi += 1
                    act(outs[oi][0:m, 0:255], ps, AF.Silu, [f"ps{b}"], [f"sh{oi}"])

            ld("pea", pea, pekT[0:128, :]); ld("peb", peb[0:64, :], pekT[128:192, :])
            load_w1(w1k[0:128], 128, 192, w1a, "w1a"); load_w1(w1k[128:192], 64, 192, w1b, "w1b")
            ld("w2s", w2s[:, 0:192], w2k[0:128, :]); cp("pool", w2a[:, 0:192], w2s[:, 0:192], ["w2s"], ["w2a"])
            ld("w2s", w2s[0:64, 0:192], w2k[128:192, :]); cp("pool", w2b[0:64, 0:192], w2s[0:64, 0:192], ["w2s"], ["w2b"])
            addpe(kcT[0:128, :], 128, pea, "pea", lo_a, hi_a, "lo_a")
            addpe(kcT[128:192, :], 64, peb, "peb", lo_b, hi_b, "lo_b")
            layer1([(128, w1a, lo_a, hi_a, ["w1a", "lo_a", "lo_ah"]), (64, w1b, lo_b, hi_b, ["w1b", "lo_b", "lo_bh"])],
                   [(0, 128), (128, 64)], [sha, shb], 192)
            for (o0, m, dst, dk) in ((0, 128, kca, "kca"), (128, 64, kcb, "kcb")):
                b = next_ps(6, 8)
                mm(PS(b)[0:m, 0:255], w2a[:, o0:o0 + m], sha[:, 0:255], True, False, ["w2a", "sh0"], [f"ps{b}"])
                mm(PS(b)[0:m, 0:255], w2b[0:64, o0:o0 + m], shb[0:64, 0:255], False, True, ["w2b", "sh1"], [f"ps{b}"])
                cp("dve", dst[0:m, 0:255], PS(b)[0:m, 0:255], [f"ps{b}"], [dk])
            mset("pool", kcb[64:66, :], 1.0, ["kcb1"])
            ld("pea", pea, pevT)
            load_w1(w1v, 128, 128, w1a, "w1a")
            ld("w2s", w2s[:, 0:128], w2v); cp("pool", w2a[:, 0:128], w2s[:, 0:128], ["w2s"], ["w2a"])
            addpe(vcT, 128, pea, "pea", lo_a, hi_a, "lo_a")
            w1v3 = w1a.rearrange("p (l o) -> p l o", l=32)

            b = next_ps(6, 8)
            for l in range(32):
                if l < 16:
                    rhs = lo_a.rearrange("p (g l) -> p g l", l=16)[:, 0:255, l]
                else:
                    rhs = hi_a.rearrange("p (g l) -> p g l", l=16)[:, 1:256, l - 16]
                mm(PS(b)[:, 0:255], w1v3[:, l, 0:128], rhs, l == 0, l == 31, ["w1a", "lo_a", "lo_ah"], [f"ps{b}"])
            act(sha[:, 0:255], PS(b)[:, 0:255], AF.Silu, [f"ps{b}"], ["sh0"])
            for blk, nk in ((0, 128), (1, 127)):
                b = next_ps(6, 8)
                mm(PS(b)[0:nk, 0:128], sha[:, blk * 128:blk * 128 + nk], w2a[:, 0:128], True, True, ["sh0", "w2a"], [f"ps{b}"])
                cp("dve", vca3[0:nk, blk, 0:128], PS(b)[0:nk, 0:128], [f"ps{b}"], [f"vca{blk}"])
            mset("pool", vca3[:, :, 128:129], 1.0, ["vca_one"])
            ld("vca_ov", vca3[:, :, 129:193], ovl)
            P.barrier()
            off[0] = mark

            qa = alloc(4 * TQ, BF16); qb = alloc(4 * TQ, BF16)
            qa3 = qa.rearrange("p (h t) -> p h t", h=4); qb3 = qb.rearrange("p (h t) -> p h t", h=4)
            kreg = off[0]
            ksa = alloc(S, BF16); ksb = alloc(S, BF16); kwa = alloc(S, BF16); kwb = alloc(S, BF16)
            vs_t = alloc(32 * 129, BF16); vw_t = alloc(32 * 129, BF16)
            vs3 = vs_t.rearrange("p (b n) -> p b n", n=129); vw3 = vw_t.rearrange("p (b n) -> p b n", n=129)
            onsa = alloc(16 * 4 * 128); onsa4 = onsa.rearrange("p (j h d) -> p j h d", j=16, h=4)
            imp = alloc(16 * 64); imp3 = imp.rearrange("p (j m) -> p j m", j=16)
            mkc = alloc(2 * TQ, BF16); mkc3 = mkc.rearrange("p (b t) -> p b t", b=2)
            es = alloc(S, BF16); selbT = alloc(TQ, BF16)
            shi = alloc(16 * 64); slo = alloc(16 * 64)
            shi3 = shi.rearrange("p (j m) -> p j m", j=16); slo3 = slo.rearrange("p (j m) -> p j m", j=16)
            kbn = alloc(4 * 32); kbn3 = kbn.rearrange("p (h b) -> p h b", h=4)
            cbn = alloc(8); cbn3 = cbn.rearrange("p (h b) -> p h b", h=4)
            gt = alloc(16 * 12); gt3 = gt.rearrange("p (j g) -> p j g", j=16)
            rden = [alloc(1) for _ in range(4)]; rg = [alloc(1) for _ in range(4)]
            mtm = [alloc(128, BF16) for _ in range(4)]
            m8a = alloc(8); m8b = alloc(8); impw = alloc(64); impf = alloc(64); selb = alloc(64, BF16)

            ld("qa", qa3, qnsaT[:, 0:128, :].rearrange("h p t -> p h t"))
            ld("qb", qb3[0:64], qnsaT[:, 128:192, :].rearrange("h p t -> p h t"))
            ld("qb2", qb3[64:66], qalibi)
            ld("ksa", ksa, ksT[0:128, :]); ld("ksb", ksb[0:64, :], ksT[128:192, :])
            ld("kwa", kwa, kwT[0:128, :]); ld("kwb", kwb[0:64, :], kwT[128:192, :])
            mset("pool", ksb[64:66, :], 1.0, ["ksb1"]); mset("pool", kwb[64:66, :], 1.0, ["kwb1"])
            ld("vs", vs3[:, :, 0:128], vsw[:, 0:128].rearrange("(b p) f -> p b f", p=128))
            ld("vw", vw3[:, :, 0:128], vsw[:, 128:256].rearrange("(b p) f -> p b f", p=128))
            mset("pool", vs3[:, :, 128:129], 1.0, ["vs1"]); mset("pool", vw3[:, :, 128:129], 1.0, ["vw1"])
            ld("mkc", mkc3, maskc); ld("es", es[0:64, :], esel)
            ld("shi", shi3, selhi); ld("slo", slo3, sello)
            ld("kbn", kbn3, kb_nsa); ld("cbn", cbn3, cb_nsa)
            ld("gt", gt3, gn.rearrange("(j p) g -> p j g", p=128))

            qkeys = ["qa", "qb", "qb2"]

            def mk_qk(h, ka, kb_, kkeys, extra=None):
                def qk_fn(s, k):
                    kb, lo, hi = k["kb"], k["lo"], k["hi"]
                    nk = k.get("nk", 128)
                    qs = slice(s * 512 + lo, s * 512 + hi)
                    ks = slice(kb * 128, kb * 128 + nk)
                    out = [(ka[:, ks], qa3[:, h, qs], lo, hi, kkeys + qkeys),
                           (kb_[0:66, ks], qb3[0:66, h, qs], lo, hi, kkeys + qkeys)]
                    if extra is not None:
                        out += extra(s, k)
                    return out + mask_mm(k)
                return qk_fn

            def nsa_fin(h, branch):
                def fin_fn(s, j, O, okey):
                    jj = s * 4 + j
                    tsc("dve", rden[j], O[:, 128:129], 1e-20, ALU.add, [okey], [f"rden{j}"])
                    recip(rden[j], rden[j], [f"rden{j}"], [f"rden{j}"])
                    tt("dve", rg[j], rden[j], gt3[:, jj, h * 3 + branch:h * 3 + branch + 1], ALU.mult, [f"rden{j}", "gt"], [f"rg{j}"])
                    if branch == 0:
                        tsc("dve", onsa4[:, jj, h, :], O[:, 0:128], rg[j], ALU.mult, [okey, f"rg{j}"], [f"onsa{jj}_{h}"])
                        if h == 0:
                            tsc("dve", imp3[:, jj, :], O[:, 129:193], rden[j], ALU.mult, [okey, f"rden{j}"], [f"imp{jj}"])
                        else:
                            stt(imp3[:, jj, :], O[:, 129:193], rden[j], imp3[:, jj, :], ALU.mult, ALU.add,
                                [okey, f"rden{j}", f"imp{jj}"], [f"imp{jj}"])
                    else:
                        stt(onsa4[:, jj, h, :], O[:, 0:128], rg[j], onsa4[:, jj, h, :], ALU.mult, ALU.add,
                            [okey, f"rg{j}", f"onsa{jj}_{h}"], [f"onsa{jj}_{h}"])
                return fin_fn

            def cmp_kbs(s):
                return [dict(kb=0, lo=0, hi=512, nk=128, mask=None), dict(kb=1, lo=0, hi=512, nk=127, mask=None)]

            for h in range(4):
                def extra(s, k):
                    nk = k["nk"]
                    return [(ident_t[0:nk, 0:nk], mkc3[0:nk, k["kb"], s * 512:(s + 1) * 512], 0, 512, ["ident", "mkc"])]

                def bias_fn(s, k, h=h):
                    return (cbn3[0:k["nk"], h, k["kb"]:k["kb"] + 1], ["cbn"])

                def v_fn(s, k):
                    return vca3[0:k["nk"], k["kb"], :], ["vca0", "vca1", "vca_one", "vca_ov"]
                attention("cmp", 4, cmp_kbs, mk_qk(h, kca, kcb, ["kca", "kcb", "kcb1"], extra), bias_fn, v_fn, 193, nsa_fin(h, 0))

            for jj in range(16):
                tt("dve", impf, imp3[:, jj, :], shi3[:, jj, :], ALU.max, [f"imp{jj}", "shi"], ["impf"])
                tt("dve", impf, impf, slo3[:, jj, :], ALU.min, ["impf", "slo"], ["impf"])
                P.op("dve", lambda e: e.max(out=m8a, in_=impf), reads=["impf"], writes=["m8a"])
                P.op("dve", lambda e: e.match_replace(out=impw, in_to_replace=m8a, in_values=impf, imm_value=-3e9),
                     reads=["impf", "m8a"], writes=["impw"])
                P.op("dve", lambda e: e.max(out=m8b, in_=impw), reads=["impw"], writes=["m8b"])
                tsc("dve", selb, impf, m8b[:, 7:8], ALU.is_lt, ["impf", "m8b"], ["selb"], s2=NEG, op1=ALU.mult)
                b = 6 + (jj % 2)
                tp(PSB(b)[0:64, 0:128], selb, ident_t, ["selb", "ident"], [f"ps{b}"])
                cp("act", selbT[0:64, jj * 128:(jj + 1) * 128], PSB(b)[0:64, 0:128], [f"ps{b}"], ["selbT"])

            for h in range(4):
                def extra(s, k):
                    kb, lo, hi = k["kb"], k["lo"], k["hi"]
                    return [(es[0:64, kb * 128:(kb + 1) * 128], selbT[0:64, s * 512 + lo:s * 512 + hi], lo, hi, ["es", "selbT"])]

                def bias_fn(s, k, h=h):
                    return (kbn3[:, h, k["kb"]:k["kb"] + 1], ["kbn"])

                def v_fn(s, k):
                    return vs3[:, k["kb"], :], ["vs", "vs1"]
                attention("slc", 4, causal_kbs, mk_qk(h, ksa, ksb, ["ksa", "ksb", "ksb1"], extra), bias_fn, v_fn, 129, nsa_fin(h, 1))

            for h in range(4):
                def bias_fn(s, k, h=h):
                    return (kbn3[:, h, k["kb"]:k["kb"] + 1], ["kbn"])

                def v_fn(s, k):
                    return vw3[:, k["kb"], :], ["vw", "vw1"]
                attention("win", 4, window_kbs, mk_qk(h, kwa, kwb, ["kwa", "kwb", "kwb1"]), bias_fn, v_fn, 129, nsa_fin(h, 2))

            P.barrier()
            save = off[0]
            off[0] = kreg
            zn = alloc(16 * 512, BF16); zn3 = zn.rearrange("p (j f) -> p j f", j=16)
            mst = alloc(4 * TQ, BF16); mst3 = mst.rearrange("p (h t) -> p h t", h=4)
            off[0] = save
            ld("zn", zn3, zs[:, 1024:1536].rearrange("(j p) f -> p j f", p=128))
            for h in range(4):
                for jj in range(16):
                    j = jj % 4
                    tt("dve", mtm[j], onsa4[:, jj, h, :], zn3[:, jj, h * 128:(h + 1) * 128], ALU.mult, [f"onsa{jj}_{h}", "zn"], [f"mtm{j}"])
                    b = 6 + (jj % 2)
                    tp(PSB(b)[:, 0:128], mtm[j], ident_t, [f"mtm{j}", "ident"], [f"ps{b}"])
                    cp("act", mst3[:, h, jj * 128:(jj + 1) * 128], PSB(b)[:, 0:128], [f"ps{b}"], ["mstn"])
            stv("mstn", mixT[1024:1536, :].rearrange("(h p) t -> p h t", p=128), mst3, "dram_mix")

        def phase_M():
            mg = alloc(16); ld("mg", mg, mg16)
            xs = [alloc(D) for _ in range(2)]; xn = [alloc(D, BF16) for _ in range(2)]; ssx = [alloc(1) for _ in range(2)]
            xmT = alloc(16 * 256, BF16); xm3 = xmT.rearrange("p (c t) -> p c t", c=16)
            wst = [alloc(4 * 512) for _ in range(2)]
            wbf = [alloc(16 * 512, BF16) for _ in range(2)]
            kmT = alloc(4 * 256, BF16); km3 = kmT.rearrange("p (h t) -> p h t", h=4)
            vm = alloc(2 * 4 * 129, BF16); vm4 = vm.rearrange("p (b h n) -> p b h n", b=2, h=4)
            qm = alloc(4 * TQ, BF16); qm3 = qm.rearrange("p (h t) -> p h t", h=4)
            zm = alloc(16 * 512, BF16); zm3 = zm.rearrange("p (j f) -> p j f", j=16)
            mst = alloc(4 * TQ, BF16); mst3 = mst.rearrange("p (h t) -> p h t", h=4)
            rden = [alloc(1) for _ in range(4)]; mtm = [alloc(128, BF16) for _ in range(4)]
            ld("qm", qm3, qmemT.rearrange("(h p) t -> p h t", p=128))
            ld("zm", zm3, zs[:, 1536:2048].rearrange("(j p) f -> p j f", p=128))
            mset("pool", vm4[:, :, :, 128:129], 1.0, ["vm1"])
            for blk in range(2):
                sl = blk
                ld(f"xs{sl}", xs[sl], memx[blk * 128:(blk + 1) * 128, :])
                act(xn[sl], xs[sl], AF.Square, [f"xs{sl}"], [f"xn{sl}", f"ssx{sl}"], accum=ssx[sl])
                rstd_of(ssx[sl], f"ssx{sl}", D)
                act(xn[sl], xs[sl], AF.Copy, [f"xs{sl}", f"ssx{sl}"], [f"xn{sl}"], scale=ssx[sl])
                for half in range(2):
                    bank = 6 + half
                    pT = PSB(bank)
                    for c in range(8):
                        cc = half * 8 + c
                        tp(pT[:, c * 128:(c + 1) * 128], xn[sl][:, cc * 128:(cc + 1) * 128], ident_t, [f"xn{sl}", "ident"], [f"ps{bank}"])
                    cp("dve", xm3[:, half * 8:(half + 1) * 8, blk * 128:(blk + 1) * 128],
                       pT[:, 0:1024].rearrange("p (c t) -> p c t", c=8), [f"ps{bank}"], [f"xm_{blk}_{half}"])
            xmk = [f"xm_{b}_{hf}" for b in range(2) for hf in range(2)]
            wp = 0
            for wt in range(2):
                w3 = wbf[wt].rearrange("p (c n) -> p c n", c=16)
                for pc in range(4):
                    ws = wp % 2
                    wp += 1
                    wv = wst[ws].rearrange("p (c n) -> p c n", c=4)
                    ld(f"wst{ws}", wv, w_mkv[pc * 512:(pc + 1) * 512, wt * 512:(wt + 1) * 512].rearrange("(c p) n -> p c n", p=128))
                    for c in range(4):
                        kc = pc * 4 + c
                        tsc("pool", w3[:, kc, :], wv[:, c, :], mg[:, kc:kc + 1], ALU.mult, [f"wst{ws}", "mg"], [f"wbf{wt}"])
            wk3 = wbf[0].rearrange("p (c n) -> p c n", c=16); wv3 = wbf[1].rearrange("p (c n) -> p c n", c=16)
            for hh in range(4):
                b = next_ps(6, 8)
                for c in range(16):
                    mm(PS(b)[:, 0:256], wk3[:, c, hh * 128:(hh + 1) * 128], xm3[:, c, :], c == 0, c == 15, ["wbf0"] + xmk, [f"ps{b}"])
                cp("dve", km3[:, hh, :], PS(b)[:, 0:256], [f"ps{b}"], ["km"])
            for blk in range(2):
                b = next_ps(6, 8)
                for c in range(16):
                    mm(PS(b), xm3[:, c, blk * 128:(blk + 1) * 128], wv3[:, c, :], c == 0, c == 15, ["wbf1"] + xmk, [f"ps{b}"])
                cp("dve", vm4[:, blk, :, 0:128], PS(b).rearrange("p (h n) -> p h n", h=4), [f"ps{b}"], ["vmv"])

            def mem_kbs(s):
                return [dict(kb=0, lo=0, hi=512, mask=None), dict(kb=1, lo=0, hi=512, mask=None)]

            for h in range(4):
                def qk_fn(s, k, h=h):
                    kb = k["kb"]
                    return [(km3[:, h, kb * 128:(kb + 1) * 128], qm3[:, h, s * 512:(s + 1) * 512], 0, 512, ["km", "qm"])]

                def bias_fn(s, k):
                    return (None, [])

                def v_fn(s, k, h=h):
                    return vm4[:, k["kb"], h, :], ["vmv", "vm1"]

                def fin_fn(s, j, O, okey, h=h):
                    jj = s * 4 + j
                    recip(rden[j], O[:, 128:129], [okey], [f"rden{j}"])
                    stt(mtm[j], O[:, 0:128], rden[j], zm3[:, jj, h * 128:(h + 1) * 128], ALU.mult, ALU.mult,
                        [okey, f"rden{j}", "zm"], [f"mtm{j}"])
                    b = 6 + (jj % 2)
                    tp(PSB(b)[:, 0:128], mtm[j], ident_t, [f"mtm{j}", "ident"], [f"ps{b}"])
                    cp("act", mst3[:, h, jj * 128:(jj + 1) * 128], PSB(b)[:, 0:128], [f"ps{b}"], ["mstm"])
                attention("mem", 4, mem_kbs, qk_fn, bias_fn, v_fn, 129, fin_fn)
            stv("mstm", mixT[1536:2048, :].rearrange("(h p) t -> p h t", p=128), mst3, "dram_mix")

        def phase_D():
            wob = alloc(16 * D, BF16); wo3 = wob.rearrange("p (c n) -> p c n", c=16)
            wst = [alloc(4 * 512) for _ in range(2)]
            fg = alloc(D); ld("fg", fg, fg_rep)
            mx = [alloc(16 * 128, BF16) for _ in range(2)]
            xs = [alloc(D) for _ in range(2)]; yt = [alloc(D) for _ in range(2)]; ssy = [alloc(1) for _ in range(2)]
            junk = alloc(D, BF16)
            wp = 0
            for nt in range(4):
                for pc in range(4):
                    ws = wp % 2
                    wp += 1
                    wv = wst[ws].rearrange("p (c n) -> p c n", c=4)
                    ld(f"wst{ws}", wv, w_out[pc * 512:(pc + 1) * 512, nt * 512:(nt + 1) * 512].rearrange("(c p) n -> p c n", p=128))
                    cp("pool", wo3[:, pc * 4:(pc + 1) * 4, nt * 512:(nt + 1) * 512], wv, [f"wst{ws}"], [f"wob{nt}"])
            wokeys = [f"wob{nt}" for nt in range(4)]
            for blk in range(16):
                sl = blk % 2
                m3 = mx[sl].rearrange("p (c t) -> p c t", c=16)
                ld(f"mx{sl}", m3, mixT[:, blk * 128:(blk + 1) * 128].rearrange("(c p) t -> p c t", p=128))
                ld(f"xs{sl}", xs[sl], x_own[blk * 128:(blk + 1) * 128, :])
                for nt in range(4):
                    b = (blk * 4 + nt) % 6
                    for c in range(16):
                        mm(PS(b), m3[:, c, :], wo3[:, c, nt * 512:(nt + 1) * 512], c == 0, c == 15, [f"mx{sl}", wokeys[nt]], [f"ps{b}"])
                    tt("dve", yt[sl][:, nt * 512:(nt + 1) * 512], PS(b), xs[sl][:, nt * 512:(nt + 1) * 512], ALU.add,
                       [f"ps{b}", f"xs{sl}"], [f"yt{sl}"])
                act(junk, yt[sl], AF.Square, [f"yt{sl}"], ["junk", f"ssy{sl}"], accum=ssy[sl])
                rstd_of(ssy[sl], f"ssy{sl}", D)
                act(yt[sl], yt[sl], AF.Copy, [f"yt{sl}", f"ssy{sl}"], [f"yt{sl}"], scale=ssy[sl])
                tt("pool", yt[sl], yt[sl], fg, ALU.mult, [f"yt{sl}", "fg"], [f"yt{sl}"])
                stv(f"yt{sl}", y_out[blk * 128:(blk + 1) * 128, :], yt[sl], "dram_y", q="pool")

        phases = [("A", phase_A), ("B", phase_B), ("C", phase_C), ("M", phase_M), ("D", phase_D)]
        for name, fn in phases:
            off[0] = persist0
            fn()
            P.barrier()
            if STOP_AFTER == name:
                break
        P.emit()
        print(f"[kernel] ops={P.nops} waits={P.nwaits} sems={len(P.streams) + 5}", flush=True)
    return nc


def _bf(a):
    return np.ascontiguousarray(a).astype(ml_dtypes.bfloat16)


def _prep_shared(inp):
    w_in = np.asarray(inp["w_in"])[0]
    o = np.cumsum([0, 512, 512, 64, 1024, 768, 192, 128, 192, 128, 192, 128, 12, 512, 512, 512])
    seg = {n: (o[i], o[i + 1]) for i, n in enumerate(
        ["c_q", "c_kv", "k_rope", "z_mla", "q_nsa", "k_c", "v_c", "k_s", "v_s", "k_w", "v_w", "g_nsa", "z_nsa", "q_mem", "z_mem"])}

    def cols(n):
        return np.arange(seg[n][0], seg[n][1])
    kr = cols("k_rope")
    order = np.concatenate([
        cols("c_kv"), cols("v_s"), cols("v_w"),
        kr, np.concatenate([kr[32:], kr[:32]]), cols("k_c"), cols("k_s"),
        cols("k_w"), cols("v_c"),
        cols("c_q"), cols("z_mla"), cols("z_nsa"), cols("z_mem"), cols("g_nsa"),
        cols("q_nsa"), cols("q_mem")])
    assert order.shape[0] == NCOL
    sh = {}
    sh["win_r"] = np.ascontiguousarray(w_in[:, order])
    sh["g16"] = np.ascontiguousarray(np.asarray(inp["norm_g"])[0].reshape(16, 128).T)
    sh["qg4"] = np.ascontiguousarray(np.asarray(inp["q_norm_g"])[0].reshape(4, 128).T)
    sh["kvg4"] = np.ascontiguousarray(np.asarray(inp["kv_norm_g"])[0].reshape(4, 128).T)
    sh["mg16"] = np.ascontiguousarray(np.asarray(inp["mem_norm_g"])[0].reshape(16, 128).T)
    wuq = np.asarray(inp["w_uq"])[0].reshape(512, 8, 192)
    sh["w_uq_r"] = np.ascontiguousarray(np.concatenate(
        [wuq[:, :, 0:128], wuq[:, :, 128:192], wuq[:, :, 160:192], wuq[:, :, 128:160]], axis=2))
    sh["w_ukv"] = np.ascontiguousarray(np.asarray(inp["w_ukv"])[0])
    sh["w1k"] = np.ascontiguousarray(np.asarray(inp["cmp_w1k"])[0].reshape(32, 192, 192).transpose(1, 0, 2))
    sh["w1v"] = np.ascontiguousarray(np.asarray(inp["cmp_w1v"])[0].reshape(32, 128, 128).transpose(1, 0, 2))
    sh["pekT"] = np.ascontiguousarray(np.asarray(inp["cmp_pe_k"])[0].T)
    sh["pevT"] = np.ascontiguousarray(np.asarray(inp["cmp_pe_v"])[0].T)
    sh["w2k"] = np.ascontiguousarray(np.asarray(inp["cmp_w2k"])[0])
    sh["w2v"] = np.ascontiguousarray(np.asarray(inp["cmp_w2v"])[0])
    sh["w_mkv"] = np.ascontiguousarray(np.asarray(inp["w_mem_kv"])[0])
    sh["w_out"] = np.ascontiguousarray(np.asarray(inp["w_out"])[0])
    sh["fg_rep"] = np.ascontiguousarray(np.broadcast_to(np.asarray(inp["final_norm_g"])[None, :], (128, D)))
    k = np.arange(128)[:, None]
    q = np.arange(128)[None, :]
    sh["tri_c"] = _bf(np.where(k > q, NEG, 0.0))
    sh["tri_w"] = _bf(np.where(q >= k, NEG, 0.0))
    sh["ident"] = _bf(np.eye(128))
    kk = np.arange(S)
    sh["esel"] = _bf((kk[None, :] // 64 == np.arange(64)[:, None]).astype(np.float32))
    return sh


def _rope_tables(pos):
    inv_freq = (10000.0 ** (-np.arange(0, 64, 2, dtype=np.float32) / 64)).astype(np.float32)
    ang = pos.astype(np.float32)[:, None] * inv_freq[None, :]
    cos = np.cos(ang).astype(np.float32).T
    sin = np.sin(ang).astype(np.float32).T
    return (np.ascontiguousarray(np.concatenate([cos, cos], 0)),
            np.ascontiguousarray(np.concatenate([-sin, sin], 0)))


def _prep_half(hf):
    t = {}
    tq = 2048 * hf + np.arange(TQ)
    kvalid = np.concatenate([np.full(TQ, hf == 1), np.ones(TQ, bool)])
    kpos = np.concatenate([np.arange(TQ), tq]).astype(np.int64)
    kpos_eff = np.where(kvalid, kpos, 0)
    t["ropeCk"], t["ropeSk"] = _rope_tables(kpos)
    t["ropeCq"], t["ropeSq"] = _rope_tables(tq)
    slopes = (2.0 ** (-8.0 * np.arange(1, 5) / 4)).astype(np.float32)
    mb = np.where(kvalid, 0.0, NEG).astype(np.float32)
    t["kb_mla"] = np.ascontiguousarray(mb.reshape(32, 128).T)
    kbn = slopes[None, :, None] * kpos_eff.reshape(32, 128).T[:, None, :].astype(np.float32) + mb.reshape(32, 128).T[:, None, :]
    t["kb_nsa"] = np.ascontiguousarray(kbn.astype(np.float32))
    qa = np.stack([-64.0 * slopes[:, None] * (tq // 64)[None, :], -slopes[:, None] * (tq % 64)[None, :]], 0)
    t["qalibi"] = _bf(qa)
    n_ = np.arange(256)
    if hf == 1:
        cvalid = n_ < 255
        nat = n_
    else:
        cvalid = (n_ >= 128) & (n_ < 255)
        nat = n_ - 128
    cpos = 16.0 * nat + 15.5
    cend = 16 * nat + 31
    cb = slopes[None, :] * np.where(cvalid, cpos, 0.0)[:, None] + np.where(cvalid, 0.0, NEG)[:, None]
    t["cb_nsa"] = np.ascontiguousarray(cb.reshape(2, 128, 4).transpose(1, 2, 0).astype(np.float32))
    mc = np.where(cvalid[:, None] & (cend[:, None] <= tq[None, :]), 0.0, NEG)
    t["maskc"] = _bf(mc.reshape(2, 128, TQ).transpose(1, 0, 2))
    j_ = np.arange(64)
    if hf == 1:
        jvalid = np.ones(64, bool)
        natj = j_
    else:
        jvalid = j_ >= 32
        natj = j_ - 32
    cst = 16 * nat
    sst = 64 * natj
    ov = (cst[:, None] < sst[None, :] + 64) & (cst[:, None] + 32 > sst[None, :]) & cvalid[:, None] & jvalid[None, :]
    t["ovl"] = _bf(ov.astype(np.float32).reshape(2, 128, 64).transpose(1, 0, 2))
    cur = tq // 64
    forced = jvalid[None, :] & ((natj[None, :] == 0) | (natj[None, :] == cur[:, None]) | (natj[None, :] == cur[:, None] - 1))
    fut = (~jvalid[None, :]) | (natj[None, :] > cur[:, None])
    t["selhi"] = np.ascontiguousarray(np.where(forced, 1e9, 0.0).astype(np.float32).reshape(16, 128, 64).transpose(1, 0, 2))
    t["sello"] = np.ascontiguousarray(np.where(fut, -1e9, 1e9).astype(np.float32).reshape(16, 128, 64).transpose(1, 0, 2))
    return t


_CACHE = {}


def kernel(**inputs):
    x = np.asarray(inputs["x"], dtype=np.float32)
    mem = np.asarray(inputs["mem"], dtype=np.float32)
    if "nc" not in _CACHE:
        _CACHE["nc"] = build_program()
        _CACHE["half"] = [_prep_half(0), _prep_half(1)]
    nc = _CACHE["nc"]
    sh = _prep_shared(inputs)
    cores = [int(c) for c in os.environ.get("MK_CORES", "0,1,2,3,4,5,6,7").split(",")]
    in_maps = []
    for c in cores:
        b, hf = c // 2, c % 2
        m = dict(sh)
        m.update(_CACHE["half"][hf])
        m["x_own"] = np.ascontiguousarray(x[b, 2048 * hf:2048 * hf + 2048])
        m["x_oth"] = np.ascontiguousarray(x[b, 0:2048])
        m["memx"] = np.ascontiguousarray(mem[b])
        in_maps.append(m)
    res = run_bass_kernel_spmd(nc, in_maps, core_ids=list(range(len(cores))))
    if DEBUG:
        _CACHE["dbg"] = res.results
    out = np.zeros((4, S, D), np.float32)
    for i, c in enumerate(cores):
        b, hf = c // 2, c % 2
        out[b, 2048 * hf:2048 * hf + 2048] = res.results[i]["y_out"]
    return out
```

```python
import os
from contextlib import ExitStack
import numpy as np
import ml_dtypes
import concourse.bass as bass
import concourse.mybir as mybir
from concourse.bass_utils import run_bass_kernel_spmd

F32 = mybir.dt.float32
BF16 = mybir.dt.bfloat16
AF = mybir.ActivationFunctionType
ALU = mybir.AluOpType

D = 2048
S = 4096
TQ = 2048
NEG = -30000.0
EPS = 1e-6
SC_MLA = 192 ** -0.5
SC_NSA = 192 ** -0.5
SC_MEM = 128 ** -0.5
NCOL = 5452
DEBUG = bool(int(os.environ.get("MK_DEBUG", "0")))
STOP_AFTER = os.environ.get("MK_STOP", "")


class Prog:
    ENG = ("pe", "act", "dve", "pool", "sp")

    def __init__(self, nc, stack):
        self.nc = nc
        self.stack = stack
        self.E = {}
        for e in self.ENG:
            self.E[e] = dict(sem=self._sem("s_" + e), n=0, known={}, ops=[])
        self.streams = {}
        self.last_w = {}
        self.reads = {}
        self.clock = {}
        self.nwaits = 0
        self.nops = 0

    def _sem(self, name):
        return self.stack.enter_context(self.nc.semaphore(name))

    def stream(self, name):
        if name not in self.streams:
            self.streams[name] = dict(sem=self._sem("d_" + name), n=0)
        return self.streams[name]

    def _deps(self, eng, reads, writes):
        toks = []
        for k in reads:
            t = self.last_w.get(k)
            if t is not None:
                toks.append(t)
        for k in writes:
            t = self.last_w.get(k)
            if t is not None:
                toks.append(t)
            toks.extend(self.reads.get(k, ()))
        known = self.E[eng]["known"]
        need = {}
        for (sid, val) in toks:
            if eng == "pe" and sid == "pe":
                continue
            if known.get(sid, 0) >= val:
                continue
            if need.get(sid, 0) < val:
                need[sid] = val
        waits = []
        for sid, val in need.items():
            if known.get(sid, 0) >= val:
                continue
            ck = self.clock.get((sid, val))
            if ck:
                for a, b in ck.items():
                    if a == eng:
                        continue
                    if known.get(a, 0) < b:
                        known[a] = b
            known[sid] = max(known.get(sid, 0), val)
            waits.append((sid, val))
        return waits

    def _semobj(self, sid):
        if sid in self.E:
            return self.E[sid]["sem"]
        return self.streams[sid]["sem"]

    def op(self, eng, fn, reads=(), writes=()):
        waits = self._deps(eng, reads, writes)
        E = self.E[eng]
        E["n"] += 1
        tok = (eng, E["n"])
        ck = dict(E["known"])
        ck[eng] = E["n"]
        self.clock[tok] = ck
        E["ops"].append((waits, fn, (eng, 1)))
        self._record(tok, reads, writes)
        self.nwaits += len(waits)
        self.nops += 1
        return tok

    def dma(self, q, stream, fn, reads=(), writes=()):
        waits = self._deps(q, reads, writes)
        St = self.stream(stream)
        St["n"] += 16
        tok = (stream, St["n"])
        ck = dict(self.E[q]["known"])
        ck[stream] = St["n"]
        self.clock[tok] = ck
        self.E[q]["ops"].append((waits, fn, (stream, 16)))
        self._record(tok, reads, writes)
        self.nwaits += len(waits)
        self.nops += 1
        return tok

    def _record(self, tok, reads, writes):
        for k in reads:
            self.reads.setdefault(k, []).append(tok)
        for k in writes:
            self.last_w[k] = tok
            self.reads[k] = []

    def barrier(self):
        allt = {}
        for e in self.ENG:
            if self.E[e]["n"] > 0:
                allt[e] = self.E[e]["n"]
        for s, St in self.streams.items():
            if St["n"] > 0:
                allt[s] = St["n"]
        for e in self.ENG:
            E = self.E[e]
            waits = []
            for sid, val in allt.items():
                if sid == e and e == "pe":
                    continue
                if E["known"].get(sid, 0) < val:
                    waits.append((sid, val))
                    E["known"][sid] = val
            if waits:
                E["ops"].append((waits, None, None))
                self.nwaits += len(waits)
        self.last_w = {}
        self.reads = {}
        self.clock = {}

    def emit(self):
        nc = self.nc
        self.barrier()
        with nc.Block() as block:
            def replay(ename):
                def body(engine):
                    for waits, fn, inc in self.E[ename]["ops"]:
                        for sid, val in waits:
                            engine.wait_ge(self._semobj(sid), val)
                        if fn is not None:
                            ins = fn(engine)
                            ins.then_inc(self._semobj(inc[0]), inc[1])
                return body
            block.tensor(replay("pe"))
            block.scalar(replay("act"))
            block.vector(replay("dve"))
            block.gpsimd(replay("pool"))
            block.sync(replay("sp"))


def build_program():
    nc = bass.Bass("TRN2", target_bir_lowering=False)
    skind = "ExternalOutput" if DEBUG else "Internal"

    def din(name, shape, dt=F32):
        return nc.dram_tensor(name, list(shape), dt, kind="ExternalInput").ap()

    def dscr(name, shape, dt=BF16):
        return nc.dram_tensor(name, list(shape), dt, kind=skind).ap()

    x_own = din("x_own", [TQ, D]); x_oth = din("x_oth", [TQ, D]); memx = din("memx", [256, D])
    win_r = din("win_r", [D, NCOL]); g16 = din("g16", [128, 16]); qg4 = din("qg4", [128, 4]); kvg4 = din("kvg4", [128, 4])
    mg16 = din("mg16", [128, 16]); w_uq_r = din("w_uq_r", [512, 8, 256]); w_ukv = din("w_ukv", [512, 2048])
    w1k = din("w1k", [192, 32, 192]); w1v = din("w1v", [128, 32, 128]); pekT = din("pekT", [192, 32]); pevT = din("pevT", [128, 32])
    w2k = din("w2k", [192, 192]); w2v = din("w2v", [128, 128]); w_mkv = din("w_mkv", [D, 1024]); w_out = din("w_out", [D, D])
    fg_rep = din("fg_rep", [128, D])
    ropeCk = din("ropeCk", [64, S]); ropeSk = din("ropeSk", [64, S]); ropeCq = din("ropeCq", [64, TQ]); ropeSq = din("ropeSq", [64, TQ])
    kb_mla = din("kb_mla", [128, 32]); kb_nsa = din("kb_nsa", [128, 4, 32]); cb_nsa = din("cb_nsa", [128, 4, 2])
    qalibi = din("qalibi", [2, 4, TQ], BF16); maskc = din("maskc", [128, 2, TQ], BF16); ovl = din("ovl", [128, 2, 64], BF16)
    selhi = din("selhi", [128, 16, 64]); sello = din("sello", [128, 16, 64]); esel = din("esel", [64, S], BF16)
    tri_c = din("tri_c", [128, 128], BF16); tri_w = din("tri_w", [128, 128], BF16); ident = din("ident", [128, 128], BF16)
    y_out = nc.dram_tensor("y_out", [TQ, D], F32, kind="ExternalOutput").ap()

    ckvnT = dscr("ckvnT", [512, S]); cqnT = dscr("cqnT", [512, TQ]); kpe = dscr("kpe", [64, S])
    zs = dscr("zs", [TQ, D]); qnsaT = dscr("qnsaT", [4, 192, TQ]); kcT = dscr("kcT", [192, S], F32); vcT = dscr("vcT", [128, S], F32)
    ksT = dscr("ksT", [192, S]); kwT = dscr("kwT", [192, S]); vsw = dscr("vsw", [S, 256]); gn = dscr("gn", [TQ, 12], F32)
    qmemT = dscr("qmemT", [512, TQ]); mixT = dscr("mixT", [D, TQ])

    with ExitStack() as st:
        ARENA = 47000
        arena = st.enter_context(nc.sbuf_tensor("arena", [128, ARENA], F32))
        psb = [st.enter_context(nc.psum_tensor(f"psb{i}", [128, 512], F32)) for i in range(8)]
        P = Prog(nc, st)
        off = [0]

        def alloc(n, dt=F32):
            nb = n * (4 if dt == F32 else 2)
            ncol = (nb + 3) // 4
            assert off[0] + ncol <= ARENA, ("SBUF overflow", off[0], ncol)
            a = arena[:, off[0]:off[0] + ncol]
            off[0] += ncol
            if dt != F32:
                a = a.bitcast(dt)[:, 0:n]
            return a

        def PS(i):
            return psb[i].ap()

        def PSB(i):
            return psb[i].ap().bitcast(BF16)

        def mm(out, lhsT, rhs, start, stop, r, w):
            P.op("pe", lambda e: e.matmul(out, lhsT=lhsT, rhs=rhs, start=start, stop=stop), reads=r, writes=w)

        def tp(out, in_, idn, r, w):
            P.op("pe", lambda e: e.transpose(out=out, in_=in_, identity=idn), reads=r, writes=w)

        def act(out, in_, func, r, w, scale=None, bias=None, accum=None):
            kw = {}
            if scale is not None:
                kw["scale"] = scale
            if bias is not None:
                kw["bias"] = bias
            if accum is not None:
                kw["accum_out"] = accum
            P.op("act", lambda e: e.activation(out=out, in_=in_, func=func, **kw), reads=r, writes=w)

        def tsc(eng, out, in0, s1, op0, r, w, s2=None, op1=None):
            if op1 is None:
                P.op(eng, lambda e: e.tensor_scalar(out=out, in0=in0, scalar1=s1, scalar2=None, op0=op0), reads=r, writes=w)
            else:
                P.op(eng, lambda e: e.tensor_scalar(out=out, in0=in0, scalar1=s1, scalar2=s2, op0=op0, op1=op1), reads=r, writes=w)

        def tt(eng, out, in0, in1, op, r, w):
            P.op(eng, lambda e: e.tensor_tensor(out=out, in0=in0, in1=in1, op=op), reads=r, writes=w)

        def stt(out, in0, scalar, in1, op0, op1, r, w):
            P.op("dve", lambda e: e.scalar_tensor_tensor(out=out, in0=in0, scalar=scalar, in1=in1, op0=op0, op1=op1), reads=r, writes=w)

        def cp(eng, out, in_, r, w):
            if eng == "act":
                act(out, in_, AF.Copy, r, w)
            else:
                P.op(eng, lambda e: e.tensor_copy(out=out, in_=in_), reads=r, writes=w)

        def recip(out, in_, r, w):
            P.op("dve", lambda e: e.reciprocal(out=out, in_=in_), reads=r, writes=w)

        def mset(eng, ap, val, w):
            P.op(eng, lambda e: e.memset(ap, val), writes=w)

        def ld(key, out, in_, q="sp", extra_r=()):
            P.dma(q, "L" + key, lambda e: e.dma_start(out=out, in_=in_), reads=list(extra_r), writes=[key])

        def stv(key, out, in_, dkey, q="sp"):
            P.dma(q, "S" + key, lambda e: e.dma_start(out=out, in_=in_), reads=[key], writes=[dkey])

        def rstd_of(ss, key, dim):
            tsc("dve", ss, ss, 1.0 / dim, ALU.mult, [key], [key], s2=EPS, op1=ALU.add)
            act(ss, ss, AF.Sqrt, [key], [key])
            recip(ss, ss, [key], [key])

        rr = {"ps": 0, "ev": 0}

        def next_ps(lo=0, hi=4):
            i = lo + rr["ps"] % (hi - lo)
            rr["ps"] += 1
            return i

        def ev_eng():
            rr["ev"] += 1
            return "act" if rr["ev"] % 2 else "dve"

        ident_t = alloc(128, BF16); tric_t = alloc(128, BF16); triw_t = alloc(128, BF16)
        ld("ident", ident_t, ident); ld("tric", tric_t, tri_c); ld("triw", triw_t, tri_w)
        Ptiles = [alloc(512, BF16) for _ in range(3)]
        persist0 = off[0]

        def phase_A():
            g16_t = alloc(16); ld("g16", g16_t, g16)
            xnT = alloc(16 * TQ, BF16)
            xnT3 = xnT.rearrange("p (c t) -> p c t", c=16)
            xs = [alloc(D) for _ in range(2)]
            xn = [alloc(D, BF16) for _ in range(2)]
            ssx = [alloc(1) for _ in range(2)]
            wst = [alloc(4 * 512) for _ in range(2)]
            wbf = [alloc(16 * 512, BF16) for _ in range(2)]
            stg = [alloc(512) for _ in range(4)]
            sse = [alloc(1) for _ in range(4)]
            rC = [alloc(512) for _ in range(2)]; rS = [alloc(512) for _ in range(2)]
            rt = [alloc(512) for _ in range(2)]

            sc = {"stg": 0, "w": 0, "wp": 0, "rope": 0}

            def next_stg():
                i = sc["stg"] % 4
                sc["stg"] += 1
                return i

            def build_xnT(xsrc):
                for blk in range(16):
                    sl = blk % 2
                    ld(f"xs{sl}", xs[sl], xsrc[blk * 128:(blk + 1) * 128, :])
                    act(xn[sl], xs[sl], AF.Square, [f"xs{sl}"], [f"xn{sl}", f"ssx{sl}"], accum=ssx[sl])
                    rstd_of(ssx[sl], f"ssx{sl}", D)
                    act(xn[sl], xs[sl], AF.Copy, [f"xs{sl}", f"ssx{sl}"], [f"xn{sl}"], scale=ssx[sl])
                    for half in range(2):
                        bank = 6 + half
                        pT = PSB(bank)
                        for c in range(8):
                            cc = half * 8 + c
                            tp(pT[:, c * 128:(c + 1) * 128], xn[sl][:, cc * 128:(cc + 1) * 128], ident_t, [f"xn{sl}", "ident"], [f"ps{bank}"])
                        cp("dve" if half == 0 else "act", xnT3[:, half * 8:(half + 1) * 8, blk * 128:(blk + 1) * 128],
                           pT[:, 0:1024].rearrange("p (c t) -> p c t", c=8), [f"ps{bank}"], [f"xnT_{blk}_{half}"])

            def xnT_keys(t0, nt):
                ks = []
                for blk in range(t0 // 128, (t0 + nt) // 128):
                    ks += [f"xnT_{blk}_0", f"xnT_{blk}_1"]
                return ks

            def load_wtile(c0, ncols):
                sl = sc["w"] % 2
                sc["w"] += 1
                w3 = wbf[sl].rearrange("p (c n) -> p c n", c=16)
                for pc in range(4):
                    ws = sc["wp"] % 2
                    sc["wp"] += 1
                    wv = wst[ws].rearrange("p (c n) -> p c n", c=4)
                    ld(f"wst{ws}", wv[:, :, 0:ncols],
                       win_r[pc * 512:(pc + 1) * 512, c0:c0 + ncols].rearrange("(c p) n -> p c n", p=128))
                    for c in range(4):
                        kc = pc * 4 + c
                        tsc("pool", w3[:, kc, 0:ncols], wv[:, c, 0:ncols], g16_t[:, kc:kc + 1], ALU.mult,
                            [f"wst{ws}", "g16"], [f"wbf{sl}"])
                return w3, f"wbf{sl}"

            def tm_unit(w3, wkey, wc0, n, blk, kind, dest):
                b = next_ps(0, 4)
                ps = PS(b)[:, 0:n]
                for c in range(16):
                    mm(ps, xnT3[:, c, blk * 128:(blk + 1) * 128], w3[:, c, wc0:wc0 + n], c == 0, c == 15,
                       xnT_keys(blk * 128, 128) + [wkey], [f"ps{b}"])
                si = next_stg()
                sk = f"stg{si}"
                if kind == "norm":
                    sb = stg[si].bitcast(BF16)
                    act(sb[:, 0:n], ps, AF.Square, [f"ps{b}"], [sk, f"sse{si}"], accum=sse[si])
                    rstd_of(sse[si], f"sse{si}", n)
                    act(sb[:, 0:n], ps, AF.Copy, [f"ps{b}", f"sse{si}"], [sk], scale=sse[si])
                    bank = 4 + (si % 2)
                    pT = PSB(bank)
                    for c in range(4):
                        tp(pT[:, c * 128:(c + 1) * 128], sb[:, c * 128:(c + 1) * 128], ident_t, [sk, "ident"], [f"ps{bank}"])
                    cp("dve", sb[:, 512:1024], pT[:, 0:512], [f"ps{bank}"], [sk])
                    stv(sk, dest.rearrange("(c p) t -> p c t", p=128)[:, :, blk * 128:(blk + 1) * 128],
                        sb[:, 512:1024].rearrange("p (c t) -> p c t", c=4), "dram_" + kind)
                elif kind == "silu":
                    sb = stg[si].bitcast(BF16)
                    act(sb[:, 0:n], ps, AF.Silu, [f"ps{b}"], [sk])
                    stv(sk, dest[blk * 128:(blk + 1) * 128, :], sb[:, 0:n], "dram_z")
                elif kind == "copy":
                    sb = stg[si].bitcast(BF16)
                    cp(ev_eng(), sb[:, 0:n], ps, [f"ps{b}"], [sk])
                    stv(sk, dest[blk * 128:(blk + 1) * 128, :], sb[:, 0:n], "dram_v")
                elif kind == "sigmoid":
                    act(stg[si][:, 0:n], ps, AF.Sigmoid, [f"ps{b}"], [sk])
                    stv(sk, dest[blk * 128:(blk + 1) * 128, :], stg[si][:, 0:n], "dram_g")

            def fm_mm(w3, wkey, wc0, m, tile):
                b = next_ps(0, 4)
                ps = PS(b)[0:m, :]
                for c in range(16):
                    mm(ps, w3[:, c, wc0:wc0 + m], xnT3[:, c, tile * 512:(tile + 1) * 512], c == 0, c == 15,
                       xnT_keys(tile * 512, 512) + [wkey], [f"ps{b}"])
                return b, ps

            def fm_unit(w3, wkey, wc0, m, tile, t_off, dest_rows, scale, dt):
                b, ps = fm_mm(w3, wkey, wc0, m, tile)
                si = next_stg()
                sk = f"stg{si}"
                sb = stg[si] if dt == F32 else stg[si].bitcast(BF16)
                eng = ev_eng()
                if eng == "act":
                    act(sb[0:m, 0:512], ps, AF.Copy, [f"ps{b}"], [sk], scale=scale)
                else:
                    tsc("dve", sb[0:m, 0:512], ps, scale, ALU.mult, [f"ps{b}"], [sk])
                stv(sk, dest_rows[:, t_off + tile * 512:t_off + (tile + 1) * 512], sb[0:m, 0:512], "dram_fm")

            def rope_unit(w3, wkey, wc0, tile, t_off):
                ba, pa = fm_mm(w3, wkey, wc0, 64, tile)
                bb, pb = fm_mm(w3, wkey, wc0 + 64, 64, tile)
                sl = sc["rope"] % 2
                sc["rope"] += 1
                tok0 = t_off + tile * 512
                ld(f"rC{sl}", rC[sl][0:64, :], ropeCk[:, tok0:tok0 + 512])
                ld(f"rS{sl}", rS[sl][0:64, :], ropeSk[:, tok0:tok0 + 512])
                tt("dve", rC[sl][0:64, :], pa, rC[sl][0:64, :], ALU.mult, [f"ps{ba}", f"rC{sl}"], [f"rC{sl}"])
                tt("dve", rS[sl][0:64, :], pb, rS[sl][0:64, :], ALU.mult, [f"ps{bb}", f"rS{sl}"], [f"rS{sl}"])
                rb = rt[sl].bitcast(BF16)
                tt("dve", rb[0:64, 0:512], rC[sl][0:64, :], rS[sl][0:64, :], ALU.add, [f"rC{sl}", f"rS{sl}"], [f"rt{sl}"])
                stv(f"rt{sl}", kpe[:, tok0:tok0 + 512], rb[0:64, 0:512], "dram_kpe")

            seq = []

            def J(c0, ncols, fn):
                seq.append(("t", c0, ncols, fn))

            for ts_i, (xsrc, t_off, own) in enumerate([(x_oth, 0, False), (x_own, TQ, True)]):
                seq.append(("x", xsrc))

                def j0(w3, wk, t_off=t_off):
                    for blk in range(16):
                        tm_unit(w3, wk, 0, 512, blk, "norm", ckvnT[:, t_off:t_off + TQ])
                J(0, 512, j0)

                def j1(w3, wk, t_off=t_off):
                    for blk in range(16):
                        tm_unit(w3, wk, 0, 256, blk, "copy", vsw[t_off:t_off + TQ, :])
                J(512, 256, j1)

                def j2(w3, wk, t_off=t_off):
                    for tile in range(4):
                        rope_unit(w3, wk, 0, tile, t_off)
                        fm_unit(w3, wk, 128, 128, tile, t_off, kcT[0:128, :], 1.0, F32)
                        fm_unit(w3, wk, 320, 128, tile, t_off, ksT[0:128, :], 1.0, BF16)
                    for tile in range(4):
                        fm_unit(w3, wk, 256, 64, tile, t_off, kcT[128:192, :], 1.0, F32)
                        fm_unit(w3, wk, 448, 64, tile, t_off, ksT[128:192, :], 1.0, BF16)
                J(768, 512, j2)

                def j3(w3, wk, t_off=t_off):
                    for tile in range(4):
                        fm_unit(w3, wk, 0, 128, tile, t_off, kwT[0:128, :], 1.0, BF16)
                        fm_unit(w3, wk, 192, 128, tile, t_off, vcT[0:128, :], 1.0, F32)
                    for tile in range(4):
                        fm_unit(w3, wk, 128, 64, tile, t_off, kwT[128:192, :], 1.0, BF16)
                J(1280, 320, j3)
                if not own:
                    continue

                def j4(w3, wk):
                    for blk in range(16):
                        tm_unit(w3, wk, 0, 512, blk, "norm", cqnT)
                J(1600, 512, j4)
                for zi in range(4):
                    def jz(w3, wk, zi=zi):
                        for blk in range(16):
                            tm_unit(w3, wk, 0, 512, blk, "silu", zs[:, zi * 512:(zi + 1) * 512])
                    J(2112 + zi * 512, 512, jz)

                def j9(w3, wk):
                    for blk in range(16):
                        tm_unit(w3, wk, 0, 12, blk, "sigmoid", gn)
                J(4160, 12, j9)

                def mk_jq(first):
                    def jq(w3, wk):
                        for m_sel in (128, 64):
                            for tile in range(4):
                                for hh in range(4):
                                    for (o, m) in ((0, 128), (128, 64)):
                                        cc = hh * 192 + o
                                        if m != m_sel or (cc < 512) != first:
                                            continue
                                        fm_unit(w3, wk, cc if first else cc - 512, m, tile, 0, qnsaT[hh, o:o + m, :], SC_NSA, BF16)
                    return jq
                J(4172, 512, mk_jq(True))
                J(4684, 256, mk_jq(False))

                def j12(w3, wk):
                    for tile in range(4):
                        for hh in range(4):
                            fm_unit(w3, wk, hh * 128, 128, tile, 0, qmemT[hh * 128:(hh + 1) * 128, :], SC_MEM, BF16)
                J(4940, 512, j12)

            tidx = [i for i, e in enumerate(seq) if e[0] == "t"]
            loaded = {}

            def ensure(i):
                if i not in loaded:
                    loaded[i] = load_wtile(seq[i][1], seq[i][2])
            ensure(tidx[0])
            for pos, e in enumerate(seq):
                if e[0] == "x":
                    build_xnT(e[1])
                    continue
                ensure(pos)
                later = [i for i in tidx if i > pos]
                if later:
                    ensure(later[0])
                w3, wk = loaded[pos]
                e[3](w3, wk)

        def attention(tag, nslots, kbs_fn, qk_fn, bias_fn, v_fn, vcols, fin_fn, fin_b=None):
            items = []
            for s in range(nslots):
                kbs = kbs_fn(s)
                cover = {j: [i for i, k in enumerate(kbs) if k["lo"] <= j * 128 < k["hi"]] for j in range(4)}
                for i, k in enumerate(kbs):
                    items.append((s, i, k, cover, len(kbs)))

            def emit_qk(idx):
                s, i, k, cover, n = items[idx]
                nk = k.get("nk", 128)
                sbk = idx % 2
                Sb = PS(sbk)
                mms = qk_fn(s, k)
                for mi, (lT, rh, clo, chi, rds) in enumerate(mms):
                    mm(Sb[0:nk, clo:chi], lT, rh, mi == 0, mi == len(mms) - 1, rds, [f"ps{sbk}"])

            pending = []
            emit_qk(0)
            for idx in range(len(items)):
                if idx + 1 < len(items):
                    emit_qk(idx + 1)
                s, i, k, cover, n = items[idx]
                lo, hi, nk = k["lo"], k["hi"], k.get("nk", 128)
                sbk = idx % 2
                Sb = PS(sbk)
                pi = idx % 3
                Pt = Ptiles[pi]
                bias = bias_fn(s, k)
                act(Pt[0:nk, lo:hi], Sb[0:nk, lo:hi], AF.Exp, [f"ps{sbk}"] + bias[1], [f"Pt{pi}"], bias=bias[0])
                vap, vkeys = v_fn(s, k)
                for j in range(lo // 128, hi // 128):
                    mm(PS(2 + j)[:, 0:vcols], Pt[0:nk, j * 128:(j + 1) * 128], vap, i == cover[j][0], i == cover[j][-1],
                       [f"Pt{pi}"] + vkeys, [f"ps{2 + j}"])
                if i == 1 and pending:
                    for f in pending:
                        f()
                    pending = []
                if i == n - 1:
                    for j in range(4):
                        fin_fn(s, j, PS(2 + j), f"ps{2 + j}")
                    if fin_b is not None:
                        pending = [(lambda s=s, j=j: fin_b(s, j)) for j in range(4)]
            for f in pending:
                f()

        def causal_kbs(s):
            kbs = [dict(kb=kb, lo=0, hi=512, mask=None) for kb in range(16)]
            for c in range(s):
                kbs += [dict(kb=16 + 4 * c + i, lo=0, hi=512, mask=None) for i in range(4)]
            kbs += [dict(kb=16 + 4 * s + i, lo=128 * i, hi=512, mask=("c", 128 * i)) for i in range(4)]
            return kbs

        def window_kbs(s):
            base = 12 if s == 0 else 16 + 4 * (s - 1)
            kbs = [dict(kb=base + i, lo=0, hi=128 * (i + 1), mask=("w", 128 * i)) for i in range(4)]
            kbs += [dict(kb=16 + 4 * s + i, lo=128 * i, hi=512, mask=("c", 128 * i)) for i in range(4)]
            return kbs

        def mask_mm(k):
            if k["mask"] is None:
                return []
            kind, c0 = k["mask"]
            t = tric_t if kind == "c" else triw_t
            return [(ident_t, t, c0, c0 + 128, ["ident", "tric", "triw"])]

        def phase_B():
            cqn = alloc(4 * TQ, BF16); cqn3 = cqn.rearrange("p (c t) -> p c t", c=4)
            ckv = alloc(4 * S, BF16); ckv3 = ckv.rearrange("p (c t) -> p c t", c=4)
            kpe_t = alloc(S, BF16)
            Cq = alloc(TQ); Sq = alloc(TQ)
            kbm = alloc(32); qg = alloc(4); kvg = alloc(4)
            ld("cqn", cqn3, cqnT.rearrange("(c p) t -> p c t", p=128))
            ld("ckv", ckv3, ckvnT.rearrange("(c p) t -> p c t", p=128))
            ld("kpe", kpe_t[0:64, :], kpe)
            mset("pool", kpe_t[64:128, :], 0.0, ["kpe0"])
            ld("Cq", Cq[0:64, :], ropeCq); ld("Sq", Sq[0:64, :], ropeSq)
            ld("kbm", kbm, kb_mla); ld("qg", qg, qg4); ld("kvg", kvg, kvg4)
            wqs = alloc(4 * 256); wks = alloc(4 * 256)
            wqb = [alloc(4 * 256, BF16) for _ in range(2)]; wkb = [alloc(4 * 256, BF16) for _ in range(2)]
            QnT = [alloc(TQ, BF16) for _ in range(2)]; QrT = [alloc(TQ, BF16) for _ in range(2)]
            KnT = [alloc(S, BF16) for _ in range(2)]; Vt = [alloc(32 * 129, BF16) for _ in range(2)]
            zt = [alloc(16 * 128, BF16) for _ in range(2)]; mst = [alloc(TQ, BF16) for _ in range(2)]
            r1 = alloc(512); r2 = alloc(512)
            rden = [alloc(1) for _ in range(4)]
            mtm = [alloc(128, BF16) for _ in range(4)]
            for p in range(2):
                v3 = Vt[p].rearrange("p (b n) -> p b n", n=129)
                mset("pool", v3[:, :, 128:129], 1.0, [f"V{p}"])
                mset("pool", QrT[p][64:128, :], 0.0, [f"Qr0{p}"])

            def prep(h):
                p = h % 2
                wq3 = wqs.rearrange("p (c n) -> p c n", c=4); wk3 = wks.rearrange("p (c n) -> p c n", c=4)
                wqb3 = wqb[p].rearrange("p (c n) -> p c n", c=4); wkb3 = wkb[p].rearrange("p (c n) -> p c n", c=4)
                ld("wqs", wq3, w_uq_r[:, h, :].rearrange("(c p) n -> p c n", p=128))
                ld("wks", wk3, w_ukv[:, h * 256:(h + 1) * 256].rearrange("(c p) n -> p c n", p=128))
                for c in range(4):
                    tsc("pool", wqb3[:, c, :], wq3[:, c, :], qg[:, c:c + 1], ALU.mult, ["wqs", "qg"], [f"wqb{p}"])
                    tsc("pool", wkb3[:, c, :], wk3[:, c, :], kvg[:, c:c + 1], ALU.mult, ["wks", "kvg"], [f"wkb{p}"])
                ld(f"zt{p}", zt[p].rearrange("p (j f) -> p j f", j=16),
                   zs[:, h * 128:(h + 1) * 128].rearrange("(j p) f -> p j f", p=128))

            prep(0)
            for h in range(8):
                p = h % 2
                wqb3 = wqb[p].rearrange("p (c n) -> p c n", c=4); wkb3 = wkb[p].rearrange("p (c n) -> p c n", c=4)
                for tile in range(4):
                    tsl = slice(tile * 512, (tile + 1) * 512)
                    b = next_ps(6, 8)
                    for c in range(4):
                        mm(PS(b), wqb3[:, c, 0:128], cqn3[:, c, tsl], c == 0, c == 3, [f"wqb{p}", "cqn"], [f"ps{b}"])
                    act(QnT[p][:, tsl], PS(b), AF.Copy, [f"ps{b}"], [f"Qn{p}_{tile}"], scale=SC_MLA)
                    ba = next_ps(6, 8)
                    for c in range(4):
                        mm(PS(ba)[0:64, :], wqb3[:, c, 128:192], cqn3[:, c, tsl], c == 0, c == 3, [f"wqb{p}", "cqn"], [f"ps{ba}"])
                    stt(r1[0:64, :], PS(ba)[0:64, :], SC_MLA, Cq[0:64, tsl], ALU.mult, ALU.mult, [f"ps{ba}", "Cq"], ["r1"])
                    bb = next_ps(6, 8)
                    for c in range(4):
                        mm(PS(bb)[0:64, :], wqb3[:, c, 192:256], cqn3[:, c, tsl], c == 0, c == 3, [f"wqb{p}", "cqn"], [f"ps{bb}"])
                    stt(r2[0:64, :], PS(bb)[0:64, :], SC_MLA, Sq[0:64, tsl], ALU.mult, ALU.mult, [f"ps{bb}", "Sq"], ["r2"])
                    tt("dve", QrT[p][0:64, tsl], r1[0:64, :], r2[0:64, :], ALU.add, ["r1", "r2"], [f"Qr{p}_{tile}"])
                for tile in range(8):
                    tsl = slice(tile * 512, (tile + 1) * 512)
                    b = next_ps(6, 8)
                    for c in range(4):
                        mm(PS(b), wkb3[:, c, 0:128], ckv3[:, c, tsl], c == 0, c == 3, [f"wkb{p}", "ckv"], [f"ps{b}"])
                    cp(ev_eng(), KnT[p][:, tsl], PS(b), [f"ps{b}"], [f"Kn{p}_{tile}"])
                v3 = Vt[p].rearrange("p (b n) -> p b n", n=129)
                for g in range(8):
                    b = next_ps(6, 8)
                    for i in range(4):
                        blk = g * 4 + i
                        for c in range(4):
                            mm(PS(b)[:, i * 128:(i + 1) * 128], ckv3[:, c, blk * 128:(blk + 1) * 128], wkb3[:, c, 128:256],
                               c == 0, c == 3, [f"wkb{p}", "ckv"], [f"ps{b}"])
                    cp(ev_eng(), v3[:, g * 4:(g + 1) * 4, 0:128], PS(b).rearrange("p (i n) -> p i n", i=4), [f"ps{b}"], [f"V{p}_{g}"])

                if h + 1 < 8:
                    prep(h + 1)

                def qk_fn(s, k, p=p):
                    kb, lo, hi = k["kb"], k["lo"], k["hi"]
                    qs = slice(s * 512 + lo, s * 512 + hi)
                    out = [(KnT[p][:, kb * 128:(kb + 1) * 128], QnT[p][:, qs], lo, hi, [f"Kn{p}_{kb // 4}", f"Qn{p}_{s}"]),
                           (kpe_t[:, kb * 128:(kb + 1) * 128], QrT[p][:, qs], lo, hi, ["kpe", "kpe0", f"Qr0{p}", f"Qr{p}_{s}"])]
                    return out + mask_mm(k)

                def bias_fn(s, k):
                    return (kbm[:, k["kb"]:k["kb"] + 1], ["kbm"])

                def v_fn(s, k, p=p, v3=v3):
                    return v3[:, k["kb"], :], [f"V{p}", f"V{p}_{k['kb'] // 4}"]

                def fin_fn(s, j, O, okey, p=p, h=h):
                    jj = s * 4 + j
                    recip(rden[j], O[:, 128:129], [okey], [f"rden{j}"])
                    z3 = zt[p].rearrange("p (j f) -> p j f", j=16)
                    stt(mtm[j], O[:, 0:128], rden[j], z3[:, jj, :], ALU.mult, ALU.mult, [okey, f"rden{j}", f"zt{p}"], [f"mtm{j}"])

                def fin_b(s, j, p=p, h=h):
                    jj = s * 4 + j
                    b = 6 + (jj % 2)
                    tp(PSB(b)[:, 0:128], mtm[j], ident_t, [f"mtm{j}", "ident"], [f"ps{b}"])
                    cp("act", mst[p][:, jj * 128:(jj + 1) * 128], PSB(b)[:, 0:128], [f"ps{b}"], [f"mst{p}"])

                attention("mla", 4, causal_kbs, qk_fn, bias_fn, v_fn, 129, fin_fn, fin_b)
                stv(f"mst{p}", mixT[h * 128:(h + 1) * 128, :], mst[p], "dram_mix")

        def phase_C():
            kca = alloc(256, BF16); kcb = alloc(256, BF16); vca = alloc(2 * 193, BF16)
            vca3 = vca.rearrange("p (b n) -> p b n", n=193)
            mark = off[0]
            w1s = alloc(8 * 192)
            w1a = alloc(32 * 192, BF16); w1b = alloc(32 * 192, BF16)
            w2a = alloc(192, BF16); w2b = alloc(192, BF16); w2s = alloc(192)
            pea = alloc(32); peb = alloc(32)
            kf = alloc(S)
            lo_a = alloc(S, BF16); hi_a = alloc(S, BF16); lo_b = alloc(S, BF16); hi_b = alloc(S, BF16)
            sha = alloc(256, BF16); shb = alloc(256, BF16)

            def load_w1(src, nd, no, dst, dkey):
                d3 = dst.rearrange("p (l o) -> p l o", l=32)
                s3 = w1s.rearrange("p (l o) -> p l o", l=8)
                for q4 in range(4):
                    ld("w1s", s3[0:nd, :, 0:no], src[:, q4 * 8:(q4 + 1) * 8, :])
                    cp("pool", d3[0:nd, q4 * 8:(q4 + 1) * 8, 0:no], s3[0:nd, :, 0:no], ["w1s"], [dkey])

            def addpe(src_rows, nd, pe_t, pekey, lo_t, hi_t, lokey):
                ld("kf", kf[0:nd, :], src_rows)
                k3 = kf.rearrange("p (g l) -> p g l", l=16)
                l3 = lo_t.rearrange("p (g l) -> p g l", l=16); h3 = hi_t.rearrange("p (g l) -> p g l", l=16)
                tt("dve", l3[0:nd], k3[0:nd], pe_t[0:nd, 0:16].unsqueeze(1).to_broadcast([nd, 256, 16]), ALU.add,
                   ["kf", pekey], [lokey])
                tt("pool", h3[0:nd], k3[0:nd], pe_t[0:nd, 16:32].unsqueeze(1).to_broadcast([nd, 256, 16]), ALU.add,
                   ["kf", pekey], [lokey + "h"])

            def layer1(chunks, ochunks, outs, no):
                for oi, (o0, m) in enumerate(ochunks):
                    b = next_ps(6, 8)
                    ps = PS(b)[0:m, 0:255]
                    n_mm = 32 * len(chunks)
                    mi = 0
                    for l in range(32):
                        for (nd, wt, lo_t, hi_t, keys) in chunks:
                            w3 = wt.rearrange("p (l o) -> p l o", l=32)
                            if l < 16:
                                rhs = lo_t.rearrange("p (g l) -> p g l", l=16)[0:nd, 0:255, l]
                            else:
                                rhs = hi_t.rearrange("p (g l) -> p g l", l=16)[0:nd, 1:256, l - 16]
                            mm(ps, w3[0:nd, l, o0:o0 + m], rhs, mi == 0, mi == n_mm - 1, keys, [f"ps{b}"])
                            mi += 1
                    act(outs[oi][0:m, 0:255], ps, AF.Silu, [f"ps{b}"], [f"sh{oi}"])

            ld("pea", pea, pekT[0:128, :]); ld("peb", peb[0:64, :], pekT[128:192, :])
            load_w1(w1k[0:128], 128, 192, w1a, "w1a"); load_w1(w1k[128:192], 64, 192, w1b, "w1b")
            ld("w2s", w2s[:, 0:192], w2k[0:128, :]); cp("pool", w2a[:, 0:192], w2s[:, 0:192], ["w2s"], ["w2a"])
            ld("w2s", w2s[0:64, 0:192], w2k[128:192, :]); cp("pool", w2b[0:64, 0:192], w2s[0:64, 0:192], ["w2s"], ["w2b"])
            addpe(kcT[0:128, :], 128, pea, "pea", lo_a, hi_a, "lo_a")
            addpe(kcT[128:192, :], 64, peb, "peb", lo_b, hi_b, "lo_b")
            layer1([(128, w1a, lo_a, hi_a, ["w1a", "lo_a", "lo_ah"]), (64, w1b, lo_b, hi_b, ["w1b", "lo_b", "lo_bh"])],
                   [(0, 128), (128, 64)], [sha, shb], 192)
            for (o0, m, dst, dk) in ((0, 128, kca, "kca"), (128, 64, kcb, "kcb")):
                b = next_ps(6, 8)
                mm(PS(b)[0:m, 0:255], w2a[:, o0:o0 + m], sha[:, 0:255], True, False, ["w2a", "sh0"], [f"ps{b}"])
                mm(PS(b)[0:m, 0:255], w2b[0:64, o0:o0 + m], shb[0:64, 0:255], False, True, ["w2b", "sh1"], [f"ps{b}"])
                cp("dve", dst[0:m, 0:255], PS(b)[0:m, 0:255], [f"ps{b}"], [dk])
            mset("pool", kcb[64:66, :], 1.0, ["kcb1"])
            ld("pea", pea, pevT)
            load_w1(w1v, 128, 128, w1a, "w1a")
            ld("w2s", w2s[:, 0:128], w2v); cp("pool", w2a[:, 0:128], w2s[:, 0:128], ["w2s"], ["w2a"])
            addpe(vcT, 128, pea, "pea", lo_a, hi_a, "lo_a")
            w1v3 = w1a.rearrange("p (l o) -> p l o", l=32)

            b = next_ps(6, 8)
            for l in range(32):
                if l < 16:
                    rhs = lo_a.rearrange("p (g l) -> p g l", l=16)[:, 0:255, l]
                else:
                    rhs = hi_a.rearrange("p (g l) -> p g l", l=16)[:, 1:256, l - 16]
                mm(PS(b)[:, 0:255], w1v3[:, l, 0:128], rhs, l == 0, l == 31, ["w1a", "lo_a", "lo_ah"], [f"ps{b}"])
            act(sha[:, 0:255], PS(b)[:, 0:255], AF.Silu, [f"ps{b}"], ["sh0"])
            for blk, nk in ((0, 128), (1, 127)):
                b = next_ps(6, 8)
                mm(PS(b)[0:nk, 0:128], sha[:, blk * 128:blk * 128 + nk], w2a[:, 0:128], True, True, ["sh0", "w2a"], [f"ps{b}"])
                cp("dve", vca3[0:nk, blk, 0:128], PS(b)[0:nk, 0:128], [f"ps{b}"], [f"vca{blk}"])
            mset("pool", vca3[:, :, 128:129], 1.0, ["vca_one"])
            ld("vca_ov", vca3[:, :, 129:193], ovl)
            P.barrier()
            off[0] = mark

            qa = alloc(4 * TQ, BF16); qb = alloc(4 * TQ, BF16)
            qa3 = qa.rearrange("p (h t) -> p h t", h=4); qb3 = qb.rearrange("p (h t) -> p h t", h=4)
            kreg = off[0]
            ksa = alloc(S, BF16); ksb = alloc(S, BF16); kwa = alloc(S, BF16); kwb = alloc(S, BF16)
            vs_t = alloc(32 * 129, BF16); vw_t = alloc(32 * 129, BF16)
            vs3 = vs_t.rearrange("p (b n) -> p b n", n=129); vw3 = vw_t.rearrange("p (b n) -> p b n", n=129)
            onsa = alloc(16 * 4 * 128); onsa4 = onsa.rearrange("p (j h d) -> p j h d", j=16, h=4)
            imp = alloc(16 * 64); imp3 = imp.rearrange("p (j m) -> p j m", j=16)
            mkc = alloc(2 * TQ, BF16); mkc3 = mkc.rearrange("p (b t) -> p b t", b=2)
            es = alloc(S, BF16); selbT = alloc(TQ, BF16)
            shi = alloc(16 * 64); slo = alloc(16 * 64)
            shi3 = shi.rearrange("p (j m) -> p j m", j=16); slo3 = slo.rearrange("p (j m) -> p j m", j=16)
            kbn = alloc(4 * 32); kbn3 = kbn.rearrange("p (h b) -> p h b", h=4)
            cbn = alloc(8); cbn3 = cbn.rearrange("p (h b) -> p h b", h=4)
            gt = alloc(16 * 12); gt3 = gt.rearrange("p (j g) -> p j g", j=16)
            rden = [alloc(1) for _ in range(4)]; rg = [alloc(1) for _ in range(4)]
            mtm = [alloc(128, BF16) for _ in range(4)]
            m8a = alloc(8); m8b = alloc(8); impw = alloc(64); impf = alloc(64); selb = alloc(64, BF16)

            ld("qa", qa3, qnsaT[:, 0:128, :].rearrange("h p t -> p h t"))
            ld("qb", qb3[0:64], qnsaT[:, 128:192, :].rearrange("h p t -> p h t"))
            ld("qb2", qb3[64:66], qalibi)
            ld("ksa", ksa, ksT[0:128, :]); ld("ksb", ksb[0:64, :], ksT[128:192, :])
            ld("kwa", kwa, kwT[0:128, :]); ld("kwb", kwb[0:64, :], kwT[128:192, :])
            mset("pool", ksb[64:66, :], 1.0, ["ksb1"]); mset("pool", kwb[64:66, :], 1.0, ["kwb1"])
            ld("vs", vs3[:, :, 0:128], vsw[:, 0:128].rearrange("(b p) f -> p b f", p=128))
            ld("vw", vw3[:, :, 0:128], vsw[:, 128:256].rearrange("(b p) f -> p b f", p=128))
            mset("pool", vs3[:, :, 128:129], 1.0, ["vs1"]); mset("pool", vw3[:, :, 128:129], 1.0, ["vw1"])
            ld("mkc", mkc3, maskc); ld("es", es[0:64, :], esel)
            mset("pool", es[64:128, :], 0.0, ["es0"]); mset("pool", selbT[64:128, :], 0.0, ["selbT0"])
            ld("shi", shi3, selhi); ld("slo", slo3, sello)
            ld("kbn", kbn3, kb_nsa); ld("cbn", cbn3, cb_nsa)
            ld("gt", gt3, gn.rearrange("(j p) g -> p j g", p=128))

            qkeys = ["qa", "qb", "qb2"]

            def mk_qk(h, ka, kb_, kkeys, extra=None):
                def qk_fn(s, k):
                    kb, lo, hi = k["kb"], k["lo"], k["hi"]
                    nk = k.get("nk", 128)
                    qs = slice(s * 512 + lo, s * 512 + hi)
                    ks = slice(kb * 128, kb * 128 + nk)
                    out = [(ka[:, ks], qa3[:, h, qs], lo, hi, kkeys + qkeys),
                           (kb_[0:66, ks], qb3[0:66, h, qs], lo, hi, kkeys + qkeys)]
                    if extra is not None:
                        out += extra(s, k)
                    return out + mask_mm(k)
                return qk_fn

            def nsa_fin(h, branch):
                def fin_fn(s, j, O, okey):
                    jj = s * 4 + j
                    tsc("dve", rden[j], O[:, 128:129], 1e-20, ALU.add, [okey], [f"rden{j}"])
                    recip(rden[j], rden[j], [f"rden{j}"], [f"rden{j}"])
                    tt("dve", rg[j], rden[j], gt3[:, jj, h * 3 + branch:h * 3 + branch + 1], ALU.mult, [f"rden{j}", "gt"], [f"rg{j}"])
                    if branch == 0:
                        tsc("dve", onsa4[:, jj, h, :], O[:, 0:128], rg[j], ALU.mult, [okey, f"rg{j}"], [f"onsa{jj}_{h}"])
                        if h == 0:
                            tsc("dve", imp3[:, jj, :], O[:, 129:193], rden[j], ALU.mult, [okey, f"rden{j}"], [f"imp{jj}"])
                        else:
                            stt(imp3[:, jj, :], O[:, 129:193], rden[j], imp3[:, jj, :], ALU.mult, ALU.add,
                                [okey, f"rden{j}", f"imp{jj}"], [f"imp{jj}"])
                    else:
                        stt(onsa4[:, jj, h, :], O[:, 0:128], rg[j], onsa4[:, jj, h, :], ALU.mult, ALU.add,
                            [okey, f"rg{j}", f"onsa{jj}_{h}"], [f"onsa{jj}_{h}"])
                return fin_fn

            def cmp_kbs(s):
                return [dict(kb=0, lo=0, hi=512, nk=128, mask=None), dict(kb=1, lo=0, hi=512, nk=127, mask=None)]

            for h in range(4):
                def extra(s, k):
                    nk = k["nk"]
                    return [(ident_t[0:nk, 0:nk], mkc3[0:nk, k["kb"], s * 512:(s + 1) * 512], 0, 512, ["ident", "mkc"])]

                def bias_fn(s, k, h=h):
                    return (cbn3[0:k["nk"], h, k["kb"]:k["kb"] + 1], ["cbn"])

                def v_fn(s, k):
                    return vca3[0:k["nk"], k["kb"], :], ["vca0", "vca1", "vca_one", "vca_ov"]
                attention("cmp", 4, cmp_kbs, mk_qk(h, kca, kcb, ["kca", "kcb", "kcb1"], extra), bias_fn, v_fn, 193, nsa_fin(h, 0))

            for jj in range(16):
                tt("dve", impf, imp3[:, jj, :], shi3[:, jj, :], ALU.max, [f"imp{jj}", "shi"], ["impf"])
                tt("dve", impf, impf, slo3[:, jj, :], ALU.min, ["impf", "slo"], ["impf"])
                P.op("dve", lambda e: e.max(out=m8a, in_=impf), reads=["impf"], writes=["m8a"])
                P.op("dve", lambda e: e.match_replace(out=impw, in_to_replace=m8a, in_values=impf, imm_value=-3e9),
                     reads=["impf", "m8a"], writes=["impw"])
                P.op("dve", lambda e: e.max(out=m8b, in_=impw), reads=["impw"], writes=["m8b"])
                tsc("dve", selb, impf, m8b[:, 7:8], ALU.is_lt, ["impf", "m8b"], ["selb"], s2=NEG, op1=ALU.mult)
                b = 6 + (jj % 2)
                tp(PSB(b)[0:64, 0:128], selb, ident_t, ["selb", "ident"], [f"ps{b}"])
                cp("act", selbT[0:64, jj * 128:(jj + 1) * 128], PSB(b)[0:64, 0:128], [f"ps{b}"], ["selbT"])

            for h in range(4):
                def extra(s, k):
                    kb, lo, hi = k["kb"], k["lo"], k["hi"]
                    return [(es[:, kb * 128:(kb + 1) * 128], selbT[:, s * 512 + lo:s * 512 + hi], lo, hi, ["es", "es0", "selbT", "selbT0"])]

                def bias_fn(s, k, h=h):
                    return (kbn3[:, h, k["kb"]:k["kb"] + 1], ["kbn"])

                def v_fn(s, k):
                    return vs3[:, k["kb"], :], ["vs", "vs1"]
                attention("slc", 4, causal_kbs, mk_qk(h, ksa, ksb, ["ksa", "ksb", "ksb1"], extra), bias_fn, v_fn, 129, nsa_fin(h, 1))

            for h in range(4):
                def bias_fn(s, k, h=h):
                    return (kbn3[:, h, k["kb"]:k["kb"] + 1], ["kbn"])

                def v_fn(s, k):
                    return vw3[:, k["kb"], :], ["vw", "vw1"]
                attention("win", 4, window_kbs, mk_qk(h, kwa, kwb, ["kwa", "kwb", "kwb1"]), bias_fn, v_fn, 129, nsa_fin(h, 2))

            P.barrier()
            save = off[0]
            off[0] = kreg
            zn = alloc(16 * 512, BF16); zn3 = zn.rearrange("p (j f) -> p j f", j=16)
            mst = alloc(4 * TQ, BF16); mst3 = mst.rearrange("p (h t) -> p h t", h=4)
            off[0] = save
            ld("zn", zn3, zs[:, 1024:1536].rearrange("(j p) f -> p j f", p=128))
            for h in range(4):
                for jj in range(16):
                    j = jj % 4
                    tt("dve", mtm[j], onsa4[:, jj, h, :], zn3[:, jj, h * 128:(h + 1) * 128], ALU.mult, [f"onsa{jj}_{h}", "zn"], [f"mtm{j}"])
                    b = 6 + (jj % 2)
                    tp(PSB(b)[:, 0:128], mtm[j], ident_t, [f"mtm{j}", "ident"], [f"ps{b}"])
                    cp("act", mst3[:, h, jj * 128:(jj + 1) * 128], PSB(b)[:, 0:128], [f"ps{b}"], ["mstn"])
            stv("mstn", mixT[1024:1536, :].rearrange("(h p) t -> p h t", p=128), mst3, "dram_mix")

        def phase_M():
            mg = alloc(16); ld("mg", mg, mg16)
            xs = [alloc(D) for _ in range(2)]; xn = [alloc(D, BF16) for _ in range(2)]; ssx = [alloc(1) for _ in range(2)]
            xmT = alloc(16 * 256, BF16); xm3 = xmT.rearrange("p (c t) -> p c t", c=16)
            wst = [alloc(4 * 512) for _ in range(2)]
            wbf = [alloc(16 * 512, BF16) for _ in range(2)]
            kmT = alloc(4 * 256, BF16); km3 = kmT.rearrange("p (h t) -> p h t", h=4)
            vm = alloc(2 * 4 * 129, BF16); vm4 = vm.rearrange("p (b h n) -> p b h n", b=2, h=4)
            qm = alloc(4 * TQ, BF16); qm3 = qm.rearrange("p (h t) -> p h t", h=4)
            zm = alloc(16 * 512, BF16); zm3 = zm.rearrange("p (j f) -> p j f", j=16)
            mst = alloc(4 * TQ, BF16); mst3 = mst.rearrange("p (h t) -> p h t", h=4)
            rden = [alloc(1) for _ in range(4)]; mtm = [alloc(128, BF16) for _ in range(4)]
            ld("qm", qm3, qmemT.rearrange("(h p) t -> p h t", p=128))
            ld("zm", zm3, zs[:, 1536:2048].rearrange("(j p) f -> p j f", p=128))
            mset("pool", vm4[:, :, :, 128:129], 1.0, ["vm1"])
            for blk in range(2):
                sl = blk
                ld(f"xs{sl}", xs[sl], memx[blk * 128:(blk + 1) * 128, :])
                act(xn[sl], xs[sl], AF.Square, [f"xs{sl}"], [f"xn{sl}", f"ssx{sl}"], accum=ssx[sl])
                rstd_of(ssx[sl], f"ssx{sl}", D)
                act(xn[sl], xs[sl], AF.Copy, [f"xs{sl}", f"ssx{sl}"], [f"xn{sl}"], scale=ssx[sl])
                for half in range(2):
                    bank = 6 + half
                    pT = PSB(bank)
                    for c in range(8):
                        cc = half * 8 + c
                        tp(pT[:, c * 128:(c + 1) * 128], xn[sl][:, cc * 128:(cc + 1) * 128], ident_t, [f"xn{sl}", "ident"], [f"ps{bank}"])
                    cp("dve", xm3[:, half * 8:(half + 1) * 8, blk * 128:(blk + 1) * 128],
                       pT[:, 0:1024].rearrange("p (c t) -> p c t", c=8), [f"ps{bank}"], [f"xm_{blk}_{half}"])
            xmk = [f"xm_{b}_{hf}" for b in range(2) for hf in range(2)]
            wp = 0
            for wt in range(2):
                w3 = wbf[wt].rearrange("p (c n) -> p c n", c=16)
                for pc in range(4):
                    ws = wp % 2
                    wp += 1
                    wv = wst[ws].rearrange("p (c n) -> p c n", c=4)
                    ld(f"wst{ws}", wv, w_mkv[pc * 512:(pc + 1) * 512, wt * 512:(wt + 1) * 512].rearrange("(c p) n -> p c n", p=128))
                    for c in range(4):
                        kc = pc * 4 + c
                        tsc("pool", w3[:, kc, :], wv[:, c, :], mg[:, kc:kc + 1], ALU.mult, [f"wst{ws}", "mg"], [f"wbf{wt}"])
            wk3 = wbf[0].rearrange("p (c n) -> p c n", c=16); wv3 = wbf[1].rearrange("p (c n) -> p c n", c=16)
            for hh in range(4):
                b = next_ps(6, 8)
                for c in range(16):
                    mm(PS(b)[:, 0:256], wk3[:, c, hh * 128:(hh + 1) * 128], xm3[:, c, :], c == 0, c == 15, ["wbf0"] + xmk, [f"ps{b}"])
                cp("dve", km3[:, hh, :], PS(b)[:, 0:256], [f"ps{b}"], ["km"])
            for blk in range(2):
                b = next_ps(6, 8)
                for c in range(16):
                    mm(PS(b), xm3[:, c, blk * 128:(blk + 1) * 128], wv3[:, c, :], c == 0, c == 15, ["wbf1"] + xmk, [f"ps{b}"])
                cp("dve", vm4[:, blk, :, 0:128], PS(b).rearrange("p (h n) -> p h n", h=4), [f"ps{b}"], ["vmv"])

            def mem_kbs(s):
                return [dict(kb=0, lo=0, hi=512, mask=None), dict(kb=1, lo=0, hi=512, mask=None)]

            for h in range(4):
                def qk_fn(s, k, h=h):
                    kb = k["kb"]
                    return [(km3[:, h, kb * 128:(kb + 1) * 128], qm3[:, h, s * 512:(s + 1) * 512], 0, 512, ["km", "qm"])]

                def bias_fn(s, k):
                    return (None, [])

                def v_fn(s, k, h=h):
                    return vm4[:, k["kb"], h, :], ["vmv", "vm1"]

                def fin_fn(s, j, O, okey, h=h):
                    jj = s * 4 + j
                    recip(rden[j], O[:, 128:129], [okey], [f"rden{j}"])
                    stt(mtm[j], O[:, 0:128], rden[j], zm3[:, jj, h * 128:(h + 1) * 128], ALU.mult, ALU.mult,
                        [okey, f"rden{j}", "zm"], [f"mtm{j}"])

                def fin_b(s, j, h=h):
                    jj = s * 4 + j
                    b = 6 + (jj % 2)
                    tp(PSB(b)[:, 0:128], mtm[j], ident_t, [f"mtm{j}", "ident"], [f"ps{b}"])
                    cp("act", mst3[:, h, jj * 128:(jj + 1) * 128], PSB(b)[:, 0:128], [f"ps{b}"], ["mstm"])
                attention("mem", 4, mem_kbs, qk_fn, bias_fn, v_fn, 129, fin_fn, fin_b)
            stv("mstm", mixT[1536:2048, :].rearrange("(h p) t -> p h t", p=128), mst3, "dram_mix")

        def phase_D():
            wob = alloc(16 * D, BF16); wo3 = wob.rearrange("p (c n) -> p c n", c=16)
            wst = [alloc(4 * 512) for _ in range(2)]
            fg = alloc(D); ld("fg", fg, fg_rep)
            mx = [alloc(16 * 128, BF16) for _ in range(2)]
            xs = [alloc(D) for _ in range(2)]; yt = [alloc(D) for _ in range(2)]; ssy = [alloc(1) for _ in range(2)]
            junk = alloc(D, BF16)
            wp = 0
            for nt in range(4):
                for pc in range(4):
                    ws = wp % 2
                    wp += 1
                    wv = wst[ws].rearrange("p (c n) -> p c n", c=4)
                    ld(f"wst{ws}", wv, w_out[pc * 512:(pc + 1) * 512, nt * 512:(nt + 1) * 512].rearrange("(c p) n -> p c n", p=128))
                    cp("pool", wo3[:, pc * 4:(pc + 1) * 4, nt * 512:(nt + 1) * 512], wv, [f"wst{ws}"], [f"wob{nt}"])
            wokeys = [f"wob{nt}" for nt in range(4)]
            for blk in range(16):
                sl = blk % 2
                m3 = mx[sl].rearrange("p (c t) -> p c t", c=16)
                ld(f"mx{sl}", m3, mixT[:, blk * 128:(blk + 1) * 128].rearrange("(c p) t -> p c t", p=128))
                ld(f"xs{sl}", xs[sl], x_own[blk * 128:(blk + 1) * 128, :])
                for nt in range(4):
                    b = (blk * 4 + nt) % 6
                    for c in range(16):
                        mm(PS(b), m3[:, c, :], wo3[:, c, nt * 512:(nt + 1) * 512], c == 0, c == 15, [f"mx{sl}", wokeys[nt]], [f"ps{b}"])
                    tt("dve", yt[sl][:, nt * 512:(nt + 1) * 512], PS(b), xs[sl][:, nt * 512:(nt + 1) * 512], ALU.add,
                       [f"ps{b}", f"xs{sl}"], [f"yt{sl}"])
                act(junk, yt[sl], AF.Square, [f"yt{sl}"], ["junk", f"ssy{sl}"], accum=ssy[sl])
                rstd_of(ssy[sl], f"ssy{sl}", D)
                act(yt[sl], yt[sl], AF.Copy, [f"yt{sl}", f"ssy{sl}"], [f"yt{sl}"], scale=ssy[sl])
                tt("pool", yt[sl], yt[sl], fg, ALU.mult, [f"yt{sl}", "fg"], [f"yt{sl}"])
                stv(f"yt{sl}", y_out[blk * 128:(blk + 1) * 128, :], yt[sl], "dram_y", q="pool")

        phases = [("A", phase_A), ("B", phase_B), ("C", phase_C), ("M", phase_M), ("D", phase_D)]
        for name, fn in phases:
            off[0] = persist0
            fn()
            P.barrier()
            if STOP_AFTER == name:
                break
        P.emit()
        print(f"[kernel] ops={P.nops} waits={P.nwaits} sems={len(P.streams) + 5}", flush=True)
    return nc


def _bf(a):
    return np.ascontiguousarray(a).astype(ml_dtypes.bfloat16)


def _prep_shared(inp):
    w_in = np.asarray(inp["w_in"])[0]
    o = np.cumsum([0, 512, 512, 64, 1024, 768, 192, 128, 192, 128, 192, 128, 12, 512, 512, 512])
    seg = {n: (o[i], o[i + 1]) for i, n in enumerate(
        ["c_q", "c_kv", "k_rope", "z_mla", "q_nsa", "k_c", "v_c", "k_s", "v_s", "k_w", "v_w", "g_nsa", "z_nsa", "q_mem", "z_mem"])}

    def cols(n):
        return np.arange(seg[n][0], seg[n][1])
    kr = cols("k_rope")
    order = np.concatenate([
        cols("c_kv"), cols("v_s"), cols("v_w"),
        kr, np.concatenate([kr[32:], kr[:32]]), cols("k_c"), cols("k_s"),
        cols("k_w"), cols("v_c"),
        cols("c_q"), cols("z_mla"), cols("z_nsa"), cols("z_mem"), cols("g_nsa"),
        cols("q_nsa"), cols("q_mem")])
    assert order.shape[0] == NCOL
    sh = {}
    sh["win_r"] = np.ascontiguousarray(w_in[:, order])
    sh["g16"] = np.ascontiguousarray(np.asarray(inp["norm_g"])[0].reshape(16, 128).T)
    sh["qg4"] = np.ascontiguousarray(np.asarray(inp["q_norm_g"])[0].reshape(4, 128).T)
    sh["kvg4"] = np.ascontiguousarray(np.asarray(inp["kv_norm_g"])[0].reshape(4, 128).T)
    sh["mg16"] = np.ascontiguousarray(np.asarray(inp["mem_norm_g"])[0].reshape(16, 128).T)
    wuq = np.asarray(inp["w_uq"])[0].reshape(512, 8, 192)
    sh["w_uq_r"] = np.ascontiguousarray(np.concatenate(
        [wuq[:, :, 0:128], wuq[:, :, 128:192], wuq[:, :, 160:192], wuq[:, :, 128:160]], axis=2))
    sh["w_ukv"] = np.ascontiguousarray(np.asarray(inp["w_ukv"])[0])
    sh["w1k"] = np.ascontiguousarray(np.asarray(inp["cmp_w1k"])[0].reshape(32, 192, 192).transpose(1, 0, 2))
    sh["w1v"] = np.ascontiguousarray(np.asarray(inp["cmp_w1v"])[0].reshape(32, 128, 128).transpose(1, 0, 2))
    sh["pekT"] = np.ascontiguousarray(np.asarray(inp["cmp_pe_k"])[0].T)
    sh["pevT"] = np.ascontiguousarray(np.asarray(inp["cmp_pe_v"])[0].T)
    sh["w2k"] = np.ascontiguousarray(np.asarray(inp["cmp_w2k"])[0])
    sh["w2v"] = np.ascontiguousarray(np.asarray(inp["cmp_w2v"])[0])
    sh["w_mkv"] = np.ascontiguousarray(np.asarray(inp["w_mem_kv"])[0])
    sh["w_out"] = np.ascontiguousarray(np.asarray(inp["w_out"])[0])
    sh["fg_rep"] = np.ascontiguousarray(np.broadcast_to(np.asarray(inp["final_norm_g"])[None, :], (128, D)))
    k = np.arange(128)[:, None]
    q = np.arange(128)[None, :]
    sh["tri_c"] = _bf(np.where(k > q, NEG, 0.0))
    sh["tri_w"] = _bf(np.where(q >= k, NEG, 0.0))
    sh["ident"] = _bf(np.eye(128))
    kk = np.arange(S)
    sh["esel"] = _bf((kk[None, :] // 64 == np.arange(64)[:, None]).astype(np.float32))
    return sh


def _rope_tables(pos):
    inv_freq = (10000.0 ** (-np.arange(0, 64, 2, dtype=np.float32) / 64)).astype(np.float32)
    ang = pos.astype(np.float32)[:, None] * inv_freq[None, :]
    cos = np.cos(ang).astype(np.float32).T
    sin = np.sin(ang).astype(np.float32).T
    return (np.ascontiguousarray(np.concatenate([cos, cos], 0)),
            np.ascontiguousarray(np.concatenate([-sin, sin], 0)))


def _prep_half(hf):
    t = {}
    tq = 2048 * hf + np.arange(TQ)
    kvalid = np.concatenate([np.full(TQ, hf == 1), np.ones(TQ, bool)])
    kpos = np.concatenate([np.arange(TQ), tq]).astype(np.int64)
    kpos_eff = np.where(kvalid, kpos, 0)
    t["ropeCk"], t["ropeSk"] = _rope_tables(kpos)
    t["ropeCq"], t["ropeSq"] = _rope_tables(tq)
    slopes = (2.0 ** (-8.0 * np.arange(1, 5) / 4)).astype(np.float32)
    mb = np.where(kvalid, 0.0, NEG).astype(np.float32)
    t["kb_mla"] = np.ascontiguousarray(mb.reshape(32, 128).T)
    kbn = slopes[None, :, None] * kpos_eff.reshape(32, 128).T[:, None, :].astype(np.float32) + mb.reshape(32, 128).T[:, None, :]
    t["kb_nsa"] = np.ascontiguousarray(kbn.astype(np.float32))
    qa = np.stack([-64.0 * slopes[:, None] * (tq // 64)[None, :], -slopes[:, None] * (tq % 64)[None, :]], 0)
    t["qalibi"] = _bf(qa)
    n_ = np.arange(256)
    if hf == 1:
        cvalid = n_ < 255
        nat = n_
    else:
        cvalid = (n_ >= 128) & (n_ < 255)
        nat = n_ - 128
    cpos = 16.0 * nat + 15.5
    cend = 16 * nat + 31
    cb = slopes[None, :] * np.where(cvalid, cpos, 0.0)[:, None] + np.where(cvalid, 0.0, NEG)[:, None]
    t["cb_nsa"] = np.ascontiguousarray(cb.reshape(2, 128, 4).transpose(1, 2, 0).astype(np.float32))
    mc = np.where(cvalid[:, None] & (cend[:, None] <= tq[None, :]), 0.0, NEG)
    t["maskc"] = _bf(mc.reshape(2, 128, TQ).transpose(1, 0, 2))
    j_ = np.arange(64)
    if hf == 1:
        jvalid = np.ones(64, bool)
        natj = j_
    else:
        jvalid = j_ >= 32
        natj = j_ - 32
    cst = 16 * nat
    sst = 64 * natj
    ov = (cst[:, None] < sst[None, :] + 64) & (cst[:, None] + 32 > sst[None, :]) & cvalid[:, None] & jvalid[None, :]
    t["ovl"] = _bf(ov.astype(np.float32).reshape(2, 128, 64).transpose(1, 0, 2))
    cur = tq // 64
    forced = jvalid[None, :] & ((natj[None, :] == 0) | (natj[None, :] == cur[:, None]) | (natj[None, :] == cur[:, None] - 1))
    fut = (~jvalid[None, :]) | (natj[None, :] > cur[:, None])
    t["selhi"] = np.ascontiguousarray(np.where(forced, 1e9, 0.0).astype(np.float32).reshape(16, 128, 64).transpose(1, 0, 2))
    t["sello"] = np.ascontiguousarray(np.where(fut, -1e9, 1e9).astype(np.float32).reshape(16, 128, 64).transpose(1, 0, 2))
    return t


_CACHE = {}


def kernel(**inputs):
    x = np.asarray(inputs["x"], dtype=np.float32)
    mem = np.asarray(inputs["mem"], dtype=np.float32)
    if "nc" not in _CACHE:
        _CACHE["nc"] = build_program()
        _CACHE["half"] = [_prep_half(0), _prep_half(1)]
    nc = _CACHE["nc"]
    sh = _prep_shared(inputs)
    cores = [int(c) for c in os.environ.get("MK_CORES", "0,1,2,3,4,5,6,7").split(",")]
    in_maps = []
    for c in cores:
        b, hf = c // 2, c % 2
        m = dict(sh)
        m.update(_CACHE["half"][hf])
        m["x_own"] = np.ascontiguousarray(x[b, 2048 * hf:2048 * hf + 2048])
        m["x_oth"] = np.ascontiguousarray(x[b, 0:2048])
        m["memx"] = np.ascontiguousarray(mem[b])
        in_maps.append(m)
    res = run_bass_kernel_spmd(nc, in_maps, core_ids=list(range(len(cores))))
    if DEBUG:
        _CACHE["dbg"] = res.results
    out = np.zeros((4, S, D), np.float32)
    for i, c in enumerate(cores):
        b, hf = c // 2, c % 2
        out[b, 2048 * hf:2048 * hf + 2048] = res.results[i]["y_out"]
    return out
```

```python
import os
from contextlib import ExitStack
import numpy as np
import ml_dtypes
import concourse.bass as bass
import concourse.mybir as mybir
from concourse.bass_utils import run_bass_kernel_spmd

F32 = mybir.dt.float32
BF16 = mybir.dt.bfloat16
AF = mybir.ActivationFunctionType
ALU = mybir.AluOpType

D = 2048
S = 4096
TQ = 2048
NEG = -30000.0
EPS = 1e-6
SC_MLA = 192 ** -0.5
SC_NSA = 192 ** -0.5
SC_MEM = 128 ** -0.5
NCOL = 5452
DEBUG = bool(int(os.environ.get("MK_DEBUG", "0")))
STOP_AFTER = os.environ.get("MK_STOP", "")


class Prog:
    ENG = ("pe", "act", "dve", "pool", "sp")

    def __init__(self, nc, stack):
        self.nc = nc
        self.stack = stack
        self.E = {}
        for e in self.ENG:
            self.E[e] = dict(sem=self._sem("s_" + e), n=0, known={}, ops=[])
        self.streams = {}
        self.last_w = {}
        self.reads = {}
        self.clock = {}
        self.nwaits = 0
        self.nops = 0

    def _sem(self, name):
        return self.stack.enter_context(self.nc.semaphore(name))

    def stream(self, name):
        if name not in self.streams:
            self.streams[name] = dict(sem=self._sem("d_" + name), n=0)
        return self.streams[name]

    def _deps(self, eng, reads, writes):
        toks = []
        for k in reads:
            t = self.last_w.get(k)
            if t is not None:
                toks.append(t)
        for k in writes:
            t = self.last_w.get(k)
            if t is not None:
                toks.append(t)
            toks.extend(self.reads.get(k, ()))
        known = self.E[eng]["known"]
        need = {}
        for (sid, val) in toks:
            if eng == "pe" and sid == "pe":
                continue
            if known.get(sid, 0) >= val:
                continue
            if need.get(sid, 0) < val:
                need[sid] = val
        waits = []
        for sid, val in need.items():
            if known.get(sid, 0) >= val:
                continue
            ck = self.clock.get((sid, val))
            if ck:
                for a, b in ck.items():
                    if a == eng:
                        continue
                    if known.get(a, 0) < b:
                        known[a] = b
            known[sid] = max(known.get(sid, 0), val)
            waits.append((sid, val))
        return waits

    def _semobj(self, sid):
        if sid in self.E:
            return self.E[sid]["sem"]
        return self.streams[sid]["sem"]

    def op(self, eng, fn, reads=(), writes=()):
        waits = self._deps(eng, reads, writes)
        E = self.E[eng]
        E["n"] += 1
        tok = (eng, E["n"])
        ck = dict(E["known"])
        ck[eng] = E["n"]
        self.clock[tok] = ck
        E["ops"].append((waits, fn, (eng, 1)))
        self._record(tok, reads, writes)
        self.nwaits += len(waits)
        self.nops += 1
        return tok

    def dma(self, q, stream, fn, reads=(), writes=()):
        waits = self._deps(q, reads, writes)
        St = self.stream(stream)
        St["n"] += 16
        tok = (stream, St["n"])
        ck = dict(self.E[q]["known"])
        ck[stream] = St["n"]
        self.clock[tok] = ck
        self.E[q]["ops"].append((waits, fn, (stream, 16)))
        self._record(tok, reads, writes)
        self.nwaits += len(waits)
        self.nops += 1
        return tok

    def _record(self, tok, reads, writes):
        for k in reads:
            self.reads.setdefault(k, []).append(tok)
        for k in writes:
            self.last_w[k] = tok
            self.reads[k] = []

    def barrier(self):
        allt = {}
        for e in self.ENG:
            if self.E[e]["n"] > 0:
                allt[e] = self.E[e]["n"]
        for s, St in self.streams.items():
            if St["n"] > 0:
                allt[s] = St["n"]
        for e in self.ENG:
            E = self.E[e]
            waits = []
            for sid, val in allt.items():
                if sid == e and e == "pe":
                    continue
                if E["known"].get(sid, 0) < val:
                    waits.append((sid, val))
                    E["known"][sid] = val
            if waits:
                E["ops"].append((waits, None, None))
                self.nwaits += len(waits)
        self.last_w = {}
        self.reads = {}
        self.clock = {}

    def emit(self):
        nc = self.nc
        self.barrier()
        with nc.Block() as block:
            def replay(ename):
                def body(engine):
                    for waits, fn, inc in self.E[ename]["ops"]:
                        for sid, val in waits:
                            engine.wait_ge(self._semobj(sid), val)
                        if fn is not None:
                            ins = fn(engine)
                            ins.then_inc(self._semobj(inc[0]), inc[1])
                return body
            block.tensor(replay("pe"))
            block.scalar(replay("act"))
            block.vector(replay("dve"))
            block.gpsimd(replay("pool"))
            block.sync(replay("sp"))


def build_program():
    nc = bass.Bass("TRN2", target_bir_lowering=False)
    skind = "ExternalOutput" if DEBUG else "Internal"

    def din(name, shape, dt=F32):
        return nc.dram_tensor(name, list(shape), dt, kind="ExternalInput").ap()

    def dscr(name, shape, dt=BF16):
        return nc.dram_tensor(name, list(shape), dt, kind=skind).ap()

    x_own = din("x_own", [TQ, D]); x_oth = din("x_oth", [TQ, D]); memx = din("memx", [256, D])
    win_r = din("win_r", [D, NCOL]); g16 = din("g16", [128, 16]); qg4 = din("qg4", [128, 4]); kvg4 = din("kvg4", [128, 4])
    mg16 = din("mg16", [128, 16]); w_uq_r = din("w_uq_r", [512, 8, 256]); w_ukv = din("w_ukv", [512, 2048])
    w1k = din("w1k", [192, 32, 192]); w1v = din("w1v", [128, 32, 128]); pekT = din("pekT", [192, 32]); pevT = din("pevT", [128, 32])
    w2k = din("w2k", [192, 192]); w2v = din("w2v", [128, 128]); w_mkv = din("w_mkv", [D, 1024]); w_out = din("w_out", [D, D])
    fg_rep = din("fg_rep", [128, D])
    ropeCk = din("ropeCk", [64, S]); ropeSk = din("ropeSk", [64, S]); ropeCq = din("ropeCq", [64, TQ]); ropeSq = din("ropeSq", [64, TQ])
    kb_mla = din("kb_mla", [128, 32]); kb_nsa = din("kb_nsa", [128, 4, 32]); cb_nsa = din("cb_nsa", [128, 4, 2])
    qalibi = din("qalibi", [2, 4, TQ], BF16); maskc = din("maskc", [128, 2, TQ], BF16); ovl = din("ovl", [128, 2, 64], BF16)
    selhi = din("selhi", [128, 16, 64]); sello = din("sello", [128, 16, 64]); esel = din("esel", [64, S], BF16)
    tri_c = din("tri_c", [128, 128], BF16); tri_w = din("tri_w", [128, 128], BF16); ident = din("ident", [128, 128], BF16)
    y_out = nc.dram_tensor("y_out", [TQ, D], F32, kind="ExternalOutput").ap()

    ckvnT = dscr("ckvnT", [512, S]); cqnT = dscr("cqnT", [512, TQ]); kpe = dscr("kpe", [64, S])
    zs = dscr("zs", [TQ, D]); qnsaT = dscr("qnsaT", [4, 192, TQ]); kcT = dscr("kcT", [192, S], F32); vcT = dscr("vcT", [128, S], F32)
    ksT = dscr("ksT", [192, S]); kwT = dscr("kwT", [192, S]); vsw = dscr("vsw", [S, 256]); gn = dscr("gn", [TQ, 12], F32)
    qmemT = dscr("qmemT", [512, TQ]); mixT = dscr("mixT", [D, TQ])

    with ExitStack() as st:
        ARENA = 47000
        arena = st.enter_context(nc.sbuf_tensor("arena", [128, ARENA], F32))
        psb = [st.enter_context(nc.psum_tensor(f"psb{i}", [128, 512], F32)) for i in range(8)]
        P = Prog(nc, st)
        off = [0]

        def alloc(n, dt=F32):
            nb = n * (4 if dt == F32 else 2)
            ncol = (nb + 3) // 4
            assert off[0] + ncol <= ARENA, ("SBUF overflow", off[0], ncol)
            a = arena[:, off[0]:off[0] + ncol]
            off[0] += ncol
            if dt != F32:
                a = a.bitcast(dt)[:, 0:n]
            return a

        def PS(i):
            return psb[i].ap()

        def PSB(i):
            return psb[i].ap().bitcast(BF16)

        def mm(out, lhsT, rhs, start, stop, r, w):
            P.op("pe", lambda e: e.matmul(out, lhsT=lhsT, rhs=rhs, start=start, stop=stop), reads=r, writes=w)

        def tp(out, in_, idn, r, w):
            P.op("pe", lambda e: e.transpose(out=out, in_=in_, identity=idn), reads=r, writes=w)

        def act(out, in_, func, r, w, scale=None, bias=None, accum=None):
            kw = {}
            if scale is not None:
                kw["scale"] = scale
            if bias is not None:
                kw["bias"] = bias
            if accum is not None:
                kw["accum_out"] = accum
            P.op("act", lambda e: e.activation(out=out, in_=in_, func=func, **kw), reads=r, writes=w)

        def tsc(eng, out, in0, s1, op0, r, w, s2=None, op1=None):
            if op1 is None:
                P.op(eng, lambda e: e.tensor_scalar(out=out, in0=in0, scalar1=s1, scalar2=None, op0=op0), reads=r, writes=w)
            else:
                P.op(eng, lambda e: e.tensor_scalar(out=out, in0=in0, scalar1=s1, scalar2=s2, op0=op0, op1=op1), reads=r, writes=w)

        def tt(eng, out, in0, in1, op, r, w):
            P.op(eng, lambda e: e.tensor_tensor(out=out, in0=in0, in1=in1, op=op), reads=r, writes=w)

        def stt(out, in0, scalar, in1, op0, op1, r, w):
            P.op("dve", lambda e: e.scalar_tensor_tensor(out=out, in0=in0, scalar=scalar, in1=in1, op0=op0, op1=op1), reads=r, writes=w)

        def cp(eng, out, in_, r, w):
            if eng == "act":
                act(out, in_, AF.Copy, r, w)
            else:
                P.op(eng, lambda e: e.tensor_copy(out=out, in_=in_), reads=r, writes=w)

        def recip(out, in_, r, w):
            P.op("dve", lambda e: e.reciprocal(out=out, in_=in_), reads=r, writes=w)

        def mset(eng, ap, val, w):
            P.op(eng, lambda e: e.memset(ap, val), writes=w)

        def ld(key, out, in_, q="sp", extra_r=()):
            P.dma(q, "L" + key, lambda e: e.dma_start(out=out, in_=in_), reads=list(extra_r), writes=[key])

        def stv(key, out, in_, dkey, q="sp"):
            P.dma(q, "S" + key, lambda e: e.dma_start(out=out, in_=in_), reads=[key], writes=[dkey])

        def rstd_of(ss, key, dim):
            tsc("dve", ss, ss, 1.0 / dim, ALU.mult, [key], [key], s2=EPS, op1=ALU.add)
            act(ss, ss, AF.Sqrt, [key], [key])
            recip(ss, ss, [key], [key])

        rr = {"ps": 0, "ev": 0}

        def next_ps(lo=0, hi=4):
            i = lo + rr["ps"] % (hi - lo)
            rr["ps"] += 1
            return i

        def ev_eng():
            rr["ev"] += 1
            return "act" if rr["ev"] % 2 else "dve"

        ident_t = alloc(128, BF16); tric_t = alloc(128, BF16); triw_t = alloc(128, BF16)
        ld("ident", ident_t, ident); ld("tric", tric_t, tri_c); ld("triw", triw_t, tri_w)
        Ptiles = [alloc(512, BF16) for _ in range(3)]
        persist0 = off[0]

        def phase_A():
            g16_t = alloc(16); ld("g16", g16_t, g16)
            xnT = alloc(16 * TQ, BF16)
            xnT3 = xnT.rearrange("p (c t) -> p c t", c=16)
            xs = [alloc(D) for _ in range(2)]
            xn = [alloc(D, BF16) for _ in range(2)]
            ssx = [alloc(1) for _ in range(2)]
            wst = [alloc(4 * 512) for _ in range(2)]
            wbf = [alloc(16 * 512, BF16) for _ in range(2)]
            stg = [alloc(512) for _ in range(4)]
            sse = [alloc(1) for _ in range(4)]
            rC = [alloc(512) for _ in range(2)]; rS = [alloc(512) for _ in range(2)]
            rt = [alloc(512) for _ in range(2)]

            sc = {"stg": 0, "w": 0, "wp": 0, "rope": 0}

            def next_stg():
                i = sc["stg"] % 4
                sc["stg"] += 1
                return i

            def build_xnT(xsrc):
                for blk in range(16):
                    sl = blk % 2
                    ld(f"xs{sl}", xs[sl], xsrc[blk * 128:(blk + 1) * 128, :])
                    act(xn[sl], xs[sl], AF.Square, [f"xs{sl}"], [f"xn{sl}", f"ssx{sl}"], accum=ssx[sl])
                    rstd_of(ssx[sl], f"ssx{sl}", D)
                    act(xn[sl], xs[sl], AF.Copy, [f"xs{sl}", f"ssx{sl}"], [f"xn{sl}"], scale=ssx[sl])
                    for half in range(2):
                        bank = 6 + half
                        pT = PSB(bank)
                        for c in range(8):
                            cc = half * 8 + c
                            tp(pT[:, c * 128:(c + 1) * 128], xn[sl][:, cc * 128:(cc + 1) * 128], ident_t, [f"xn{sl}", "ident"], [f"ps{bank}"])
                        cp("dve" if half == 0 else "act", xnT3[:, half * 8:(half + 1) * 8, blk * 128:(blk + 1) * 128],
                           pT[:, 0:1024].rearrange("p (c t) -> p c t", c=8), [f"ps{bank}"], [f"xnT_{blk}_{half}"])

            def xnT_keys(t0, nt):
                ks = []
                for blk in range(t0 // 128, (t0 + nt) // 128):
                    ks += [f"xnT_{blk}_0", f"xnT_{blk}_1"]
                return ks

            def load_wtile(c0, ncols):
                sl = sc["w"] % 2
                sc["w"] += 1
                w3 = wbf[sl].rearrange("p (c n) -> p c n", c=16)
                for pc in range(4):
                    ws = sc["wp"] % 2
                    sc["wp"] += 1
                    wv = wst[ws].rearrange("p (c n) -> p c n", c=4)
                    ld(f"wst{ws}", wv[:, :, 0:ncols],
                       win_r[pc * 512:(pc + 1) * 512, c0:c0 + ncols].rearrange("(c p) n -> p c n", p=128))
                    for c in range(4):
                        kc = pc * 4 + c
                        if ev_eng() == "act":
                            act(w3[:, kc, 0:ncols], wv[:, c, 0:ncols], AF.Copy, [f"wst{ws}", "g16"], [f"wbf{sl}_{kc}"],
                                scale=g16_t[:, kc:kc + 1])
                        else:
                            tsc("dve", w3[:, kc, 0:ncols], wv[:, c, 0:ncols], g16_t[:, kc:kc + 1], ALU.mult,
                                [f"wst{ws}", "g16"], [f"wbf{sl}_{kc}"])
                return w3, f"wbf{sl}"

            def tm_unit(w3, wkey, wc0, n, blk, kind, dest):
                b = next_ps(0, 4)
                ps = PS(b)[:, 0:n]
                for c in range(16):
                    mm(ps, xnT3[:, c, blk * 128:(blk + 1) * 128], w3[:, c, wc0:wc0 + n], c == 0, c == 15,
                       xnT_keys(blk * 128, 128) + [f"{wkey}_{c}"], [f"ps{b}"])
                si = next_stg()
                sk = f"stg{si}"
                if kind == "norm":
                    sb = stg[si].bitcast(BF16)
                    act(sb[:, 0:n], ps, AF.Square, [f"ps{b}"], [sk, f"sse{si}"], accum=sse[si])
                    rstd_of(sse[si], f"sse{si}", n)
                    act(sb[:, 0:n], ps, AF.Copy, [f"ps{b}", f"sse{si}"], [sk], scale=sse[si])
                    bank = 4 + (si % 2)
                    pT = PSB(bank)
                    for c in range(4):
                        tp(pT[:, c * 128:(c + 1) * 128], sb[:, c * 128:(c + 1) * 128], ident_t, [sk, "ident"], [f"ps{bank}"])
                    cp("dve", sb[:, 512:1024], pT[:, 0:512], [f"ps{bank}"], [sk])
                    stv(sk, dest.rearrange("(c p) t -> p c t", p=128)[:, :, blk * 128:(blk + 1) * 128],
                        sb[:, 512:1024].rearrange("p (c t) -> p c t", c=4), "dram_" + kind)
                elif kind == "silu":
                    sb = stg[si].bitcast(BF16)
                    act(sb[:, 0:n], ps, AF.Silu, [f"ps{b}"], [sk])
                    stv(sk, dest[blk * 128:(blk + 1) * 128, :], sb[:, 0:n], "dram_z")
                elif kind == "copy":
                    sb = stg[si].bitcast(BF16)
                    cp(ev_eng(), sb[:, 0:n], ps, [f"ps{b}"], [sk])
                    stv(sk, dest[blk * 128:(blk + 1) * 128, :], sb[:, 0:n], "dram_v")
                elif kind == "sigmoid":
                    act(stg[si][:, 0:n], ps, AF.Sigmoid, [f"ps{b}"], [sk])
                    stv(sk, dest[blk * 128:(blk + 1) * 128, :], stg[si][:, 0:n], "dram_g")

            def fm_mm(w3, wkey, wc0, m, tile):
                b = next_ps(0, 4)
                ps = PS(b)[0:m, :]
                for c in range(16):
                    mm(ps, w3[:, c, wc0:wc0 + m], xnT3[:, c, tile * 512:(tile + 1) * 512], c == 0, c == 15,
                       xnT_keys(tile * 512, 512) + [f"{wkey}_{c}"], [f"ps{b}"])
                return b, ps

            def fm_unit(w3, wkey, wc0, m, tile, t_off, dest_rows, scale, dt):
                b, ps = fm_mm(w3, wkey, wc0, m, tile)
                si = next_stg()
                sk = f"stg{si}"
                sb = stg[si] if dt == F32 else stg[si].bitcast(BF16)
                eng = ev_eng()
                if eng == "act":
                    act(sb[0:m, 0:512], ps, AF.Copy, [f"ps{b}"], [sk], scale=scale)
                else:
                    tsc("dve", sb[0:m, 0:512], ps, scale, ALU.mult, [f"ps{b}"], [sk])
                stv(sk, dest_rows[:, t_off + tile * 512:t_off + (tile + 1) * 512], sb[0:m, 0:512], "dram_fm")

            def rope_unit(w3, wkey, wc0, tile, t_off):
                ba, pa = fm_mm(w3, wkey, wc0, 64, tile)
                bb, pb = fm_mm(w3, wkey, wc0 + 64, 64, tile)
                sl = sc["rope"] % 2
                sc["rope"] += 1
                tok0 = t_off + tile * 512
                ld(f"rC{sl}", rC[sl][0:64, :], ropeCk[:, tok0:tok0 + 512])
                ld(f"rS{sl}", rS[sl][0:64, :], ropeSk[:, tok0:tok0 + 512])
                tt("dve", rC[sl][0:64, :], pa, rC[sl][0:64, :], ALU.mult, [f"ps{ba}", f"rC{sl}"], [f"rC{sl}"])
                tt("dve", rS[sl][0:64, :], pb, rS[sl][0:64, :], ALU.mult, [f"ps{bb}", f"rS{sl}"], [f"rS{sl}"])
                rb = rt[sl].bitcast(BF16)
                tt("dve", rb[0:64, 0:512], rC[sl][0:64, :], rS[sl][0:64, :], ALU.add, [f"rC{sl}", f"rS{sl}"], [f"rt{sl}"])
                stv(f"rt{sl}", kpe[:, tok0:tok0 + 512], rb[0:64, 0:512], "dram_kpe")

            seq = []

            def J(c0, ncols, fn):
                seq.append(("t", c0, ncols, fn))

            for ts_i, (xsrc, t_off, own) in enumerate([(x_oth, 0, False), (x_own, TQ, True)]):
                seq.append(("x", xsrc))

                def j0(w3, wk, t_off=t_off):
                    for blk in range(16):
                        tm_unit(w3, wk, 0, 512, blk, "norm", ckvnT[:, t_off:t_off + TQ])
                J(0, 512, j0)

                def j1(w3, wk, t_off=t_off):
                    for blk in range(16):
                        tm_unit(w3, wk, 0, 256, blk, "copy", vsw[t_off:t_off + TQ, :])
                J(512, 256, j1)

                def j2(w3, wk, t_off=t_off):
                    for tile in range(4):
                        rope_unit(w3, wk, 0, tile, t_off)
                        fm_unit(w3, wk, 128, 128, tile, t_off, kcT[0:128, :], 1.0, F32)
                        fm_unit(w3, wk, 320, 128, tile, t_off, ksT[0:128, :], 1.0, BF16)
                    for tile in range(4):
                        fm_unit(w3, wk, 256, 64, tile, t_off, kcT[128:192, :], 1.0, F32)
                        fm_unit(w3, wk, 448, 64, tile, t_off, ksT[128:192, :], 1.0, BF16)
                J(768, 512, j2)

                def j3(w3, wk, t_off=t_off):
                    for tile in range(4):
                        fm_unit(w3, wk, 0, 128, tile, t_off, kwT[0:128, :], 1.0, BF16)
                        fm_unit(w3, wk, 192, 128, tile, t_off, vcT[0:128, :], 1.0, F32)
                    for tile in range(4):
                        fm_unit(w3, wk, 128, 64, tile, t_off, kwT[128:192, :], 1.0, BF16)
                J(1280, 320, j3)
                if not own:
                    continue

                def j4(w3, wk):
                    for blk in range(16):
                        tm_unit(w3, wk, 0, 512, blk, "norm", cqnT)
                J(1600, 512, j4)
                for zi in range(4):
                    def jz(w3, wk, zi=zi):
                        for blk in range(16):
                            tm_unit(w3, wk, 0, 512, blk, "silu", zs[:, zi * 512:(zi + 1) * 512])
                    J(2112 + zi * 512, 512, jz)

                def j9(w3, wk):
                    for blk in range(16):
                        tm_unit(w3, wk, 0, 12, blk, "sigmoid", gn)
                J(4160, 12, j9)

                def mk_jq(first):
                    def jq(w3, wk):
                        for m_sel in (128, 64):
                            for tile in range(4):
                                for hh in range(4):
                                    for (o, m) in ((0, 128), (128, 64)):
                                        cc = hh * 192 + o
                                        if m != m_sel or (cc < 512) != first:
                                            continue
                                        fm_unit(w3, wk, cc if first else cc - 512, m, tile, 0, qnsaT[hh, o:o + m, :], SC_NSA, BF16)
                    return jq
                J(4172, 512, mk_jq(True))
                J(4684, 256, mk_jq(False))

                def j12(w3, wk):
                    for tile in range(4):
                        for hh in range(4):
                            fm_unit(w3, wk, hh * 128, 128, tile, 0, qmemT[hh * 128:(hh + 1) * 128, :], SC_MEM, BF16)
                J(4940, 512, j12)

            tidx = [i for i, e in enumerate(seq) if e[0] == "t"]
            loaded = {}

            def ensure(i):
                if i not in loaded:
                    loaded[i] = load_wtile(seq[i][1], seq[i][2])
            ensure(tidx[0])
            for pos, e in enumerate(seq):
                if e[0] == "x":
                    build_xnT(e[1])
                    continue
                ensure(pos)
                later = [i for i in tidx if i > pos]
                if later:
                    ensure(later[0])
                w3, wk = loaded[pos]
                e[3](w3, wk)

        def attention(tag, nslots, kbs_fn, qk_fn, bias_fn, v_fn, vcols, fin_fn, fin_b=None):
            items = []
            for s in range(nslots):
                kbs = kbs_fn(s)
                cover = {j: [i for i, k in enumerate(kbs) if k["lo"] <= j * 128 < k["hi"]] for j in range(4)}
                for i, k in enumerate(kbs):
                    items.append((s, i, k, cover, len(kbs)))

            def emit_qk(idx):
                s, i, k, cover, n = items[idx]
                nk = k.get("nk", 128)
                sbk = idx % 2
                Sb = PS(sbk)
                mms = qk_fn(s, k)
                for mi, (lT, rh, clo, chi, rds) in enumerate(mms):
                    mm(Sb[0:nk, clo:chi], lT, rh, mi == 0, mi == len(mms) - 1, rds, [f"ps{sbk}"])

            pending = []
            emit_qk(0)
            for idx in range(len(items)):
                if idx + 1 < len(items):
                    emit_qk(idx + 1)
                s, i, k, cover, n = items[idx]
                lo, hi, nk = k["lo"], k["hi"], k.get("nk", 128)
                sbk = idx % 2
                Sb = PS(sbk)
                pi = idx % 3
                Pt = Ptiles[pi]
                bias = bias_fn(s, k)
                act(Pt[0:nk, lo:hi], Sb[0:nk, lo:hi], AF.Exp, [f"ps{sbk}"] + bias[1], [f"Pt{pi}"], bias=bias[0])
                vap, vkeys = v_fn(s, k)
                for j in range(lo // 128, hi // 128):
                    mm(PS(2 + j)[:, 0:vcols], Pt[0:nk, j * 128:(j + 1) * 128], vap, i == cover[j][0], i == cover[j][-1],
                       [f"Pt{pi}"] + vkeys, [f"ps{2 + j}"])
                if i == 1 and pending:
                    for f in pending:
                        f()
                    pending = []
                if i == n - 1:
                    for j in range(4):
                        fin_fn(s, j, PS(2 + j), f"ps{2 + j}")
                    if fin_b is not None:
                        pending = [(lambda s=s, j=j: fin_b(s, j)) for j in range(4)]
            for f in pending:
                f()

        def causal_kbs(s):
            kbs = [dict(kb=kb, lo=0, hi=512, mask=None) for kb in range(16)]
            for c in range(s):
                kbs += [dict(kb=16 + 4 * c + i, lo=0, hi=512, mask=None) for i in range(4)]
            kbs += [dict(kb=16 + 4 * s + i, lo=128 * i, hi=512, mask=("c", 128 * i)) for i in range(4)]
            return kbs

        def window_kbs(s):
            base = 12 if s == 0 else 16 + 4 * (s - 1)
            kbs = [dict(kb=base + i, lo=0, hi=128 * (i + 1), mask=("w", 128 * i)) for i in range(4)]
            kbs += [dict(kb=16 + 4 * s + i, lo=128 * i, hi=512, mask=("c", 128 * i)) for i in range(4)]
            return kbs

        def mask_mm(k):
            if k["mask"] is None:
                return []
            kind, c0 = k["mask"]
            t = tric_t if kind == "c" else triw_t
            return [(ident_t, t, c0, c0 + 128, ["ident", "tric", "triw"])]

        def phase_B():
            cqn = alloc(4 * TQ, BF16); cqn3 = cqn.rearrange("p (c t) -> p c t", c=4)
            ckv = alloc(4 * S, BF16); ckv3 = ckv.rearrange("p (c t) -> p c t", c=4)
            kpe_t = alloc(S, BF16)
            Cq = alloc(TQ); Sq = alloc(TQ)
            kbm = alloc(32); qg = alloc(4); kvg = alloc(4)
            ld("cqn", cqn3, cqnT.rearrange("(c p) t -> p c t", p=128))
            ld("ckv", ckv3, ckvnT.rearrange("(c p) t -> p c t", p=128))
            ld("kpe", kpe_t[0:64, :], kpe)
            mset("pool", kpe_t[64:128, :], 0.0, ["kpe0"])
            ld("Cq", Cq[0:64, :], ropeCq); ld("Sq", Sq[0:64, :], ropeSq)
            ld("kbm", kbm, kb_mla); ld("qg", qg, qg4); ld("kvg", kvg, kvg4)
            wqs = alloc(4 * 256); wks = alloc(4 * 256)
            wqb = [alloc(4 * 256, BF16) for _ in range(2)]; wkb = [alloc(4 * 256, BF16) for _ in range(2)]
            QnT = [alloc(TQ, BF16) for _ in range(2)]; QrT = [alloc(TQ, BF16) for _ in range(2)]
            KnT = [alloc(S, BF16) for _ in range(2)]; Vt = [alloc(32 * 129, BF16) for _ in range(2)]
            zt = [alloc(16 * 128, BF16) for _ in range(2)]; mst = [alloc(TQ, BF16) for _ in range(2)]
            r1 = alloc(512); r2 = alloc(512)
            rden = [alloc(1) for _ in range(4)]
            mtm = [alloc(128, BF16) for _ in range(4)]
            for p in range(2):
                v3 = Vt[p].rearrange("p (b n) -> p b n", n=129)
                mset("pool", v3[:, :, 128:129], 1.0, [f"V{p}"])
                mset("pool", QrT[p][64:128, :], 0.0, [f"Qr0{p}"])

            def prep(h):
                p = h % 2
                wq3 = wqs.rearrange("p (c n) -> p c n", c=4); wk3 = wks.rearrange("p (c n) -> p c n", c=4)
                wqb3 = wqb[p].rearrange("p (c n) -> p c n", c=4); wkb3 = wkb[p].rearrange("p (c n) -> p c n", c=4)
                ld("wqs", wq3, w_uq_r[:, h, :].rearrange("(c p) n -> p c n", p=128))
                ld("wks", wk3, w_ukv[:, h * 256:(h + 1) * 256].rearrange("(c p) n -> p c n", p=128))
                for c in range(4):
                    tsc("pool", wqb3[:, c, :], wq3[:, c, :], qg[:, c:c + 1], ALU.mult, ["wqs", "qg"], [f"wqb{p}"])
                    tsc("pool", wkb3[:, c, :], wk3[:, c, :], kvg[:, c:c + 1], ALU.mult, ["wks", "kvg"], [f"wkb{p}"])
                ld(f"zt{p}", zt[p].rearrange("p (j f) -> p j f", j=16),
                   zs[:, h * 128:(h + 1) * 128].rearrange("(j p) f -> p j f", p=128))

            prep(0)
            for h in range(8):
                p = h % 2
                wqb3 = wqb[p].rearrange("p (c n) -> p c n", c=4); wkb3 = wkb[p].rearrange("p (c n) -> p c n", c=4)
                for tile in range(4):
                    tsl = slice(tile * 512, (tile + 1) * 512)
                    b = next_ps(6, 8)
                    for c in range(4):
                        mm(PS(b), wqb3[:, c, 0:128], cqn3[:, c, tsl], c == 0, c == 3, [f"wqb{p}", "cqn"], [f"ps{b}"])
                    act(QnT[p][:, tsl], PS(b), AF.Copy, [f"ps{b}"], [f"Qn{p}_{tile}"], scale=SC_MLA)
                    ba = next_ps(6, 8)
                    for c in range(4):
                        mm(PS(ba)[0:64, :], wqb3[:, c, 128:192], cqn3[:, c, tsl], c == 0, c == 3, [f"wqb{p}", "cqn"], [f"ps{ba}"])
                    stt(r1[0:64, :], PS(ba)[0:64, :], SC_MLA, Cq[0:64, tsl], ALU.mult, ALU.mult, [f"ps{ba}", "Cq"], ["r1"])
                    bb = next_ps(6, 8)
                    for c in range(4):
                        mm(PS(bb)[0:64, :], wqb3[:, c, 192:256], cqn3[:, c, tsl], c == 0, c == 3, [f"wqb{p}", "cqn"], [f"ps{bb}"])
                    stt(r2[0:64, :], PS(bb)[0:64, :], SC_MLA, Sq[0:64, tsl], ALU.mult, ALU.mult, [f"ps{bb}", "Sq"], ["r2"])
                    tt("dve", QrT[p][0:64, tsl], r1[0:64, :], r2[0:64, :], ALU.add, ["r1", "r2"], [f"Qr{p}_{tile}"])
                for tile in range(8):
                    tsl = slice(tile * 512, (tile + 1) * 512)
                    b = next_ps(6, 8)
                    for c in range(4):
                        mm(PS(b), wkb3[:, c, 0:128], ckv3[:, c, tsl], c == 0, c == 3, [f"wkb{p}", "ckv"], [f"ps{b}"])
                    cp(ev_eng(), KnT[p][:, tsl], PS(b), [f"ps{b}"], [f"Kn{p}_{tile}"])
                v3 = Vt[p].rearrange("p (b n) -> p b n", n=129)
                for g in range(8):
                    b = next_ps(6, 8)
                    for i in range(4):
                        blk = g * 4 + i
                        for c in range(4):
                            mm(PS(b)[:, i * 128:(i + 1) * 128], ckv3[:, c, blk * 128:(blk + 1) * 128], wkb3[:, c, 128:256],
                               c == 0, c == 3, [f"wkb{p}", "ckv"], [f"ps{b}"])
                    cp(ev_eng(), v3[:, g * 4:(g + 1) * 4, 0:128], PS(b).rearrange("p (i n) -> p i n", i=4), [f"ps{b}"], [f"V{p}_{g}"])

                if h + 1 < 8:
                    prep(h + 1)

                def qk_fn(s, k, p=p):
                    kb, lo, hi = k["kb"], k["lo"], k["hi"]
                    qs = slice(s * 512 + lo, s * 512 + hi)
                    out = [(KnT[p][:, kb * 128:(kb + 1) * 128], QnT[p][:, qs], lo, hi, [f"Kn{p}_{kb // 4}", f"Qn{p}_{s}"]),
                           (kpe_t[:, kb * 128:(kb + 1) * 128], QrT[p][:, qs], lo, hi, ["kpe", "kpe0", f"Qr0{p}", f"Qr{p}_{s}"])]
                    return out + mask_mm(k)

                def bias_fn(s, k):
                    return (kbm[:, k["kb"]:k["kb"] + 1], ["kbm"])

                def v_fn(s, k, p=p, v3=v3):
                    return v3[:, k["kb"], :], [f"V{p}", f"V{p}_{k['kb'] // 4}"]

                def fin_fn(s, j, O, okey, p=p, h=h):
                    jj = s * 4 + j
                    recip(rden[j], O[:, 128:129], [okey], [f"rden{j}"])
                    z3 = zt[p].rearrange("p (j f) -> p j f", j=16)
                    stt(mtm[j], O[:, 0:128], rden[j], z3[:, jj, :], ALU.mult, ALU.mult, [okey, f"rden{j}", f"zt{p}"], [f"mtm{j}"])

                def fin_b(s, j, p=p, h=h):
                    jj = s * 4 + j
                    b = 6 + (jj % 2)
                    tp(PSB(b)[:, 0:128], mtm[j], ident_t, [f"mtm{j}", "ident"], [f"ps{b}"])
                    cp("act", mst[p][:, jj * 128:(jj + 1) * 128], PSB(b)[:, 0:128], [f"ps{b}"], [f"mst{p}"])

                attention("mla", 4, causal_kbs, qk_fn, bias_fn, v_fn, 129, fin_fn, fin_b)
                stv(f"mst{p}", mixT[h * 128:(h + 1) * 128, :], mst[p], "dram_mix")

        def phase_C():
            kca = alloc(256, BF16); kcb = alloc(256, BF16); vca = alloc(2 * 193, BF16)
            vca3 = vca.rearrange("p (b n) -> p b n", n=193)
            mark = off[0]
            w1s = alloc(8 * 192)
            w1a = alloc(32 * 192, BF16); w1b = alloc(32 * 192, BF16)
            w2a = alloc(192, BF16); w2b = alloc(192, BF16); w2s = alloc(192)
            pea = alloc(32); peb = alloc(32)
            kf = alloc(S)
            lo_a = alloc(S, BF16); hi_a = alloc(S, BF16); lo_b = alloc(S, BF16); hi_b = alloc(S, BF16)
            sha = alloc(256, BF16); shb = alloc(256, BF16)

            def load_w1(src, nd, no, dst, dkey):
                d3 = dst.rearrange("p (l o) -> p l o", l=32)
                s3 = w1s.rearrange("p (l o) -> p l o", l=8)
                for q4 in range(4):
                    ld("w1s", s3[0:nd, :, 0:no], src[:, q4 * 8:(q4 + 1) * 8, :])
                    cp("dve", d3[0:nd, q4 * 8:(q4 + 1) * 8, 0:no], s3[0:nd, :, 0:no], ["w1s"], [dkey])

            def addpe(src_rows, nd, pe_t, pekey, lo_t, hi_t, lokey):
                ld("kf", kf[0:nd, :], src_rows)
                k3 = kf.rearrange("p (g l) -> p g l", l=16)
                l3 = lo_t.rearrange("p (g l) -> p g l", l=16); h3 = hi_t.rearrange("p (g l) -> p g l", l=16)
                tt("dve", l3[0:nd], k3[0:nd], pe_t[0:nd, 0:16].unsqueeze(1).to_broadcast([nd, 256, 16]), ALU.add,
                   ["kf", pekey], [lokey])
                tt("pool", h3[0:nd], k3[0:nd], pe_t[0:nd, 16:32].unsqueeze(1).to_broadcast([nd, 256, 16]), ALU.add,
                   ["kf", pekey], [lokey + "h"])

            def layer1(chunks, ochunks, outs, no):
                for oi, (o0, m) in enumerate(ochunks):
                    b = next_ps(6, 8)
                    ps = PS(b)[0:m, 0:255]
                    n_mm = 32 * len(chunks)
                    mi = 0
                    for l in range(32):
                        for (nd, wt, lo_t, hi_t, keys) in chunks:
                            w3 = wt.rearrange("p (l o) -> p l o", l=32)
                            if l < 16:
                                rhs = lo_t.rearrange("p (g l) -> p g l", l=16)[0:nd, 0:255, l]
                            else:
                                rhs = hi_t.rearrange("p (g l) -> p g l", l=16)[0:nd, 1:256, l - 16]
                            mm(ps, w3[0:nd, l, o0:o0 + m], rhs, mi == 0, mi == n_mm - 1, keys, [f"ps{b}"])
                            mi += 1
                    act(outs[oi][0:m, 0:255], ps, AF.Silu, [f"ps{b}"], [f"sh{oi}"])

            ld("pea", pea, pekT[0:128, :]); ld("peb", peb[0:64, :], pekT[128:192, :])
            load_w1(w1k[0:128], 128, 192, w1a, "w1a"); load_w1(w1k[128:192], 64, 192, w1b, "w1b")
            ld("w2s", w2s[:, 0:192], w2k[0:128, :]); cp("dve", w2a[:, 0:192], w2s[:, 0:192], ["w2s"], ["w2a"])
            ld("w2s", w2s[0:64, 0:192], w2k[128:192, :]); cp("dve", w2b[0:64, 0:192], w2s[0:64, 0:192], ["w2s"], ["w2b"])
            addpe(kcT[0:128, :], 128, pea, "pea", lo_a, hi_a, "lo_a")
            addpe(kcT[128:192, :], 64, peb, "peb", lo_b, hi_b, "lo_b")
            layer1([(128, w1a, lo_a, hi_a, ["w1a", "lo_a", "lo_ah"]), (64, w1b, lo_b, hi_b, ["w1b", "lo_b", "lo_bh"])],
                   [(0, 128), (128, 64)], [sha, shb], 192)
            for (o0, m, dst, dk) in ((0, 128, kca, "kca"), (128, 64, kcb, "kcb")):
                b = next_ps(6, 8)
                mm(PS(b)[0:m, 0:255], w2a[:, o0:o0 + m], sha[:, 0:255], True, False, ["w2a", "sh0"], [f"ps{b}"])
                mm(PS(b)[0:m, 0:255], w2b[0:64, o0:o0 + m], shb[0:64, 0:255], False, True, ["w2b", "sh1"], [f"ps{b}"])
                cp("dve", dst[0:m, 0:255], PS(b)[0:m, 0:255], [f"ps{b}"], [dk])
            mset("pool", kcb[64:66, :], 1.0, ["kcb1"])
            ld("pea", pea, pevT)
            load_w1(w1v, 128, 128, w1a, "w1a")
            ld("w2s", w2s[:, 0:128], w2v); cp("dve", w2a[:, 0:128], w2s[:, 0:128], ["w2s"], ["w2a"])
            addpe(vcT, 128, pea, "pea", lo_a, hi_a, "lo_a")
            w1v3 = w1a.rearrange("p (l o) -> p l o", l=32)

            b = next_ps(6, 8)
            for l in range(32):
                if l < 16:
                    rhs = lo_a.rearrange("p (g l) -> p g l", l=16)[:, 0:255, l]
                else:
                    rhs = hi_a.rearrange("p (g l) -> p g l", l=16)[:, 1:256, l - 16]
                mm(PS(b)[:, 0:255], w1v3[:, l, 0:128], rhs, l == 0, l == 31, ["w1a", "lo_a", "lo_ah"], [f"ps{b}"])
            act(sha[:, 0:255], PS(b)[:, 0:255], AF.Silu, [f"ps{b}"], ["sh0"])
            for blk, nk in ((0, 128), (1, 127)):
                b = next_ps(6, 8)
                mm(PS(b)[0:nk, 0:128], sha[:, blk * 128:blk * 128 + nk], w2a[:, 0:128], True, True, ["sh0", "w2a"], [f"ps{b}"])
                cp("dve", vca3[0:nk, blk, 0:128], PS(b)[0:nk, 0:128], [f"ps{b}"], [f"vca{blk}"])
            mset("pool", vca3[:, :, 128:129], 1.0, ["vca_one"])
            ld("vca_ov", vca3[:, :, 129:193], ovl)
            P.barrier()
            off[0] = mark

            qa = alloc(4 * TQ, BF16); qb = alloc(4 * TQ, BF16)
            qa3 = qa.rearrange("p (h t) -> p h t", h=4); qb3 = qb.rearrange("p (h t) -> p h t", h=4)
            kreg = off[0]
            ksa = alloc(S, BF16); ksb = alloc(S, BF16); kwa = alloc(S, BF16); kwb = alloc(S, BF16)
            vs_t = alloc(32 * 129, BF16); vw_t = alloc(32 * 129, BF16)
            vs3 = vs_t.rearrange("p (b n) -> p b n", n=129); vw3 = vw_t.rearrange("p (b n) -> p b n", n=129)
            onsa = alloc(16 * 4 * 128); onsa4 = onsa.rearrange("p (j h d) -> p j h d", j=16, h=4)
            imp = alloc(16 * 64); imp3 = imp.rearrange("p (j m) -> p j m", j=16)
            mkc = alloc(2 * TQ, BF16); mkc3 = mkc.rearrange("p (b t) -> p b t", b=2)
            es = alloc(S, BF16); selbT = alloc(TQ, BF16)
            shi = alloc(16 * 64); slo = alloc(16 * 64)
            shi3 = shi.rearrange("p (j m) -> p j m", j=16); slo3 = slo.rearrange("p (j m) -> p j m", j=16)
            kbn = alloc(4 * 32); kbn3 = kbn.rearrange("p (h b) -> p h b", h=4)
            cbn = alloc(8); cbn3 = cbn.rearrange("p (h b) -> p h b", h=4)
            gt = alloc(16 * 12); gt3 = gt.rearrange("p (j g) -> p j g", j=16)
            rden = [alloc(1) for _ in range(4)]; rg = [alloc(1) for _ in range(4)]
            mtm = [alloc(128, BF16) for _ in range(4)]
            m8a = alloc(8); m8b = alloc(8); impw = alloc(64); impf = alloc(64); selb = alloc(64, BF16)

            ld("qa", qa3, qnsaT[:, 0:128, :].rearrange("h p t -> p h t"))
            ld("qb", qb3[0:64], qnsaT[:, 128:192, :].rearrange("h p t -> p h t"))
            ld("qb2", qb3[64:66], qalibi)
            ld("ksa", ksa, ksT[0:128, :]); ld("ksb", ksb[0:64, :], ksT[128:192, :])
            ld("kwa", kwa, kwT[0:128, :]); ld("kwb", kwb[0:64, :], kwT[128:192, :])
            mset("pool", ksb[64:66, :], 1.0, ["ksb1"]); mset("pool", kwb[64:66, :], 1.0, ["kwb1"])
            ld("vs", vs3[:, :, 0:128], vsw[:, 0:128].rearrange("(b p) f -> p b f", p=128))
            ld("vw", vw3[:, :, 0:128], vsw[:, 128:256].rearrange("(b p) f -> p b f", p=128))
            mset("pool", vs3[:, :, 128:129], 1.0, ["vs1"]); mset("pool", vw3[:, :, 128:129], 1.0, ["vw1"])
            ld("mkc", mkc3, maskc); ld("es", es[0:64, :], esel)
            mset("pool", es[64:128, :], 0.0, ["es0"]); mset("pool", selbT[64:128, :], 0.0, ["selbT0"])
            ld("shi", shi3, selhi); ld("slo", slo3, sello)
            ld("kbn", kbn3, kb_nsa); ld("cbn", cbn3, cb_nsa)
            ld("gt", gt3, gn.rearrange("(j p) g -> p j g", p=128))

            qkeys = ["qa", "qb", "qb2"]

            def mk_qk(h, ka, kb_, kkeys, extra=None):
                def qk_fn(s, k):
                    kb, lo, hi = k["kb"], k["lo"], k["hi"]
                    nk = k.get("nk", 128)
                    qs = slice(s * 512 + lo, s * 512 + hi)
                    ks = slice(kb * 128, kb * 128 + nk)
                    out = [(ka[:, ks], qa3[:, h, qs], lo, hi, kkeys + qkeys),
                           (kb_[0:66, ks], qb3[0:66, h, qs], lo, hi, kkeys + qkeys)]
                    if extra is not None:
                        out += extra(s, k)
                    return out + mask_mm(k)
                return qk_fn

            def nsa_fin(h, branch):
                def fin_fn(s, j, O, okey):
                    jj = s * 4 + j
                    tsc("dve", rden[j], O[:, 128:129], 1e-20, ALU.add, [okey], [f"rden{j}"])
                    recip(rden[j], rden[j], [f"rden{j}"], [f"rden{j}"])
                    tt("dve", rg[j], rden[j], gt3[:, jj, h * 3 + branch:h * 3 + branch + 1], ALU.mult, [f"rden{j}", "gt"], [f"rg{j}"])
                    if branch == 0:
                        tsc("dve", onsa4[:, jj, h, :], O[:, 0:128], rg[j], ALU.mult, [okey, f"rg{j}"], [f"onsa{jj}_{h}"])
                        if h == 0:
                            tsc("dve", imp3[:, jj, :], O[:, 129:193], rden[j], ALU.mult, [okey, f"rden{j}"], [f"imp{jj}"])
                        else:
                            stt(imp3[:, jj, :], O[:, 129:193], rden[j], imp3[:, jj, :], ALU.mult, ALU.add,
                                [okey, f"rden{j}", f"imp{jj}"], [f"imp{jj}"])
                    else:
                        stt(onsa4[:, jj, h, :], O[:, 0:128], rg[j], onsa4[:, jj, h, :], ALU.mult, ALU.add,
                            [okey, f"rg{j}", f"onsa{jj}_{h}"], [f"onsa{jj}_{h}"])
                return fin_fn

            def cmp_kbs(s):
                return [dict(kb=0, lo=0, hi=512, nk=128, mask=None), dict(kb=1, lo=0, hi=512, nk=127, mask=None)]

            for h in range(4):
                def extra(s, k):
                    nk = k["nk"]
                    return [(ident_t[0:nk, 0:nk], mkc3[0:nk, k["kb"], s * 512:(s + 1) * 512], 0, 512, ["ident", "mkc"])]

                def bias_fn(s, k, h=h):
                    return (cbn3[0:k["nk"], h, k["kb"]:k["kb"] + 1], ["cbn"])

                def v_fn(s, k):
                    return vca3[0:k["nk"], k["kb"], :], ["vca0", "vca1", "vca_one", "vca_ov"]
                attention("cmp", 4, cmp_kbs, mk_qk(h, kca, kcb, ["kca", "kcb", "kcb1"], extra), bias_fn, v_fn, 193, nsa_fin(h, 0))

            for jj in range(16):
                tt("dve", impf, imp3[:, jj, :], shi3[:, jj, :], ALU.max, [f"imp{jj}", "shi"], ["impf"])
                tt("dve", impf, impf, slo3[:, jj, :], ALU.min, ["impf", "slo"], ["impf"])
                P.op("dve", lambda e: e.max(out=m8a, in_=impf), reads=["impf"], writes=["m8a"])
                P.op("dve", lambda e: e.match_replace(out=impw, in_to_replace=m8a, in_values=impf, imm_value=-3e9),
                     reads=["impf", "m8a"], writes=["impw"])
                P.op("dve", lambda e: e.max(out=m8b, in_=impw), reads=["impw"], writes=["m8b"])
                tsc("dve", selb, impf, m8b[:, 7:8], ALU.is_lt, ["impf", "m8b"], ["selb"], s2=NEG, op1=ALU.mult)
                b = 6 + (jj % 2)
                tp(PSB(b)[0:64, 0:128], selb, ident_t, ["selb", "ident"], [f"ps{b}"])
                cp("act", selbT[0:64, jj * 128:(jj + 1) * 128], PSB(b)[0:64, 0:128], [f"ps{b}"], ["selbT"])

            for h in range(4):
                def extra(s, k):
                    kb, lo, hi = k["kb"], k["lo"], k["hi"]
                    return [(es[:, kb * 128:(kb + 1) * 128], selbT[:, s * 512 + lo:s * 512 + hi], lo, hi, ["es", "es0", "selbT", "selbT0"])]

                def bias_fn(s, k, h=h):
                    return (kbn3[:, h, k["kb"]:k["kb"] + 1], ["kbn"])

                def v_fn(s, k):
                    return vs3[:, k["kb"], :], ["vs", "vs1"]
                attention("slc", 4, causal_kbs, mk_qk(h, ksa, ksb, ["ksa", "ksb", "ksb1"], extra), bias_fn, v_fn, 129, nsa_fin(h, 1))

            for h in range(4):
                def bias_fn(s, k, h=h):
                    return (kbn3[:, h, k["kb"]:k["kb"] + 1], ["kbn"])

                def v_fn(s, k):
                    return vw3[:, k["kb"], :], ["vw", "vw1"]
                attention("win", 4, window_kbs, mk_qk(h, kwa, kwb, ["kwa", "kwb", "kwb1"]), bias_fn, v_fn, 129, nsa_fin(h, 2))

            P.barrier()
            save = off[0]
            off[0] = kreg
            zn = alloc(16 * 512, BF16); zn3 = zn.rearrange("p (j f) -> p j f", j=16)
            mst = alloc(4 * TQ, BF16); mst3 = mst.rearrange("p (h t) -> p h t", h=4)
            off[0] = save
            ld("zn", zn3, zs[:, 1024:1536].rearrange("(j p) f -> p j f", p=128))
            for h in range(4):
                for jj in range(16):
                    j = jj % 4
                    tt("dve", mtm[j], onsa4[:, jj, h, :], zn3[:, jj, h * 128:(h + 1) * 128], ALU.mult, [f"onsa{jj}_{h}", "zn"], [f"mtm{j}"])
                    b = 6 + (jj % 2)
                    tp(PSB(b)[:, 0:128], mtm[j], ident_t, [f"mtm{j}", "ident"], [f"ps{b}"])
                    cp("act", mst3[:, h, jj * 128:(jj + 1) * 128], PSB(b)[:, 0:128], [f"ps{b}"], ["mstn"])
            stv("mstn", mixT[1024:1536, :].rearrange("(h p) t -> p h t", p=128), mst3, "dram_mix")

        def phase_M():
            mg = alloc(16); ld("mg", mg, mg16)
            xs = [alloc(D) for _ in range(2)]; xn = [alloc(D, BF16) for _ in range(2)]; ssx = [alloc(1) for _ in range(2)]
            xmT = alloc(16 * 256, BF16); xm3 = xmT.rearrange("p (c t) -> p c t", c=16)
            wst = [alloc(4 * 512) for _ in range(2)]
            wbf = [alloc(16 * 512, BF16) for _ in range(2)]
            kmT = alloc(4 * 256, BF16); km3 = kmT.rearrange("p (h t) -> p h t", h=4)
            vm = alloc(2 * 4 * 129, BF16); vm4 = vm.rearrange("p (b h n) -> p b h n", b=2, h=4)
            qm = alloc(4 * TQ, BF16); qm3 = qm.rearrange("p (h t) -> p h t", h=4)
            zm = alloc(16 * 512, BF16); zm3 = zm.rearrange("p (j f) -> p j f", j=16)
            mst = alloc(4 * TQ, BF16); mst3 = mst.rearrange("p (h t) -> p h t", h=4)
            rden = [alloc(1) for _ in range(4)]; mtm = [alloc(128, BF16) for _ in range(4)]
            ld("qm", qm3, qmemT.rearrange("(h p) t -> p h t", p=128))
            ld("zm", zm3, zs[:, 1536:2048].rearrange("(j p) f -> p j f", p=128))
            mset("pool", vm4[:, :, :, 128:129], 1.0, ["vm1"])
            for blk in range(2):
                sl = blk
                ld(f"xs{sl}", xs[sl], memx[blk * 128:(blk + 1) * 128, :])
                act(xn[sl], xs[sl], AF.Square, [f"xs{sl}"], [f"xn{sl}", f"ssx{sl}"], accum=ssx[sl])
                rstd_of(ssx[sl], f"ssx{sl}", D)
                act(xn[sl], xs[sl], AF.Copy, [f"xs{sl}", f"ssx{sl}"], [f"xn{sl}"], scale=ssx[sl])
                for half in range(2):
                    bank = 6 + half
                    pT = PSB(bank)
                    for c in range(8):
                        cc = half * 8 + c
                        tp(pT[:, c * 128:(c + 1) * 128], xn[sl][:, cc * 128:(cc + 1) * 128], ident_t, [f"xn{sl}", "ident"], [f"ps{bank}"])
                    cp("dve", xm3[:, half * 8:(half + 1) * 8, blk * 128:(blk + 1) * 128],
                       pT[:, 0:1024].rearrange("p (c t) -> p c t", c=8), [f"ps{bank}"], [f"xm_{blk}_{half}"])
            xmk = [f"xm_{b}_{hf}" for b in range(2) for hf in range(2)]
            wp = 0
            for wt in range(2):
                w3 = wbf[wt].rearrange("p (c n) -> p c n", c=16)
                for pc in range(4):
                    ws = wp % 2
                    wp += 1
                    wv = wst[ws].rearrange("p (c n) -> p c n", c=4)
                    ld(f"wst{ws}", wv, w_mkv[pc * 512:(pc + 1) * 512, wt * 512:(wt + 1) * 512].rearrange("(c p) n -> p c n", p=128))
                    for c in range(4):
                        kc = pc * 4 + c
                        if ev_eng() == "act":
                            act(w3[:, kc, :], wv[:, c, :], AF.Copy, [f"wst{ws}", "mg"], [f"wbf{wt}_{kc}"], scale=mg[:, kc:kc + 1])
                        else:
                            tsc("dve", w3[:, kc, :], wv[:, c, :], mg[:, kc:kc + 1], ALU.mult, [f"wst{ws}", "mg"], [f"wbf{wt}_{kc}"])
            wk3 = wbf[0].rearrange("p (c n) -> p c n", c=16); wv3 = wbf[1].rearrange("p (c n) -> p c n", c=16)
            for hh in range(4):
                b = next_ps(6, 8)
                for c in range(16):
                    mm(PS(b)[:, 0:256], wk3[:, c, hh * 128:(hh + 1) * 128], xm3[:, c, :], c == 0, c == 15, [f"wbf0_{c}"] + xmk, [f"ps{b}"])
                cp("dve", km3[:, hh, :], PS(b)[:, 0:256], [f"ps{b}"], ["km"])
            for blk in range(2):
                b = next_ps(6, 8)
                for c in range(16):
                    mm(PS(b), xm3[:, c, blk * 128:(blk + 1) * 128], wv3[:, c, :], c == 0, c == 15, [f"wbf1_{c}"] + xmk, [f"ps{b}"])
                cp("dve", vm4[:, blk, :, 0:128], PS(b).rearrange("p (h n) -> p h n", h=4), [f"ps{b}"], ["vmv"])

            def mem_kbs(s):
                return [dict(kb=0, lo=0, hi=512, mask=None), dict(kb=1, lo=0, hi=512, mask=None)]

            for h in range(4):
                def qk_fn(s, k, h=h):
                    kb = k["kb"]
                    return [(km3[:, h, kb * 128:(kb + 1) * 128], qm3[:, h, s * 512:(s + 1) * 512], 0, 512, ["km", "qm"])]

                def bias_fn(s, k):
                    return (None, [])

                def v_fn(s, k, h=h):
                    return vm4[:, k["kb"], h, :], ["vmv", "vm1"]

                def fin_fn(s, j, O, okey, h=h):
                    jj = s * 4 + j
                    recip(rden[j], O[:, 128:129], [okey], [f"rden{j}"])
                    stt(mtm[j], O[:, 0:128], rden[j], zm3[:, jj, h * 128:(h + 1) * 128], ALU.mult, ALU.mult,
                        [okey, f"rden{j}", "zm"], [f"mtm{j}"])

                def fin_b(s, j, h=h):
                    jj = s * 4 + j
                    b = 6 + (jj % 2)
                    tp(PSB(b)[:, 0:128], mtm[j], ident_t, [f"mtm{j}", "ident"], [f"ps{b}"])
                    cp("act", mst3[:, h, jj * 128:(jj + 1) * 128], PSB(b)[:, 0:128], [f"ps{b}"], ["mstm"])
                attention("mem", 4, mem_kbs, qk_fn, bias_fn, v_fn, 129, fin_fn, fin_b)
            stv("mstm", mixT[1536:2048, :].rearrange("(h p) t -> p h t", p=128), mst3, "dram_mix")

        def phase_D():
            wob = alloc(16 * D, BF16); wo3 = wob.rearrange("p (c n) -> p c n", c=16)
            wst = [alloc(4 * 512) for _ in range(2)]
            fg = alloc(D); ld("fg", fg, fg_rep)
            mx = [alloc(16 * 128, BF16) for _ in range(2)]
            xs = [alloc(D) for _ in range(2)]; yt = [alloc(D) for _ in range(2)]; ssy = [alloc(1) for _ in range(2)]
            junk = alloc(D, BF16)
            wp = 0
            for nt in range(4):
                for pc in range(4):
                    ws = wp % 2
                    wp += 1
                    wv = wst[ws].rearrange("p (c n) -> p c n", c=4)
                    ld(f"wst{ws}", wv, w_out[pc * 512:(pc + 1) * 512, nt * 512:(nt + 1) * 512].rearrange("(c p) n -> p c n", p=128))
                    cp(ev_eng(), wo3[:, pc * 4:(pc + 1) * 4, nt * 512:(nt + 1) * 512], wv, [f"wst{ws}"], [f"wob{nt}_{pc}"])
            for blk in range(16):
                sl = blk % 2
                m3 = mx[sl].rearrange("p (c t) -> p c t", c=16)
                ld(f"mx{sl}", m3, mixT[:, blk * 128:(blk + 1) * 128].rearrange("(c p) t -> p c t", p=128))
                ld(f"xs{sl}", xs[sl], x_own[blk * 128:(blk + 1) * 128, :])
                for nt in range(4):
                    b = (blk * 4 + nt) % 6
                    for c in range(16):
                        mm(PS(b), m3[:, c, :], wo3[:, c, nt * 512:(nt + 1) * 512], c == 0, c == 15, [f"mx{sl}", f"wob{nt}_{c // 4}"], [f"ps{b}"])
                    tt("dve", yt[sl][:, nt * 512:(nt + 1) * 512], PS(b), xs[sl][:, nt * 512:(nt + 1) * 512], ALU.add,
                       [f"ps{b}", f"xs{sl}"], [f"yt{sl}"])
                act(junk, yt[sl], AF.Square, [f"yt{sl}"], ["junk", f"ssy{sl}"], accum=ssy[sl])
                rstd_of(ssy[sl], f"ssy{sl}", D)
                act(yt[sl], yt[sl], AF.Copy, [f"yt{sl}", f"ssy{sl}"], [f"yt{sl}"], scale=ssy[sl])
                tt("pool", yt[sl], yt[sl], fg, ALU.mult, [f"yt{sl}", "fg"], [f"yt{sl}"])
                stv(f"yt{sl}", y_out[blk * 128:(blk + 1) * 128, :], yt[sl], "dram_y", q="pool")

        phases = [("A", phase_A), ("B", phase_B), ("C", phase_C), ("M", phase_M), ("D", phase_D)]
        for name, fn in phases:
            off[0] = persist0
            fn()
            P.barrier()
            if STOP_AFTER == name:
                break
        P.emit()
        print(f"[kernel] ops={P.nops} waits={P.nwaits} sems={len(P.streams) + 5}", flush=True)
    return nc


def _bf(a):
    return np.ascontiguousarray(a).astype(ml_dtypes.bfloat16)


def _prep_shared(inp):
    w_in = np.asarray(inp["w_in"])[0]
    o = np.cumsum([0, 512, 512, 64, 1024, 768, 192, 128, 192, 128, 192, 128, 12, 512, 512, 512])
    seg = {n: (o[i], o[i + 1]) for i, n in enumerate(
        ["c_q", "c_kv", "k_rope", "z_mla", "q_nsa", "k_c", "v_c", "k_s", "v_s", "k_w", "v_w", "g_nsa", "z_nsa", "q_mem", "z_mem"])}

    def cols(n):
        return np.arange(seg[n][0], seg[n][1])
    kr = cols("k_rope")
    order = np.concatenate([
        cols("c_kv"), cols("v_s"), cols("v_w"),
        kr, np.concatenate([kr[32:], kr[:32]]), cols("k_c"), cols("k_s"),
        cols("k_w"), cols("v_c"),
        cols("c_q"), cols("z_mla"), cols("z_nsa"), cols("z_mem"), cols("g_nsa"),
        cols("q_nsa"), cols("q_mem")])
    assert order.shape[0] == NCOL
    sh = {}
    sh["win_r"] = np.ascontiguousarray(w_in[:, order])
    sh["g16"] = np.ascontiguousarray(np.asarray(inp["norm_g"])[0].reshape(16, 128).T)
    sh["qg4"] = np.ascontiguousarray(np.asarray(inp["q_norm_g"])[0].reshape(4, 128).T)
    sh["kvg4"] = np.ascontiguousarray(np.asarray(inp["kv_norm_g"])[0].reshape(4, 128).T)
    sh["mg16"] = np.ascontiguousarray(np.asarray(inp["mem_norm_g"])[0].reshape(16, 128).T)
    wuq = np.asarray(inp["w_uq"])[0].reshape(512, 8, 192)
    sh["w_uq_r"] = np.ascontiguousarray(np.concatenate(
        [wuq[:, :, 0:128], wuq[:, :, 128:192], wuq[:, :, 160:192], wuq[:, :, 128:160]], axis=2))
    sh["w_ukv"] = np.ascontiguousarray(np.asarray(inp["w_ukv"])[0])
    sh["w1k"] = np.ascontiguousarray(np.asarray(inp["cmp_w1k"])[0].reshape(32, 192, 192).transpose(1, 0, 2))
    sh["w1v"] = np.ascontiguousarray(np.asarray(inp["cmp_w1v"])[0].reshape(32, 128, 128).transpose(1, 0, 2))
    sh["pekT"] = np.ascontiguousarray(np.asarray(inp["cmp_pe_k"])[0].T)
    sh["pevT"] = np.ascontiguousarray(np.asarray(inp["cmp_pe_v"])[0].T)
    sh["w2k"] = np.ascontiguousarray(np.asarray(inp["cmp_w2k"])[0])
    sh["w2v"] = np.ascontiguousarray(np.asarray(inp["cmp_w2v"])[0])
    sh["w_mkv"] = np.ascontiguousarray(np.asarray(inp["w_mem_kv"])[0])
    sh["w_out"] = np.ascontiguousarray(np.asarray(inp["w_out"])[0])
    sh["fg_rep"] = np.ascontiguousarray(np.broadcast_to(np.asarray(inp["final_norm_g"])[None, :], (128, D)))
    k = np.arange(128)[:, None]
    q = np.arange(128)[None, :]
    sh["tri_c"] = _bf(np.where(k > q, NEG, 0.0))
    sh["tri_w"] = _bf(np.where(q >= k, NEG, 0.0))
    sh["ident"] = _bf(np.eye(128))
    kk = np.arange(S)
    sh["esel"] = _bf((kk[None, :] // 64 == np.arange(64)[:, None]).astype(np.float32))
    return sh


def _rope_tables(pos):
    inv_freq = (10000.0 ** (-np.arange(0, 64, 2, dtype=np.float32) / 64)).astype(np.float32)
    ang = pos.astype(np.float32)[:, None] * inv_freq[None, :]
    cos = np.cos(ang).astype(np.float32).T
    sin = np.sin(ang).astype(np.float32).T
    return (np.ascontiguousarray(np.concatenate([cos, cos], 0)),
            np.ascontiguousarray(np.concatenate([-sin, sin], 0)))


def _prep_half(hf):
    t = {}
    tq = 2048 * hf + np.arange(TQ)
    kvalid = np.concatenate([np.full(TQ, hf == 1), np.ones(TQ, bool)])
    kpos = np.concatenate([np.arange(TQ), tq]).astype(np.int64)
    kpos_eff = np.where(kvalid, kpos, 0)
    t["ropeCk"], t["ropeSk"] = _rope_tables(kpos)
    t["ropeCq"], t["ropeSq"] = _rope_tables(tq)
    slopes = (2.0 ** (-8.0 * np.arange(1, 5) / 4)).astype(np.float32)
    mb = np.where(kvalid, 0.0, NEG).astype(np.float32)
    t["kb_mla"] = np.ascontiguousarray(mb.reshape(32, 128).T)
    kbn = slopes[None, :, None] * kpos_eff.reshape(32, 128).T[:, None, :].astype(np.float32) + mb.reshape(32, 128).T[:, None, :]
    t["kb_nsa"] = np.ascontiguousarray(kbn.astype(np.float32))
    qa = np.stack([-64.0 * slopes[:, None] * (tq // 64)[None, :], -slopes[:, None] * (tq % 64)[None, :]], 0)
    t["qalibi"] = _bf(qa)
    n_ = np.arange(256)
    if hf == 1:
        cvalid = n_ < 255
        nat = n_
    else:
        cvalid = (n_ >= 128) & (n_ < 255)
        nat = n_ - 128
    cpos = 16.0 * nat + 15.5
    cend = 16 * nat + 31
    cb = slopes[None, :] * np.where(cvalid, cpos, 0.0)[:, None] + np.where(cvalid, 0.0, NEG)[:, None]
    t["cb_nsa"] = np.ascontiguousarray(cb.reshape(2, 128, 4).transpose(1, 2, 0).astype(np.float32))
    mc = np.where(cvalid[:, None] & (cend[:, None] <= tq[None, :]), 0.0, NEG)
    t["maskc"] = _bf(mc.reshape(2, 128, TQ).transpose(1, 0, 2))
    j_ = np.arange(64)
    if hf == 1:
        jvalid = np.ones(64, bool)
        natj = j_
    else:
        jvalid = j_ >= 32
        natj = j_ - 32
    cst = 16 * nat
    sst = 64 * natj
    ov = (cst[:, None] < sst[None, :] + 64) & (cst[:, None] + 32 > sst[None, :]) & cvalid[:, None] & jvalid[None, :]
    t["ovl"] = _bf(ov.astype(np.float32).reshape(2, 128, 64).transpose(1, 0, 2))
    cur = tq // 64
    forced = jvalid[None, :] & ((natj[None, :] == 0) | (natj[None, :] == cur[:, None]) | (natj[None, :] == cur[:, None] - 1))
    fut = (~jvalid[None, :]) | (natj[None, :] > cur[:, None])
    t["selhi"] = np.ascontiguousarray(np.where(forced, 1e9, 0.0).astype(np.float32).reshape(16, 128, 64).transpose(1, 0, 2))
    t["sello"] = np.ascontiguousarray(np.where(fut, -1e9, 1e9).astype(np.float32).reshape(16, 128, 64).transpose(1, 0, 2))
    return t


_CACHE = {}


def kernel(**inputs):
    x = np.asarray(inputs["x"], dtype=np.float32)
    mem = np.asarray(inputs["mem"], dtype=np.float32)
    if "nc" not in _CACHE:
        _CACHE["nc"] = build_program()
        _CACHE["half"] = [_prep_half(0), _prep_half(1)]
    nc = _CACHE["nc"]
    sh = _prep_shared(inputs)
    cores = [int(c) for c in os.environ.get("MK_CORES", "0,1,2,3,4,5,6,7").split(",")]
    in_maps = []
    for c in cores:
        b, hf = c // 2, c % 2
        m = dict(sh)
        m.update(_CACHE["half"][hf])
        m["x_own"] = np.ascontiguousarray(x[b, 2048 * hf:2048 * hf + 2048])
        m["x_oth"] = np.ascontiguousarray(x[b, 0:2048])
        m["memx"] = np.ascontiguousarray(mem[b])
        in_maps.append(m)
    res = run_bass_kernel_spmd(nc, in_maps, core_ids=list(range(len(cores))))
    if DEBUG:
        _CACHE["dbg"] = res.results
    out = np.zeros((4, S, D), np.float32)
    for i, c in enumerate(cores):
        b, hf = c // 2, c % 2
        out[b, 2048 * hf:2048 * hf + 2048] = res.results[i]["y_out"]
    return out
```
